# Optimizing a Trainium2 kernel written in Bass

```python
import jax, jax.numpy as jnp
from jax import lax
import numpy as np

D_MODEL = 1024
BATCH = 8
SEQ = 2048
DEPTH = 4

N_MIXERS = 2
N_DN_LAYERS = (DEPTH + 1) // 2
N_SB_LAYERS = DEPTH // 2
DN_HEADS = 8
DN_HEAD_DIM = D_MODEL // DN_HEADS
DN_CONV = 4
DN_CHUNK = 64
SB_HEADS = 16
SB_HEAD_DIM = D_MODEL // SB_HEADS
SB_BLOCK = 128
D_FF = -(-8 * D_MODEL // (3 * 256)) * 256
N_MOD = 6
EPS = 1e-6
ADA_INIT = 0.25

kernel_name = 'hybrid_deltanet_stickbreaking_adaln_trunk'


def rms_norm(x, g):
    xf = x.astype(jnp.float32)
    y = xf * lax.rsqrt(jnp.mean(xf * xf, axis=-1, keepdims=True) + EPS)
    return (y * g.astype(jnp.float32)).astype(x.dtype)


def l2_norm(x):
    return x * lax.rsqrt(jnp.sum(x * x, axis=-1, keepdims=True) + EPS)


def causal_dwconv(x, w):
    width = w.shape[0]
    return lax.conv_general_dilated(
        x, w[:, None, :].astype(x.dtype), window_strides=(1,), padding=[(width - 1, 0)],
        dimension_numbers=('NWC', 'WIO', 'NWC'), feature_group_count=x.shape[-1])


def chunk_gated_delta_rule(q, k, v, g, beta):
    B, T, H, dk = q.shape
    dv = v.shape[-1]
    C = DN_CHUNK
    N = T // C
    to_c = lambda t: t.transpose(0, 2, 1, 3).reshape(B, H, N, C, t.shape[-1])
    q = to_c(q) * (dk ** -0.5)
    k = to_c(k)
    v = to_c(v)
    g = g.transpose(0, 2, 1).reshape(B, H, N, C)
    beta = beta.transpose(0, 2, 1).reshape(B, H, N, C)
    G = jnp.cumsum(g, axis=-1)
    idx = jnp.arange(C)
    incl = idx[:, None] >= idx[None, :]
    strict = idx[:, None] > idx[None, :]
    decay = jnp.exp(jnp.where(incl, G[..., :, None] - G[..., None, :], -jnp.inf))
    k_beta = k * beta[..., None]
    v_beta = v * beta[..., None]
    A = jnp.where(strict, jnp.einsum('bhnid,bhnjd->bhnij', k_beta, k) * decay, 0.0)
    tri = jnp.eye(C, dtype=A.dtype) + A
    U = lax.linalg.triangular_solve(tri, v_beta, left_side=True, lower=True, unit_diagonal=True)
    W = lax.linalg.triangular_solve(tri, k_beta * jnp.exp(G)[..., None], left_side=True,
                                    lower=True, unit_diagonal=True)
    attn_intra = jnp.einsum('bhnid,bhnjd->bhnij', q, k) * decay
    q_dec = q * jnp.exp(G)[..., None]
    k_tail = k * jnp.exp(G[..., -1:] - G)[..., None]
    g_last = jnp.exp(G[..., -1])
    xs = tuple(jnp.moveaxis(t, 2, 0) for t in (q_dec, k_tail, U, W, attn_intra, g_last))

    def step(S, inp):
        qd, kt, u, w, a, gl = inp
        v_new = u - jnp.einsum('bhck,bhkv->bhcv', w, S)
        o = jnp.einsum('bhck,bhkv->bhcv', qd, S) + jnp.einsum('bhij,bhjv->bhiv', a, v_new)
        S = S * gl[..., None, None] + jnp.einsum('bhck,bhcv->bhkv', kt, v_new)
        return S, o

    S0 = jnp.zeros((B, H, dk, dv), jnp.float32)
    _, o = lax.scan(step, S0, xs)
    return jnp.moveaxis(o, 0, 2).reshape(B, H, T, dv).transpose(0, 2, 1, 3)


def gated_deltanet_mixer(h, w_in, conv_w, a_log, dt_bias, onorm_g, w_out):
    B, T, _ = h.shape
    H, d = DN_HEADS, DN_HEAD_DIM
    proj = h @ w_in
    qkv, z, a, b = jnp.split(proj, [3 * H * d, 4 * H * d, 4 * H * d + H], axis=-1)
    qkv = jax.nn.silu(causal_dwconv(qkv, conv_w)).astype(jnp.float32)
    q, k, v = [t.reshape(B, T, H, d) for t in jnp.split(qkv, 3, axis=-1)]
    q = l2_norm(q)
    k = l2_norm(k)
    beta = jax.nn.sigmoid(b.astype(jnp.float32))
    g = -jnp.exp(a_log.astype(jnp.float32)) * jax.nn.softplus(
        a.astype(jnp.float32) + dt_bias.astype(jnp.float32))
    o = chunk_gated_delta_rule(q, k, v, g, beta)
    o = rms_norm(o, onorm_g) * jax.nn.silu(z.reshape(B, T, H, d).astype(jnp.float32))
    return o.reshape(B, T, H * d).astype(h.dtype) @ w_out


def stick_breaking_mixer(h, w_qkv, q_norm_g, k_norm_g, w_out):
    B, T, _ = h.shape
    H, d = SB_HEADS, SB_HEAD_DIM
    qkv = (h @ w_qkv).reshape(B, T, 3, H, d)
    q = rms_norm(qkv[:, :, 0], q_norm_g).astype(jnp.float32).transpose(0, 2, 1, 3)
    k = rms_norm(qkv[:, :, 1], k_norm_g).astype(jnp.float32).transpose(0, 2, 1, 3)
    v = qkv[:, :, 2].astype(jnp.float32).transpose(0, 2, 1, 3)
    scale = d ** -0.5
    outs = []
    for blk in range(T // SB_BLOCK):
        q0 = blk * SB_BLOCK
        kv_len = q0 + SB_BLOCK
        z = jnp.einsum('bhqd,bhkd->bhqk', q[:, :, q0:kv_len], k[:, :, :kv_len]) * scale
        t_pos = q0 + jnp.arange(SB_BLOCK)
        s_pos = jnp.arange(kv_len)
        causal = s_pos[None, :] < t_pos[:, None]
        log_1m = jnp.where(causal, jax.nn.log_sigmoid(-z), 0.0)
        log_stick = lax.cumsum(log_1m, axis=3, reverse=True) - log_1m
        a = jnp.where(causal, jnp.exp(jax.nn.log_sigmoid(z) + log_stick), 0.0)
        outs.append(jnp.einsum('bhqk,bhkd->bhqd', a, v[:, :, :kv_len]))
    o = jnp.concatenate(outs, axis=2).transpose(0, 2, 1, 3).reshape(B, T, H * d)
    return o.astype(h.dtype) @ w_out


def swiglu(h, w_in, w_out):
    gate, up = jnp.split(h @ w_in, 2, axis=-1)
    return (jax.nn.silu(gate) * up) @ w_out


def setup_inputs(seed: int = 0) -> dict:
    key = jax.random.key(seed)
    ks = jax.random.split(key, 20)
    D, H, d = D_MODEL, DN_HEADS, DN_HEAD_DIM
    nrm = lambda k, shape, s: jax.random.normal(k, shape, jnp.float32) * s
    dn_in_cols = 4 * H * d + 2 * H
    dt = jnp.exp(jax.random.uniform(ks[8], (N_DN_LAYERS, H), jnp.float32, np.log(1e-3), np.log(1e-1)))
    return {
        'x': nrm(ks[0], (BATCH, SEQ, D), 1.0),
        'c': nrm(ks[1], (BATCH, D), 1.0),
        'ada_w': nrm(ks[2], (DEPTH, D, N_MOD * D), ADA_INIT * D ** -0.5),
        'ada_b': nrm(ks[3], (DEPTH, N_MOD * D), 0.01),
        'norm1_g': 1.0 + nrm(ks[4], (DEPTH, D), 0.02),
        'norm2_g': 1.0 + nrm(ks[5], (DEPTH, D), 0.02),
        'dn_w_in': nrm(ks[6], (N_DN_LAYERS, D, dn_in_cols), D ** -0.5),
        'dn_conv_w': nrm(ks[7], (N_DN_LAYERS, DN_CONV, 3 * H * d), DN_CONV ** -0.5),
        'dn_a_log': jnp.log(jax.random.uniform(ks[9], (N_DN_LAYERS, H), jnp.float32, 1.0, 16.0)),
        'dn_dt_bias': dt + jnp.log(-jnp.expm1(-dt)),
        'dn_onorm_g': 1.0 + nrm(ks[10], (N_DN_LAYERS, d), 0.02),
        'dn_w_out': nrm(ks[11], (N_DN_LAYERS, H * d, D), (H * d) ** -0.5),
        'sb_w_qkv': nrm(ks[12], (N_SB_LAYERS, D, 3 * SB_HEADS * SB_HEAD_DIM), D ** -0.5),
        'sb_q_norm_g': 1.0 + nrm(ks[13], (N_SB_LAYERS, SB_HEAD_DIM), 0.02),
        'sb_k_norm_g': 1.0 + nrm(ks[14], (N_SB_LAYERS, SB_HEAD_DIM), 0.02),
        'sb_w_out': nrm(ks[15], (N_SB_LAYERS, SB_HEADS * SB_HEAD_DIM, D), (SB_HEADS * SB_HEAD_DIM) ** -0.5),
        'ffn_w_in': nrm(ks[16], (DEPTH, D, 2 * D_FF), D ** -0.5),
        'ffn_w_out': nrm(ks[17], (DEPTH, D_FF, D), D_FF ** -0.5),
    }


def reference(x, c, ada_w, ada_b, norm1_g, norm2_g, dn_w_in, dn_conv_w, dn_a_log, dn_dt_bias,
              dn_onorm_g, dn_w_out, sb_w_qkv, sb_q_norm_g, sb_k_norm_g, sb_w_out, ffn_w_in, ffn_w_out):
    cond = jax.nn.silu(c)
    for i in range(DEPTH):
        mod = (cond @ ada_w[i] + ada_b[i])[:, None, :]
        sh1, sc1, gt1, sh2, sc2, gt2 = jnp.split(mod, N_MOD, axis=-1)
        h = rms_norm(x, norm1_g[i]) * (1.0 + sc1) + sh1
        j = i // N_MIXERS
        if i % N_MIXERS == 0:
            y = gated_deltanet_mixer(h, dn_w_in[j], dn_conv_w[j], dn_a_log[j], dn_dt_bias[j],
                                     dn_onorm_g[j], dn_w_out[j])
        else:
            y = stick_breaking_mixer(h, sb_w_qkv[j], sb_q_norm_g[j], sb_k_norm_g[j], sb_w_out[j])
        x = x + gt1 * y
        h = rms_norm(x, norm2_g[i]) * (1.0 + sc2) + sh2
        x = x + gt2 * swiglu(h, ffn_w_in[i], ffn_w_out[i])
    return x
```

```python
import contextlib
import numpy as np
import concourse.bass as bass
import concourse.mybir as mybir
from concourse.bass_utils import run_bass_kernel_spmd

F32 = mybir.dt.float32
F32R = mybir.dt.float32r
BF16 = mybir.dt.bfloat16
F16 = mybir.dt.float16
AF = mybir.ActivationFunctionType
ALU = mybir.AluOpType

D = 1024
T = 2048
DFF = 2816
NL = 4
KC = 8
NTB = 4
EPS = 1e-6
NMOD = 48


class _Op:
    __slots__ = ("eng", "fn", "deps", "sig", "count", "dma", "batch", "is_mm")

    def __init__(self, eng, fn, is_mm=False):
        self.eng = eng
        self.fn = fn
        self.deps = []
        self.sig = False
        self.count = 0
        self.dma = None
        self.batch = 0
        self.is_mm = is_mm


class DmaGroup:
    def __init__(self, name):
        self.name = name
        self.ops = []
        self.batch = -1
        self.sem = None
        self.cum = {}

    def new_batch(self):
        self.batch += 1


class Prog:
    ENGS = ("pe", "act", "dve", "pool", "sp")

    def __init__(self, nc):
        self.nc = nc
        self.ops = {e: [] for e in self.ENGS}
        self.last_w = {}
        self.readers = {}
        self.groups = []

    def dma_group(self, name):
        g = DmaGroup(name)
        self.groups.append(g)
        return g

    def _track(self, op, reads, writes):
        psr = [r for r in reads if isinstance(r, tuple) and r[0] == "ps"]
        if psr:
            reads = [r for r in reads if not (isinstance(r, tuple) and r[0] == "ps")]
            writes = list(writes) + psr
        deps = {}
        for r in reads:
            w = self.last_w.get(r)
            if w is not None:
                deps[id(w)] = w
            self.readers.setdefault(r, []).append(op)
        for wkey in writes:
            w = self.last_w.get(wkey)
            if w is not None:
                deps[id(w)] = w
            for rd in self.readers.get(wkey, ()):
                if rd is not op:
                    deps[id(rd)] = rd
            self.readers[wkey] = []
            self.last_w[wkey] = op
        for d in deps.values():
            if d is op:
                continue
            if d.is_mm and op.is_mm:
                continue
            if d.dma is not None and op.dma is d.dma and d.batch == op.batch:
                continue
            d.sig = True
            op.deps.append(d)

    def op(self, eng, fn, reads=(), writes=(), is_mm=False):
        o = _Op(eng, fn, is_mm=is_mm)
        self._track(o, reads, writes)
        self.ops[eng].append(o)
        return o

    def dma(self, eng, grp, out, in_, reads=(), writes=()):
        o = _Op(eng, lambda e: e.dma_start(out=out, in_=in_))
        o.dma = grp
        if grp.batch < 0:
            grp.new_batch()
        o.batch = grp.batch
        grp.ops.append(o)
        self._track(o, reads, writes)
        self.ops[eng].append(o)
        return o

    def mm(self, out, lhsT, rhs, start, stop, reads, writes):
        return self.op("pe", lambda e: e.matmul(out, lhsT, rhs, start=start, stop=stop), reads, writes, is_mm=True)

    def tr(self, out, in_, ident, reads, writes):
        return self.op("pe", lambda e: e.transpose(out, in_, ident), reads, writes, is_mm=True)

    def act(self, out, in_, func, reads, writes, bias=0.0, scale=1.0):
        return self.op("act", lambda e: e.activation(out, in_, func, bias=bias, scale=scale), reads, writes)

    def tt(self, eng, out, in0, in1, op, reads, writes):
        return self.op(eng, lambda e: e.tensor_tensor(out, in0, in1, op), reads, writes)

    def stt(self, eng, out, in0, scalar, in1, op0, op1, reads, writes):
        return self.op(eng, lambda e: e.scalar_tensor_tensor(out, in0, scalar, in1, op0, op1), reads, writes)

    def ts(self, eng, out, in0, s1, s2, op0, op1, reads, writes):
        if s2 is None:
            return self.op(eng, lambda e: e.tensor_scalar(out, in0, s1, None, op0), reads, writes)
        return self.op(eng, lambda e: e.tensor_scalar(out, in0, s1, s2, op0, op1), reads, writes)

    def copy(self, eng, out, in_, reads, writes):
        if eng == "act":
            return self.op("act", lambda e: e.activation(out, in_, AF.Copy), reads, writes)
        return self.op(eng, lambda e: e.tensor_copy(out, in_), reads, writes)

    def fence(self, fn, fams):
        fam = lambda k: k if isinstance(k, str) else k[0]
        keys = [k for k in (set(self.last_w) | set(self.readers)) if fam(k) in fams]
        return self.op("dve", fn, reads=(), writes=keys)

    def emit(self):
        nc = self.nc
        for e in self.ENGS:
            c = 0
            for o in self.ops[e]:
                if o.dma is None and o.sig:
                    c += 1
                    o.count = c
        for g in self.groups:
            n = 0
            for o in g.ops:
                n += 1
                g.cum[o.batch] = n
        with contextlib.ExitStack() as st:
            esem = {e: st.enter_context(nc.semaphore("s_" + e)) for e in self.ENGS}
            for g in self.groups:
                g.sem = st.enter_context(nc.semaphore("d_" + g.name))
            block = st.enter_context(nc.Block())

            def run(ename, eng):
                waited = {}
                for o in self.ops[ename]:
                    for d in o.deps:
                        if d.dma is not None:
                            sem, val = d.dma.sem, 16 * d.dma.cum[d.batch]
                        else:
                            sem, val = esem[d.eng], d.count
                        k = id(sem)
                        if waited.get(k, 0) < val:
                            eng.wait_ge(sem, val)
                            waited[k] = val
                    ins = o.fn(eng)
                    if o.dma is not None:
                        ins.then_inc(o.dma.sem, 16)
                    elif o.sig:
                        ins.then_inc(esem[ename], 1)

            @block.tensor
            def _(eng):
                run("pe", eng)

            @block.scalar
            def _(eng):
                run("act", eng)

            @block.vector
            def _(eng):
                run("dve", eng)

            @block.gpsimd
            def _(eng):
                run("pool", eng)

            @block.sync
            def _(eng):
                run("sp", eng)


class Ring:
    def __init__(self, items):
        self.items = items
        self.i = -1

    def next(self):
        self.i = (self.i + 1) % len(self.items)
        return self.items[self.i]


C_ONES = 0
C_BLK = 128
C_IDENT = 256
C_NEGTRI = 384
C_NEGONES = 512
C_MASK = 640
C_DN = C_MASK + 4 * 512
DN_TRI_LE = 0
DN_TRI_GT = 64
DN_NEG_STRICT = 128
DN_NEG_GET = 192
DN_IDENT64 = 256
DN_NEGONES = 320
DN_NCOL = 384
C_TOTAL = C_DN + DN_NCOL


def _consts():
    c = np.zeros((128, C_TOTAL), np.float32)
    c[:, C_ONES:C_ONES + 128] = 1.0
    c[:64, C_BLK:C_BLK + 64] = 1.0
    c[64:, C_BLK + 64:C_BLK + 128] = 1.0
    c[:, C_IDENT:C_IDENT + 128] = np.eye(128, dtype=np.float32)
    j = np.arange(128)[:, None]
    s = np.arange(128)[None, :]
    c[:, C_NEGTRI:C_NEGTRI + 128] = -(j >= s).astype(np.float32)
    c[:, C_NEGONES:C_NEGONES + 128] = -1.0
    t = np.arange(512)[None, :]
    for jj in range(4):
        c[:, C_MASK + jj * 512:C_MASK + (jj + 1) * 512] = ((j + 128 * jj) < t).astype(np.float32)
    a = np.arange(64)[:, None]
    b = np.arange(64)[None, :]
    for half in range(2):
        r = slice(half * 64, half * 64 + 64)
        o = C_DN
        c[r, o + DN_TRI_LE:o + DN_TRI_LE + 64] = (a <= b)
        c[r, o + DN_TRI_GT:o + DN_TRI_GT + 64] = (a > b)
        c[r, o + DN_NEG_STRICT:o + DN_NEG_STRICT + 64] = ((a > b) - 1.0) * 3e4
        c[r, o + DN_NEG_GET:o + DN_NEG_GET + 64] = ((b >= a) - 1.0) * 3e4
        c[r, o + DN_IDENT64:o + DN_IDENT64 + 64] = (a == b)
        c[r, o + DN_NEGONES:o + DN_NEGONES + 64] = -1.0
    return c


def build(plan=None):
    if plan is None:
        plan = [(l, True, True) for l in range(NL)]
    nc = bass.Bass("TRN2", target_bir_lowering=False)
    dt_in = lambda n, s: nc.dram_tensor(n, s, F32, kind="ExternalInput").ap()
    xT_d = dt_in("xT", [D, T])
    cT_d = dt_in("cT", [128, KC])
    ada_w_d = dt_in("ada_w", [NL, D, 6 * D])
    ada_b_d = dt_in("ada_b", [NL, 6 * D])
    n1g_d = dt_in("n1g", [128, NL * KC])
    n2g_d = dt_in("n2g", [128, NL * KC])
    dn_w_in_d = dt_in("dn_w_in", [2, D, 4112])
    dn_conv_d = dt_in("dn_conv", [128, 2 * 24 * 4])
    dn_alog_d = dt_in("dn_alog", [128, 16])
    dn_dtb_d = dt_in("dn_dtb", [128, 16])
    dn_onorm_d = dt_in("dn_onorm", [128, 2])
    dn_w_out_d = dt_in("dn_w_out", [2, D, D])
    sb_w_qkv_d = dt_in("sb_w_qkv", [2, D, 3 * D])
    sb_qg_d = dt_in("sb_qg", [128, 2])
    sb_kg_d = dt_in("sb_kg", [128, 2])
    sb_w_out_d = dt_in("sb_w_out", [2, D, D])
    ffn_w_in_d = dt_in("ffn_w_in", [NL, D, 2 * DFF])
    ffn_w_out_d = dt_in("ffn_w_out", [NL, DFF, D])
    cst_d = dt_in("cst", [128, C_TOTAL])
    yT_d = nc.dram_tensor("yT", [D, T], F32, kind="ExternalOutput").ap()

    P = Prog(nc)
    with contextlib.ExitStack() as st:
        sb = lambda n, s, d: st.enter_context(nc.sbuf_tensor("sb_" + n, s, d))
        xT = sb("xT", [128, KC, T], F32)
        hT = sb("hT", [128, KC, T], BF16)
        S = sb("S", [128, KC, T], BF16)
        wslots = [sb(f"wslot{i}", [128, 4096], BF16) for i in range(3)]
        wgrp = [P.dma_group(f"w{i}") for i in range(3)]
        wring = Ring(list(range(3)))
        adast = [sb(f"adast{i}", [128, 512], F32R) for i in range(3)]
        adagrp = [P.dma_group(f"a{i}") for i in range(3)]
        adaring = Ring(list(range(3)))
        adab = sb("adab", [1, 512], F32)
        adabgrp = P.dma_group("adab")
        modrow = adab
        modT = sb("modT", [128, NL, NMOD], F32)
        AB = sb("AB", [128, NL, 2, KC], F32)
        smalls = sb("smalls", [128, 2 * NL * KC + 16 + 16 + 2 + 2 + 2 + KC], F32)
        o_n1 = 0
        o_n2 = NL * KC
        o_alog = 2 * NL * KC
        o_dtb = o_alog + 16
        o_onorm = o_dtb + 16
        o_qg = o_onorm + 2
        o_kg = o_qg + 2
        o_c = o_kg + 2
        smalls_conv = sb("convw", [128, 192], F32)
        condr = sb("condr", [128, KC], F32R)
        one11 = sb("one11", [1, 1], F32)
        qgs = sb("qgs", [128, 2], F32)
        ones_bf = sb("ones_bf", [128, 128], BF16)
        blk_bf = sb("blk_bf", [128, 128], BF16)
        ident_bf = sb("ident_bf", [128, 128], BF16)
        negtri_r = sb("negtri_r", [128, 128], F16)
        negones_r = sb("negones_r", [128, 128], F16)
        masks = sb("masks", [128, 4, 512], BF16)
        sq = [sb(f"sq{i}", [128, 512], BF16) for i in range(2)]
        sqring = Ring([0, 1])
        rstd = [sb(f"rstd{i}", [128, 512], F32) for i in range(2)]
        rstdring = Ring([0, 1])
        tmp = [sb(f"tmp{i}", [128, 512], F32) for i in range(2)]
        tmpring = Ring([0, 1])
        qT = sb("qT", [128, T], BF16)
        kT = sb("kT", [128, T], BF16)
        vS = sb("vS", [128, 16, 128], BF16)
        sp4 = sb("sp4", [128, 2048], F16)
        spt = [sp4[:, i * 512:(i + 1) * 512] for i in range(3)]
        spring = Ring([0, 1, 2])
        spsum = sp4[:, 1536:2048]
        aT = [sb(f"aT{i}", [128, 512], BF16) for i in range(2)]
        aring = Ring([0, 1])

        wab = sb("wab", [128, KC, 16], BF16)
        wabgrp = P.dma_group("wab")
        dncst = sb("dncst", [128, DN_NCOL], F32)
        onesf = sb("onesf", [128, 128], F32)
        dummy = sb("dummy", [1, 8], F32)
        S2 = S[:].rearrange("p k t -> p (k t)")
        d_oh = S2[:, 0:2048]
        d_ktok = S2[:, 2176:4224].rearrange("p (a b) -> p a b", a=16)
        d_ktail = S2[:, 4224:6272].rearrange("p (a b) -> p a b", a=16)
        d_nwt = S2[:, 6272:8320]
        d_Y = S2[:, 8320:10368].bitcast(F32).rearrange("p (a b) -> p a b", a=16)
        d_attn = S2[:, 10368:11392].rearrange("p (a b) -> p a b", a=16)
        d_TT = S2[:, 11392:12416].rearrange("p (a b) -> p a b", a=16)
        d_diag = S2[:, 12416:13952].rearrange("p (a b) -> p a b", a=12)
        d_sc = S2[:, 13952:16384].bitcast(F32)
        d_ab = d_sc[:, 0:256].rearrange("p (a b) -> p a b", a=16)
        sc3 = lambda i: d_sc[:, 256 + i * 128:256 + (i + 1) * 128].rearrange("p (a b) -> p a b", a=16)
        d_g, d_beta, d_nbeta, d_beg, d_et, d_egl0, d_egl1 = [sc3(i) for i in range(7)]
        d_nea = d_sc[:, 1152:1160]
        dxbuf = sb("dxbuf", [128, 4, 512], BF16)
        d_pre = sb("pre2", [128, 2176], BF16)[:]
        d_vf = sp4[:].bitcast(BF16)
        qTalt = masks[:].rearrange("p j t -> p (j t)")
        d_X = [dxbuf[:, 0, :].rearrange("p (a b) -> p a b", a=8), dxbuf[:, 1, :].rearrange("p (a b) -> p a b", a=8)]
        d_XT = [dxbuf[:, 2, :].rearrange("p (a b) -> p a b", a=8), dxbuf[:, 3, :].rearrange("p (a b) -> p a b", a=8)]
        d_R = rstd[1][:]
        d_Rb = aT[0][:].rearrange("p (a b) -> p a b", a=8)
        d_S = aT[1][:, 0:256].bitcast(F32)
        d_Sb = aT[1][:, 256:384]
        d_vnb = aT[1][:, 384:512]
        ARENA_FAMS = ("S", "sp", "spsum", "aT", "dn", "masks", "qTb")

        ps = [st.enter_context(nc.psum_tensor(f"ps{i}", [128, 512], F32)) for i in range(8)]
        psP = Ring([0, 1])
        PS_N = 2
        psZ = Ring([3, 4])
        psE = Ring([5, 6])
        psZE = Ring([3, 4, 5, 6])
        PS_O = 7

        g0 = P.dma_group("ld0")
        for kc in range(KC):
            P.dma("sp", g0, xT[:, kc, :], xT_d[kc * 128:(kc + 1) * 128, :],
                  writes=[("x", kc, tb) for tb in range(NTB)])
        gs = P.dma_group("lds")
        P.dma("sp", gs, smalls[:, o_n1:o_n1 + NL * KC], n1g_d, writes=["smalls"])
        P.dma("sp", gs, smalls[:, o_n2:o_n2 + NL * KC], n2g_d, writes=["smalls"])
        P.dma("sp", gs, smalls[:, o_alog:o_alog + 16], dn_alog_d, writes=["smalls"])
        P.dma("sp", gs, smalls[:, o_dtb:o_dtb + 16], dn_dtb_d, writes=["smalls"])
        P.dma("sp", gs, smalls[:, o_onorm:o_onorm + 2], dn_onorm_d, writes=["smalls"])
        P.dma("sp", gs, smalls[:, o_qg:o_qg + 2], sb_qg_d, writes=["smalls"])
        P.dma("sp", gs, smalls[:, o_kg:o_kg + 2], sb_kg_d, writes=["smalls"])
        P.dma("sp", gs, smalls[:, o_c:o_c + KC], cT_d, writes=["smalls"])
        P.dma("sp", gs, smalls_conv[:], dn_conv_d, writes=["convw"])
        gc = P.dma_group("ldc")
        P.dma("pool", gc, ones_bf[:], cst_d[:, C_ONES:C_ONES + 128], writes=["cst"])
        P.dma("pool", gc, blk_bf[:], cst_d[:, C_BLK:C_BLK + 128], writes=["cst"])
        P.dma("pool", gc, ident_bf[:], cst_d[:, C_IDENT:C_IDENT + 128], writes=["cst"])
        P.dma("pool", gc, negtri_r[:], cst_d[:, C_NEGTRI:C_NEGTRI + 128], writes=["cst"])
        P.dma("pool", gc, negones_r[:], cst_d[:, C_NEGONES:C_NEGONES + 128], writes=["cst"])
        mgrp = P.dma_group("masks")
        P.op("dve", lambda e: e.memset(one11[:], 1.0), writes=["one11"])
        P.dma("sp", gs, dncst[:], cst_d[:, C_DN:C_DN + DN_NCOL], writes=["smalls"])
        P.dma("sp", gs, onesf[:], cst_d[:, C_ONES:C_ONES + 128], writes=["smalls"])
        P.act(condr[:], smalls[:, o_c:o_c + KC], AF.Silu, reads=["smalls"], writes=["condr"])
        P.ts("dve", qgs[:], smalls[:, o_qg:o_qg + 2], 0.125, None, ALU.mult, None, reads=["smalls"], writes=["qgs"])

        def wload(pieces):
            s = wring.next()
            wgrp[s].new_batch()
            for (off, kcs, ncols, src) in pieces:
                dst = wslots[s][:, off:off + kcs * ncols].rearrange("p (k n) -> p k n", k=kcs)
                P.dma("pool", wgrp[s], dst, src.rearrange("(k p) n -> p k n", p=128), writes=[("w", s)])
            return s

        def wview(s, off, kcs, ncols):
            return wslots[s][:, off:off + kcs * ncols].rearrange("p (k n) -> p k n", k=kcs)

        def ada_block(l, nb):
            pso = ps[PS_O]
            for kc in range(KC):
                a = adaring.next()
                adagrp[a].new_batch()
                P.dma("pool", adagrp[a], adast[a][:], ada_w_d[l, kc * 128:(kc + 1) * 128, nb * 512:(nb + 1) * 512],
                      writes=[("adast", a)])
                P.mm(pso[0:1, :], condr[:, kc:kc + 1], adast[a][:], kc == 0, kc == KC - 1,
                     reads=[("adast", a), "condr"], writes=[("ps", PS_O)])
            adabgrp.new_batch()
            P.dma("sp", adabgrp, adab[:], ada_b_d[l:l + 1, nb * 512:(nb + 1) * 512], writes=["adab"])
            P.tt("dve", modrow[:], pso[0:1, :], adab[:], ALU.add, reads=[("ps", PS_O), "adab"], writes=["adab"])
            for j in range(4):
                col = nb * 4 + j
                P.mm(ps[PS_N][:, col:col + 1], modrow[0:1, j * 128:(j + 1) * 128], one11[0:1, 0:1], True, True,
                     reads=["adab", "one11"], writes=[("ps", PS_N)])
            P.copy("dve", modT[:, l, nb * 4:(nb + 1) * 4], ps[PS_N][:, nb * 4:(nb + 1) * 4], reads=[("ps", PS_N)], writes=[("modT", l)])
            if nb == 11:
                P.stt("dve", AB[:, l, 0, :], modT[:, l, 8:16], 1.0, smalls[:, o_n1 + l * KC:o_n1 + (l + 1) * KC],
                      ALU.add, ALU.mult, reads=[("modT", l), "smalls"], writes=[("AB", l)])
                P.stt("dve", AB[:, l, 1, :], modT[:, l, 32:40], 1.0, smalls[:, o_n2 + l * KC:o_n2 + (l + 1) * KC],
                      ALU.add, ALU.mult, reads=[("modT", l), "smalls"], writes=[("AB", l)])

        def norm_mod(l, which):
            sh = 0 if which == 0 else 24
            for tb in range(NTB):
                tsl = slice(tb * 512, (tb + 1) * 512)
                for kc in range(KC):
                    i = sqring.next()
                    P.act(sq[i][:], xT[:, kc, tsl], AF.Square, reads=[("x", kc, tb)], writes=[("sq", i)])
                    P.mm(ps[PS_N][:], ones_bf[:], sq[i][:], kc == 0, kc == KC - 1,
                         reads=[("sq", i), "cst"], writes=[("ps", PS_N)])
                r = rstdring.next()
                P.act(rstd[r][:], ps[PS_N][:], AF.Ln, reads=[("ps", PS_N)], writes=[("rstd", r)], bias=EPS, scale=1.0 / D)
                P.act(rstd[r][:], rstd[r][:], AF.Exp, reads=[("rstd", r)], writes=[("rstd", r)], scale=-0.5)
                for kc in range(KC):
                    i = tmpring.next()
                    P.tt("dve", tmp[i][:], xT[:, kc, tsl], rstd[r][:], ALU.mult,
                         reads=[("x", kc, tb), ("rstd", r)], writes=[("tmp", i)])
                    P.act(hT[:, kc, tsl], tmp[i][:], AF.Identity, reads=[("tmp", i), ("AB", l), ("modT", l)],
                          writes=[("h", kc, tb)], bias=modT[:, l, sh + kc:sh + kc + 1], scale=AB[:, l, which, kc:kc + 1])

        def out_proj(l, w_d, nkc, gate_off, skeys_fn, s_view_fn):
            for mb in range(2 if nkc <= 8 else 4):
                ncols = 512 if nkc <= 8 else 256
                s = wload([(0, nkc, ncols, w_d[:, mb * ncols:(mb + 1) * ncols])])
                wv = wview(s, 0, nkc, ncols)
                for mc in range(ncols // 128):
                    m = mb * (ncols // 128) + mc
                    for tb in range(NTB):
                        tsl = slice(tb * 512, (tb + 1) * 512)
                        p = psP.next()
                        for kc in range(nkc):
                            P.mm(ps[p][:], wv[:, kc, mc * 128:(mc + 1) * 128], s_view_fn(kc, tsl), kc == 0, kc == nkc - 1,
                                 reads=[("w", s), skeys_fn(kc, tb)], writes=[("ps", p)])
                        P.stt("dve", xT[:, m, tsl], ps[p][:], modT[:, l, gate_off + m:gate_off + m + 1], xT[:, m, tsl],
                              ALU.mult, ALU.add, reads=[("ps", p), ("modT", l), ("x", m, tb)], writes=[("x", m, tb)])

        def ffn(l, ada_next):
            groups = [(0, 8), (8, 7), (15, 7)]
            for gi, (c0, ncg) in enumerate(groups):
                j = 0
                while j < ncg:
                    nsub = min(4, ncg - j)
                    cc = c0 + j
                    sg = wload([(0, KC, nsub * 128, ffn_w_in_d[l, :, cc * 128:(cc + nsub) * 128])])
                    su = wload([(0, KC, nsub * 128, ffn_w_in_d[l, :, DFF + cc * 128:DFF + (cc + nsub) * 128])])
                    wg = wview(sg, 0, KC, nsub * 128)
                    wu = wview(su, 0, KC, nsub * 128)
                    for q in range(nsub):
                        for tb in range(NTB):
                            tsl = slice(tb * 512, (tb + 1) * 512)
                            pg = psZ.next()
                            pu = psE.next()
                            for kc in range(KC):
                                P.mm(ps[pg][:], wg[:, kc, q * 128:(q + 1) * 128], hT[:, kc, tsl], kc == 0, kc == KC - 1,
                                     reads=[("w", sg), ("h", kc, tb)], writes=[("ps", pg)])
                            for kc in range(KC):
                                P.mm(ps[pu][:], wu[:, kc, q * 128:(q + 1) * 128], hT[:, kc, tsl], kc == 0, kc == KC - 1,
                                     reads=[("w", su), ("h", kc, tb)], writes=[("ps", pu)])
                            i = tmpring.next()
                            P.act(tmp[i][:], ps[pg][:], AF.Silu, reads=[("ps", pg)], writes=[("tmp", i)])
                            P.tt("dve", S[:, j + q, tsl], tmp[i][:], ps[pu][:], ALU.mult,
                                 reads=[("tmp", i), ("ps", pu)], writes=[("S", j + q, tb)])
                    j += nsub
                out_proj(l, ffn_w_out_d[l, c0 * 128:(c0 + ncg) * 128, :], ncg, 40,
                         lambda kc, tb: ("S", kc, tb), lambda kc, tsl: S[:, kc, tsl])
                if ada_next is not None:
                    for nb in range(gi * 4, gi * 4 + 4):
                        ada_block(ada_next, nb)

        ada_hook = [None]

        def run_hook(i):
            if ada_hook[0] is not None and 1 <= i <= 4:
                for nb in range((i - 1) * 3, (i - 1) * 3 + 3):
                    ada_block(ada_hook[0], nb)

        def sb_mixer(l):
            jl = l // 2
            w_d = sb_w_qkv_d[jl]
            mgrp.new_batch()
            P.dma("pool", mgrp, masks[:], cst_d[:, C_MASK:C_MASK + 2048].rearrange("p (j t) -> p j t", j=4), writes=["masks"])
            for hp in range(8):
                s = wload([(0, KC, 128, w_d[:, hp * 128:(hp + 1) * 128]),
                           (1024, KC, 128, w_d[:, D + hp * 128:D + (hp + 1) * 128]),
                           (2048, KC, 128, w_d[:, 2 * D + hp * 128:2 * D + (hp + 1) * 128])])
                wq = wview(s, 0, KC, 128)
                wk = wview(s, 1024, KC, 128)
                wv = wview(s, 2048, KC, 128)
                for (wmat, dst, dkey, gcol) in ((wq, qT, "qT", qgs[:, jl:jl + 1]),
                                                (wk, kT, "kT", smalls[:, o_kg + jl:o_kg + jl + 1])):
                    for tb in range(NTB):
                        tsl = slice(tb * 512, (tb + 1) * 512)
                        p = psP.next()
                        for kc in range(KC):
                            P.mm(ps[p][:], wmat[:, kc, :], hT[:, kc, tsl], kc == 0, kc == KC - 1,
                                 reads=[("w", s), ("h", kc, tb)], writes=[("ps", p)])
                        i = sqring.next()
                        P.act(sq[i][:], ps[p][:], AF.Square, reads=[("ps", p)], writes=[("sq", i)])
                        P.mm(ps[PS_N][:], blk_bf[:], sq[i][:], True, True, reads=[("sq", i), "cst"], writes=[("ps", PS_N)])
                        r = rstdring.next()
                        P.act(rstd[r][:], ps[PS_N][:], AF.Ln, reads=[("ps", PS_N)], writes=[("rstd", r)], bias=EPS, scale=1.0 / 64)
                        P.act(rstd[r][:], rstd[r][:], AF.Exp, reads=[("rstd", r)], writes=[("rstd", r)], scale=-0.5)
                        P.stt("dve", dst[:, tsl], ps[p][:], gcol, rstd[r][:], ALU.mult, ALU.mult,
                              reads=[("ps", p), ("rstd", r), "qgs", "smalls"], writes=[(dkey, tb)])
                for t4 in range(4):
                    p = psP.next()
                    for jj in range(4):
                        tt_ = t4 * 4 + jj
                        for kc in range(KC):
                            P.mm(ps[p][:, jj * 128:(jj + 1) * 128], hT[:, kc, tt_ * 128:(tt_ + 1) * 128], wv[:, kc, :],
                                 kc == 0, kc == KC - 1, reads=[("w", s), ("h", kc, t4)], writes=[("ps", p)])
                    P.copy("act", vS[:, t4 * 4:(t4 + 1) * 4, :], ps[p][:].rearrange("p (j n) -> p j n", j=4),
                           reads=[("ps", p)], writes=[("vS", t4)])
                tiles = []
                for hd in range(2):
                    for qb in range(NTB):
                        nkb = 4 * (qb + 1)
                        for kb in range(nkb - 1, -1, -1):
                            tiles.append(dict(hd=hd, qb=qb, kb=kb, first=(kb == nkb - 1), last=(kb == 0),
                                              diag=(kb >= 4 * qb), jm=kb - 4 * qb))

                def stA(t):
                    r0 = t["hd"] * 64
                    ksl = slice(t["kb"] * 128, (t["kb"] + 1) * 128)
                    c0 = 128 * t["jm"] if t["diag"] else 0
                    t["c0"] = c0
                    qsl = slice(t["qb"] * 512 + c0, (t["qb"] + 1) * 512)
                    cs = slice(c0, 512)
                    pz = psZ.next()
                    P.mm(ps[pz][:, cs], kT[r0:r0 + 64, ksl], qT[r0:r0 + 64, qsl], True, True,
                         reads=[("kT", t["kb"] // 4), ("qT", t["qb"])], writes=[("ps", pz)])
                    si = spring.next()
                    t["si"] = si
                    P.act(spt[si][:, cs], ps[pz][:, cs], AF.Exp, reads=[("ps", pz)], writes=[("sp", si)])
                    P.act(spt[si][:, cs], spt[si][:, cs], AF.Ln, reads=[("sp", si)], writes=[("sp", si)], bias=1.0)
                    if t["diag"]:
                        P.tt("dve", spt[si][:, cs], spt[si][:, cs], masks[:, t["jm"], cs], ALU.mult,
                             reads=[("sp", si), "masks"], writes=[("sp", si)])

                def stB(t):
                    r0 = t["hd"] * 64
                    ksl = slice(t["kb"] * 128, (t["kb"] + 1) * 128)
                    c0 = t["c0"]
                    qsl = slice(t["qb"] * 512 + c0, (t["qb"] + 1) * 512)
                    cs = slice(c0, 512)
                    si = t["si"]
                    pe_ = psE.next()
                    P.mm(ps[pe_][:, cs], kT[r0:r0 + 64, ksl], qT[r0:r0 + 64, qsl], True, False,
                         reads=[("kT", t["kb"] // 4), ("qT", t["qb"])], writes=[("ps", pe_)])
                    P.mm(ps[pe_][:, cs], negtri_r[:], spt[si][:, cs], False, t["first"],
                         reads=[("sp", si), "cst"], writes=[("ps", pe_)])
                    if not t["first"]:
                        P.mm(ps[pe_][:, cs], negones_r[:], spsum[:, cs], False, True,
                             reads=["spsum", "cst"], writes=[("ps", pe_)])
                    if not t["last"]:
                        if t["first"]:
                            if c0 > 0:
                                P.ts("dve", spsum[:, 0:c0], masks[:, 0, 0:c0], 0.0, None, ALU.mult, None,
                                     reads=["masks"], writes=["spsum"])
                            P.copy("dve", spsum[:, cs], spt[si][:, cs], reads=[("sp", si)], writes=["spsum"])
                        else:
                            P.tt("dve", spsum[:, cs], spsum[:, cs], spt[si][:, cs], ALU.add,
                                 reads=[("sp", si), "spsum"], writes=["spsum"])
                    ai = aring.next()
                    t["ai"] = ai
                    if t["first"] and c0 > 0:
                        P.op("dve", lambda e, ai=ai, c0=c0: e.memset(aT[ai][:, 0:c0], 0.0), writes=[("aT", ai)])
                    P.act(aT[ai][:, cs], ps[pe_][:, cs], AF.Exp, reads=[("ps", pe_)], writes=[("aT", ai)])
                    if t["diag"]:
                        P.tt("dve", aT[ai][:, cs], aT[ai][:, cs], masks[:, t["jm"], cs], ALU.mult,
                             reads=[("aT", ai), "masks"], writes=[("aT", ai)])

                def stC(t):
                    r0 = t["hd"] * 64
                    qsl = slice(t["qb"] * 512, (t["qb"] + 1) * 512)
                    ai = t["ai"]
                    cs = slice(0, 512) if t["first"] else slice(t["c0"], 512)
                    P.mm(ps[PS_O][:, cs], vS[:, t["kb"], :], aT[ai][:, cs], t["first"], t["last"],
                         reads=[("vS", t["kb"] // 4), ("aT", ai)], writes=[("ps", PS_O)])
                    if t["last"]:
                        P.copy("act", S[r0:r0 + 64, hp, qsl], ps[PS_O][r0:r0 + 64, :],
                               reads=[("ps", PS_O)], writes=[("S", hp, t["qb"])])

                nt = len(tiles)
                for idx in range(nt + 2):
                    if idx < nt:
                        stA(tiles[idx])
                    if 0 <= idx - 1 < nt:
                        stB(tiles[idx - 1])
                    if 0 <= idx - 2 < nt:
                        stC(tiles[idx - 2])
                run_hook(hp)
            out_proj(l, sb_w_out_d[jl], KC, 16, lambda kc, tb: ("S", kc, tb), lambda kc, tsl: S[:, kc, tsl])

        class _Stop(Exception):
            pass

        def stage(n):
            import os
            if float(os.environ.get("KDN", "99")) < n:
                raise _Stop()

        def dn_mixer(l):
            try:
                dn_mixer_(l)
            except _Stop:
                P.fence(lambda e: e.memset(dummy[0:1, 1:2], 0.0), ARENA_FAMS)

        def dn_mixer_(l):
            jl = l // 2
            w_d = dn_w_in_d[jl]
            K = lambda *a: ("dn",) + a
            tri_le = dncst[:, DN_TRI_LE:DN_TRI_LE + 64]
            tri_gt = dncst[:, DN_TRI_GT:DN_TRI_GT + 64]
            neg_strict = dncst[:, DN_NEG_STRICT:DN_NEG_STRICT + 64]
            neg_get = dncst[:, DN_NEG_GET:DN_NEG_GET + 64]
            ident64 = dncst[:, DN_IDENT64:DN_IDENT64 + 64]
            negones = dncst[:, DN_NEGONES:DN_NEGONES + 64]
            HS = [slice(0, 64), slice(64, 128)]
            P.fence(lambda e: e.memset(dummy[0:1, 0:1], 0.0), ARENA_FAMS)
            wabgrp.new_batch()
            P.dma("pool", wabgrp, wab[:], w_d[:, 4096:4112].rearrange("(k p) n -> p k n", p=128), writes=["wab"])
            for tt_ in range(16):
                for kc in range(KC):
                    P.mm(ps[PS_O][:, tt_ * 16:(tt_ + 1) * 16], hT[:, kc, tt_ * 128:(tt_ + 1) * 128], wab[:, kc, :],
                         kc == 0, kc == KC - 1, reads=["wab", ("h", kc, tt_ // 4)], writes=[("ps", PS_O)])
            P.copy("dve", d_ab, ps[PS_O][:, 0:256].rearrange("p (a b) -> p a b", a=16), reads=[("ps", PS_O)], writes=[K("ab")])
            P.act(d_nea, smalls[:, o_alog + jl * 8:o_alog + jl * 8 + 8], AF.Exp, reads=["smalls"], writes=[K("nea")])
            P.ts("dve", d_nea, d_nea, -1.0, None, ALU.mult, None, reads=[K("nea")], writes=[K("nea")])
            dtb_bc = smalls[:, o_dtb + jl * 8:o_dtb + jl * 8 + 8][:, None, :].broadcast_to([128, 16, 8])
            P.tt("dve", d_g, d_ab[:, :, 0:8], dtb_bc, ALU.add, reads=[K("ab"), "smalls"], writes=[K("g")])
            P.act(d_g, d_g, AF.Exp, reads=[K("g")], writes=[K("g")])
            P.act(d_g, d_g, AF.Ln, reads=[K("g")], writes=[K("g")], bias=1.0)
            P.tt("dve", d_g, d_g, d_nea[:, None, :].broadcast_to([128, 16, 8]), ALU.mult, reads=[K("g"), K("nea")], writes=[K("g")])
            P.act(d_beta, d_ab[:, :, 8:16], AF.Exp, reads=[K("ab")], writes=[K("beta")], scale=-1.0)
            P.ts("dve", d_beta, d_beta, 1.0, None, ALU.add, None, reads=[K("beta")], writes=[K("beta")])
            P.op("dve", lambda e: e.reciprocal(d_beta, d_beta), reads=[K("beta")], writes=[K("beta")])
            P.ts("dve", d_nbeta, d_beta, -1.0, None, ALU.mult, None, reads=[K("beta")], writes=[K("nbeta")])
            g2d = d_g.rearrange("p a b -> p (a b)")
            for par in range(2):
                hs = HS[par]
                P.mm(ps[PS_N][hs, 0:128], tri_le[hs, :], g2d[hs, :], True, True, reads=[K("g"), "smalls"], writes=[("ps", PS_N)])
                P.mm(ps[PS_N][hs, 128:256], tri_gt[hs, :], g2d[hs, :], True, True, reads=[K("g"), "smalls"], writes=[("ps", PS_N)])
            pgl = [PS_O, psP.next()]
            for par in range(2):
                P.mm(ps[pgl[par]][:, 0:128], onesf[HS[par], :], g2d[HS[par], :], True, True,
                     reads=[K("g"), "smalls"], writes=[("ps", pgl[par])])
            f2 = lambda v: v.rearrange("p a b -> p (a b)")
            P.act(f2(d_beg), ps[PS_N][:, 0:128], AF.Exp, reads=[("ps", PS_N)], writes=[K("beg")])
            P.tt("dve", f2(d_beg), f2(d_beg), f2(d_beta), ALU.mult, reads=[K("beg"), K("beta")], writes=[K("beg")])
            P.act(f2(d_et), ps[PS_N][:, 128:256], AF.Exp, reads=[("ps", PS_N)], writes=[K("et")])
            P.act(f2(d_egl0), ps[pgl[0]][:, 0:128], AF.Exp, reads=[("ps", pgl[0])], writes=[K("egl")])
            P.act(f2(d_egl1), ps[pgl[1]][:, 0:128], AF.Exp, reads=[("ps", pgl[1])], writes=[K("egl")])
            d_egl = [d_egl0, d_egl1]

            hinfo = {}

            def prep1(h):
                qb, qk = (qT, "qT") if h % 2 == 0 else (qTalt, "qTb")
                s = wload([(0, KC, 128, w_d[:, h * 128:(h + 1) * 128]),
                           (1024, KC, 128, w_d[:, D + h * 128:D + (h + 1) * 128]),
                           (2048, KC, 128, w_d[:, 2 * D + h * 128:2 * D + (h + 1) * 128]),
                           (3072, KC, 128, w_d[:, 3 * D + h * 128:3 * D + (h + 1) * 128])])
                wz = wview(s, 3072, KC, 128)
                for which in range(3):
                    ch = which * 8 + h
                    for tap in range(4):
                        col = smalls_conv[:, ((jl * 24 + ch) * 4 + tap):((jl * 24 + ch) * 4 + tap) + 1]
                        P.ts("dve", d_diag[:, which * 4 + tap, :], ident_bf[:], col, None, ALU.mult, None,
                             reads=["cst", "convw"], writes=[K("diag", which)])
                yield
                for which in range(3):
                    wmat = wview(s, which * 1024, KC, 128)
                    P.op("dve", lambda e: e.memset(d_pre[:, 0:3], 0.0), writes=[K("pre", 0)])
                    for tb in range(NTB):
                        tsl = slice(tb * 512, (tb + 1) * 512)
                        p = psP.next()
                        for kc in range(KC):
                            P.mm(ps[p][:], wmat[:, kc, :], hT[:, kc, tsl], kc == 0, kc == KC - 1,
                                 reads=[("w", s), ("h", kc, tb)], writes=[("ps", p)])
                            if kc == 3:
                                yield
                        P.copy("act", d_pre[:, 3 + tb * 512:3 + (tb + 1) * 512], ps[p][:], reads=[("ps", p)], writes=[K("pre", tb), K("pre", tb + 1)])
                        yield
                    for tb in range(NTB):
                        tsl = slice(tb * 512, (tb + 1) * 512)
                        pc = psZ.next()
                        for tap in range(4):
                            P.mm(ps[pc][:], d_diag[:, which * 4 + tap, :], d_pre[:, tb * 512 + tap:tb * 512 + tap + 512],
                                 tap == 0, tap == 3, reads=[K("diag", which), K("pre", tb), K("pre", tb + 1)], writes=[("ps", pc)])
                        yield
                        if which == 2:
                            P.act(d_vf[:, tsl], ps[pc][:], AF.Silu, reads=[("ps", pc)], writes=[K("vf", tb)])
                        else:
                            dst, dkey = (qb, qk) if which == 0 else (kT, "kT")
                            i = tmpring.next()
                            P.act(tmp[i][:], ps[pc][:], AF.Silu, reads=[("ps", pc)], writes=[("tmp", i)])
                            j = sqring.next()
                            P.act(sq[j][:], tmp[i][:], AF.Square, reads=[("tmp", i)], writes=[("sq", j)])
                            P.mm(ps[PS_N][:], ones_bf[:], sq[j][:], True, True, reads=[("sq", j), "cst"], writes=[("ps", PS_N)])
                            r = rstdring.next()
                            P.act(rstd[r][:], ps[PS_N][:], AF.Ln, reads=[("ps", PS_N)], writes=[("rstd", r)], bias=EPS, scale=1.0)
                            P.act(rstd[r][:], rstd[r][:], AF.Exp, reads=[("rstd", r)], writes=[("rstd", r)], scale=-0.5)
                            P.stt("dve", dst[:, tsl], tmp[i][:], (128.0 ** -0.5) if which == 0 else 1.0, rstd[r][:], ALU.mult, ALU.mult,
                                  reads=[("tmp", i), ("rstd", r)], writes=[(dkey, tb)])
                        yield
                hinfo[h] = (s, wz)

            def part2(h):
                qb, qk = (qT, "qT") if h % 2 == 0 else (qTalt, "qTb")
                for (src, skey, dst, dkey) in ((kT, "kT", d_ktok, "ktok"), (d_vf, None, vS, "vS")):
                    for t4 in range(4):
                        p = psP.next()
                        for jj in range(4):
                            tt_ = t4 * 4 + jj
                            rk = (skey, t4) if skey else K("vf", t4)
                            P.mm(ps[p][:, jj * 128:(jj + 1) * 128], src[:, tt_ * 128:(tt_ + 1) * 128], ident_bf[:], True, True,
                                 reads=[rk, "cst"], writes=[("ps", p)])
                        wk_ = K("ktok", t4) if dkey == "ktok" else ("vS", t4)
                        P.copy("act", dst[:, t4 * 4:(t4 + 1) * 4, :], ps[p][:].rearrange("p (j n) -> p j n", j=4),
                               reads=[("ps", p)], writes=[wk_])
                allk = [K("ktok", t4) for t4 in range(4)]
                allv = [("vS", t4) for t4 in range(4)]
                bc = lambda v: v[:, :, h:h + 1].broadcast_to([128, 16, 128])
                P.tt("dve", vS[:], vS[:], bc(d_beta), ALU.mult, reads=allv + [K("beta")], writes=allv)
                P.tt("dve", d_ktail, d_ktok, bc(d_et), ALU.mult, reads=allk + [K("et")], writes=[K("ktail")])
                P.tt("dve", d_ktok, d_ktok, bc(d_beg), ALU.mult, reads=allk + [K("beg")], writes=allk)
                P.tt("dve", d_Y, tri_le[:, None, :].broadcast_to([128, 16, 64]), d_g[:, :, h:h + 1].broadcast_to([128, 16, 64]),
                     ALU.mult, reads=[K("g"), "smalls"], writes=[K("Y")])
                for gq in range(2):
                    def blocks():
                        for t8 in range(8):
                            for par in range(2):
                                tt_ = gq * 8 + t8
                                yield t8, tt_, HS[par], par, tt_ * 128 + par * 64, slice(t8 * 64, (t8 + 1) * 64)
                    kkeys = [("kT", gq * 2), ("kT", gq * 2 + 1)]
                    qkeys = [(qk, gq * 2), (qk, gq * 2 + 1)]
                    pk = psZ.next()
                    for t8, tt_, hs, par, c0, cs in blocks():
                        P.mm(ps[pk][hs, cs], kT[:, c0:c0 + 64], kT[:, c0:c0 + 64], True, True, reads=kkeys, writes=[("ps", pk)])
                    pd = psE.next()
                    for t8, tt_, hs, par, c0, cs in blocks():
                        P.mm(ps[pd][hs, cs], d_Y[hs, tt_, :], onesf[hs, 0:64], True, False, reads=[K("Y"), "smalls"], writes=[("ps", pd)])
                        P.mm(ps[pd][hs, cs], negones[hs, :], d_Y[hs, tt_, :], False, True, reads=[K("Y"), "smalls"], writes=[("ps", pd)])
                    i = tmpring.next()
                    v3 = lambda ap: ap.rearrange("p (a b) -> p a b", a=8)
                    P.tt("dve", v3(tmp[i][:]), v3(ps[pd][:]), neg_strict[:, None, :].broadcast_to([128, 8, 64]), ALU.add,
                         reads=[("ps", pd), "smalls"], writes=[("tmp", i)])
                    P.act(tmp[i][:], tmp[i][:], AF.Exp, reads=[("tmp", i)], writes=[("tmp", i)])
                    P.tt("dve", tmp[i][:], tmp[i][:], ps[pk][:], ALU.mult, reads=[("tmp", i), ("ps", pk)], writes=[("tmp", i)])
                    cur, nxt = 0, 1
                    nb_bc = d_nbeta[:, gq * 8:(gq + 1) * 8, h:h + 1].broadcast_to([128, 8, 64])
                    P.tt("dve", d_XT[cur], v3(tmp[i][:]), nb_bc, ALU.mult, reads=[("tmp", i), K("nbeta")], writes=[K("XT", cur)])
                    px = psZ.next()
                    for t8, tt_, hs, par, c0, cs in blocks():
                        P.mm(ps[px][hs, cs], d_XT[cur][hs, t8, :], ident_bf[hs, par * 64:par * 64 + 64], True, True,
                             reads=[K("XT", cur), "cst"], writes=[("ps", px)])
                    P.copy("act", d_X[cur], v3(ps[px][:]), reads=[("ps", px)], writes=[K("X", cur)])
                    P.tt("dve", v3(d_R), v3(ps[px][:]), ident64[:, None, :].broadcast_to([128, 8, 64]), ALU.add,
                         reads=[("ps", px), "smalls"], writes=[("rstd", 1)])
                    P.copy("act", d_Rb, v3(d_R), reads=[("rstd", 1)], writes=[K("Rb")])
                    for lvl in range(1, 6):
                        if lvl < 5:
                            p2 = psZ.next()
                            for t8, tt_, hs, par, c0, cs in blocks():
                                P.mm(ps[p2][hs, cs], d_XT[cur][hs, t8, :], d_X[cur][hs, t8, :], True, True,
                                     reads=[K("XT", cur), K("X", cur)], writes=[("ps", p2)])
                        p2t = psE.next()
                        for t8, tt_, hs, par, c0, cs in blocks():
                            P.mm(ps[p2t][hs, cs], d_X[cur][hs, t8, :], d_XT[cur][hs, t8, :], True, True,
                                 reads=[K("XT", cur), K("X", cur)], writes=[("ps", p2t)])
                        if lvl < 5:
                            P.copy("act", d_X[nxt], v3(ps[p2][:]), reads=[("ps", p2)], writes=[K("X", nxt)])
                        P.copy("dve", d_XT[nxt], v3(ps[p2t][:]), reads=[("ps", p2t)], writes=[K("XT", nxt)])
                        pr = psP.next()
                        for t8, tt_, hs, par, c0, cs in blocks():
                            P.mm(ps[pr][hs, cs], d_XT[nxt][hs, t8, :], d_Rb[hs, t8, :], True, True,
                                 reads=[K("XT", nxt), K("Rb")], writes=[("ps", pr)])
                        P.tt("dve", d_R, d_R, ps[pr][:], ALU.add, reads=[("rstd", 1), ("ps", pr)], writes=[("rstd", 1)])
                        if lvl < 5:
                            P.copy("act", d_Rb, v3(d_R), reads=[("rstd", 1)], writes=[K("Rb")])
                        else:
                            P.copy("act", d_TT[:, gq * 8:(gq + 1) * 8, :], v3(d_R), reads=[("rstd", 1)], writes=[K("TT", gq)])
                        cur, nxt = nxt, cur
                    pq = psZ.next()
                    for t8, tt_, hs, par, c0, cs in blocks():
                        P.mm(ps[pq][hs, cs], kT[:, c0:c0 + 64], qb[:, c0:c0 + 64], True, True, reads=kkeys + qkeys, writes=[("ps", pq)])
                    pdt = psE.next()
                    for t8, tt_, hs, par, c0, cs in blocks():
                        P.mm(ps[pdt][hs, cs], onesf[hs, 0:64], d_Y[hs, tt_, :], True, False, reads=[K("Y"), "smalls"], writes=[("ps", pdt)])
                        P.mm(ps[pdt][hs, cs], d_Y[hs, tt_, :], negones[hs, :], False, True, reads=[K("Y"), "smalls"], writes=[("ps", pdt)])
                    i = tmpring.next()
                    P.tt("dve", v3(tmp[i][:]), v3(ps[pdt][:]), neg_get[:, None, :].broadcast_to([128, 8, 64]), ALU.add,
                         reads=[("ps", pdt), "smalls"], writes=[("tmp", i)])
                    P.act(tmp[i][:], tmp[i][:], AF.Exp, reads=[("tmp", i)], writes=[("tmp", i)])
                    P.tt("dve", d_attn[:, gq * 8:(gq + 1) * 8, :], v3(tmp[i][:]), v3(ps[pq][:]), ALU.mult,
                         reads=[("tmp", i), ("ps", pq)], writes=[K("attn", gq)])
                for tb in range(NTB):
                    tsl = slice(tb * 512, (tb + 1) * 512)
                    pab = [psE.next(), psE.next()]
                    for par in range(2):
                        for t4 in range(4):
                            tt_ = tb * 4 + t4
                            P.mm(ps[pab[par]][:, t4 * 64:(t4 + 1) * 64], onesf[HS[par], :], d_Y[HS[par], tt_, :], True, True,
                                 reads=[K("Y"), "smalls"], writes=[("ps", pab[par])])
                    i = tmpring.next()
                    tv = tmp[i][:].rearrange("p (t q i) -> p t q i", t=4, q=2)
                    for par in range(2):
                        P.act(tv[:, :, par, :], ps[pab[par]][:, 0:256].rearrange("p (t i) -> p t i", t=4), AF.Exp,
                              reads=[("ps", pab[par])], writes=[("tmp", i)])
                    P.tt("dve", qb[:, tsl], qb[:, tsl], tmp[i][:], ALU.mult, reads=[("tmp", i), (qk, tb)], writes=[(qk, tb)])
                    pwb = [psP.next(), psP.next()]
                    for par in range(2):
                        for t4 in range(4):
                            tt_ = tb * 4 + t4
                            P.mm(ps[pwb[par]][:, t4 * 64:(t4 + 1) * 64], d_ktok[HS[par], tt_, :], d_TT[HS[par], tt_, :], True, True,
                                 reads=[K("ktok", tb), K("TT", tb // 2)], writes=[("ps", pwb[par])])
                    nv = d_nwt[:, tsl].rearrange("p (t q i) -> p t q i", t=4, q=2)
                    for par in range(2):
                        P.act(nv[:, :, par, :], ps[pwb[par]][:, 0:256].rearrange("p (t i) -> p t i", t=4), AF.Identity,
                              reads=[("ps", pwb[par])], writes=[K("nwt", tb)], scale=-1.0)

            def scan(h, gen):
                qb, qk = (qT, "qT") if h % 2 == 0 else (qTalt, "qTb")
                s, wz = hinfo[h]

                def pull(n):
                    if gen is None:
                        return
                    for _ in range(n):
                        try:
                            next(gen)
                        except StopIteration:
                            return

                P.op("dve", lambda e: e.memset(d_S, 0.0), writes=[K("S")])
                P.op("dve", lambda e: e.memset(d_Sb, 0.0), writes=[K("Sb")])
                for n in range(32):
                    tt_, par = n // 2, n % 2
                    hs = HS[par]
                    c0 = n * 64
                    tb = n // 8
                    oc = slice((n % 8) * 64, (n % 8) * 64 + 64)
                    pv = psZ.next()
                    P.mm(ps[pv][hs, 0:128], d_TT[hs, tt_, :], vS[hs, tt_, :], True, False,
                         reads=[K("TT", tt_ // 8), ("vS", tt_ // 4)], writes=[("ps", pv)])
                    P.mm(ps[pv][hs, 0:128], d_nwt[:, c0:c0 + 64], d_Sb, False, True, reads=[K("nwt", tb), K("Sb")], writes=[("ps", pv)])
                    P.mm(ps[PS_O][:, oc], d_Sb, qb[:, c0:c0 + 64], True, False, reads=[K("Sb"), (qk, tb)], writes=[("ps", PS_O)])
                    P.copy("act", d_vnb[hs, :], ps[pv][hs, 0:128], reads=[("ps", pv)], writes=[K("vnb")])
                    pull(1)
                    P.mm(ps[PS_O][:, oc], d_vnb[hs, :], d_attn[hs, tt_, :], False, True, reads=[K("vnb"), K("attn", tt_ // 8)], writes=[("ps", PS_O)])
                    pS = psE.next()
                    P.mm(ps[pS][:, 0:128], d_ktail[hs, tt_, :], d_vnb[hs, :], True, True, reads=[K("ktail"), K("vnb")], writes=[("ps", pS)])
                    P.stt("dve", d_S, d_S, d_egl[par][:, tt_, h:h + 1], ps[pS][:, 0:128], ALU.mult, ALU.add,
                          reads=[K("S"), K("egl"), ("ps", pS)], writes=[K("S")])
                    P.copy("dve", d_Sb, d_S, reads=[K("S")], writes=[K("Sb")])
                    pull(1)
                    if n % 8 == 7:
                        tsl = slice(tb * 512, (tb + 1) * 512)
                        j = sqring.next()
                        P.act(sq[j][:], ps[PS_O][:], AF.Square, reads=[("ps", PS_O)], writes=[("sq", j)])
                        P.mm(ps[PS_N][:], ones_bf[:], sq[j][:], True, True, reads=[("sq", j), "cst"], writes=[("ps", PS_N)])
                        r = rstdring.next()
                        P.act(rstd[r][:], ps[PS_N][:], AF.Ln, reads=[("ps", PS_N)], writes=[("rstd", r)], bias=EPS, scale=1.0 / 128)
                        P.act(rstd[r][:], rstd[r][:], AF.Exp, reads=[("rstd", r)], writes=[("rstd", r)], scale=-0.5)
                        p = psP.next()
                        for kc in range(KC):
                            P.mm(ps[p][:], wz[:, kc, :], hT[:, kc, tsl], kc == 0, kc == KC - 1,
                                 reads=[("w", s), ("h", kc, tb)], writes=[("ps", p)])
                        i = tmpring.next()
                        P.act(tmp[i][:], ps[p][:], AF.Silu, reads=[("ps", p)], writes=[("tmp", i)])
                        i2 = tmpring.next()
                        P.stt("dve", tmp[i2][:], ps[PS_O][:], smalls[:, o_onorm + jl:o_onorm + jl + 1], rstd[r][:], ALU.mult, ALU.mult,
                              reads=[("ps", PS_O), ("rstd", r), "smalls"], writes=[("tmp", i2)])
                        P.tt("dve", d_oh[:, tsl], tmp[i2][:], tmp[i][:], ALU.mult, reads=[("tmp", i), ("tmp", i2)], writes=[K("oh", tb)])

            def outp(h):
                so = wload([(0, 1, 1024, dn_w_out_d[jl][h * 128:(h + 1) * 128, :])])
                wo = wview(so, 0, 1, 1024)
                for m in range(KC):
                    for tb in range(NTB):
                        tsl = slice(tb * 512, (tb + 1) * 512)
                        p = psP.next()
                        P.mm(ps[p][:], wo[:, 0, m * 128:(m + 1) * 128], d_oh[:, tsl], True, True,
                             reads=[("w", so), K("oh", tb)], writes=[("ps", p)])
                        P.stt("dve", xT[:, m, tsl], ps[p][:], modT[:, l, 16 + m:17 + m], xT[:, m, tsl], ALU.mult, ALU.add,
                              reads=[("ps", p), ("modT", l), ("x", m, tb)], writes=[("x", m, tb)])

            for _ in prep1(0):
                pass
            for h in range(8):
                part2(h)
                gen = prep1(h + 1) if h < 7 else None
                scan(h, gen)
                if gen is not None:
                    for _ in gen:
                        pass
                outp(h)
                run_hook(h)
            P.fence(lambda e: e.memset(dummy[0:1, 1:2], 0.0), ARENA_FAMS)

        import os
        DBG = os.environ.get("KDBG", "")
        first_layer = plan[0][0]
        for nb in range(0 if "noada" in DBG else 12):
            ada_block(first_layer, nb)
        for pi, (l, do_mix, do_ffn) in enumerate(plan):
            nxt = plan[pi + 1][0] if pi + 1 < len(plan) else None
            if do_mix:
                ada_hook[0] = nxt
                norm_mod(l, 0)
                if l % 2 == 0:
                    dn_mixer(l)
                else:
                    sb_mixer(l)
                ada_hook[0] = None
            if do_ffn:
                norm_mod(l, 1)
                ffn(l, None if do_mix else nxt)
            elif nxt is not None and not do_mix:
                for nb in range(12):
                    ada_block(nxt, nb)

        go = P.dma_group("st")
        for kc in range(KC):
            P.dma("sp", go, yT_d[kc * 128:(kc + 1) * 128, :], xT[:, kc, :],
                  reads=[("x", kc, tb) for tb in range(NTB)], writes=["yT"])
        P.op("sp", lambda e: e.nop(), reads=["yT"])
        P.emit()
    return nc


def _layout(inputs):
    f = lambda a: np.ascontiguousarray(np.asarray(a, dtype=np.float32))
    x = f(inputs["x"])
    c = f(inputs["c"])
    B = x.shape[0]
    shared = {
        "ada_w": f(inputs["ada_w"]),
        "ada_b": f(inputs["ada_b"]),
        "n1g": f(np.asarray(inputs["norm1_g"]).reshape(NL, KC, 128).transpose(2, 0, 1).reshape(128, NL * KC)),
        "n2g": f(np.asarray(inputs["norm2_g"]).reshape(NL, KC, 128).transpose(2, 0, 1).reshape(128, NL * KC)),
        "dn_w_in": f(inputs["dn_w_in"]),
        "dn_conv": f(np.asarray(inputs["dn_conv_w"]).reshape(2, 4, 24, 128).transpose(3, 0, 2, 1).reshape(128, 192)),
        "dn_alog": f(np.broadcast_to(np.asarray(inputs["dn_a_log"]).reshape(1, 16), (128, 16))),
        "dn_dtb": f(np.broadcast_to(np.asarray(inputs["dn_dt_bias"]).reshape(1, 16), (128, 16))),
        "dn_onorm": f(np.asarray(inputs["dn_onorm_g"]).T),
        "dn_w_out": f(inputs["dn_w_out"]),
        "sb_w_qkv": f(inputs["sb_w_qkv"]),
        "sb_qg": f(np.tile(np.asarray(inputs["sb_q_norm_g"]).T, (2, 1))),
        "sb_kg": f(np.tile(np.asarray(inputs["sb_k_norm_g"]).T, (2, 1))),
        "sb_w_out": f(inputs["sb_w_out"]),
        "ffn_w_in": f(inputs["ffn_w_in"]),
        "ffn_w_out": f(inputs["ffn_w_out"]),
        "cst": _consts(),
    }
    maps = []
    for b in range(B):
        m = dict(shared)
        m["xT"] = f(x[b].T)
        m["cT"] = f(c[b].reshape(KC, 128).T)
        maps.append(m)
    return maps


def kernel(**inputs):
    maps = _layout(inputs)
    nc = build()
    res = run_bass_kernel_spmd(nc, maps, core_ids=list(range(len(maps))))
    out = np.stack([np.ascontiguousarray(r["yT"].T) for r in res.results], axis=0)
    return out.astype(np.float32)
```

```python
import contextlib
import numpy as np
import concourse.bass as bass
import concourse.mybir as mybir
from concourse.bass_utils import run_bass_kernel_spmd

F32 = mybir.dt.float32
F32R = mybir.dt.float32r
BF16 = mybir.dt.bfloat16
F16 = mybir.dt.float16
AF = mybir.ActivationFunctionType
ALU = mybir.AluOpType

D = 1024
T = 2048
DFF = 2816
NL = 4
KC = 8
NTB = 4
EPS = 1e-6
NMOD = 48


class _Op:
    __slots__ = ("eng", "fn", "deps", "sig", "count", "dma", "batch", "is_mm")

    def __init__(self, eng, fn, is_mm=False):
        self.eng = eng
        self.fn = fn
        self.deps = []
        self.sig = False
        self.count = 0
        self.dma = None
        self.batch = 0
        self.is_mm = is_mm


class DmaGroup:
    def __init__(self, name):
        self.name = name
        self.ops = []
        self.batch = -1
        self.sem = None
        self.cum = {}

    def new_batch(self):
        self.batch += 1


class Prog:
    ENGS = ("pe", "act", "dve", "pool", "sp")

    def __init__(self, nc):
        self.nc = nc
        self.ops = {e: [] for e in self.ENGS}
        self.last_w = {}
        self.readers = {}
        self.groups = []

    def dma_group(self, name):
        g = DmaGroup(name)
        self.groups.append(g)
        return g

    def _track(self, op, reads, writes):
        psr = [r for r in reads if isinstance(r, tuple) and r[0] == "ps"]
        if psr:
            reads = [r for r in reads if not (isinstance(r, tuple) and r[0] == "ps")]
            writes = list(writes) + psr
        deps = {}
        for r in reads:
            w = self.last_w.get(r)
            if w is not None:
                deps[id(w)] = w
            self.readers.setdefault(r, []).append(op)
        for wkey in writes:
            w = self.last_w.get(wkey)
            if w is not None:
                deps[id(w)] = w
            for rd in self.readers.get(wkey, ()):
                if rd is not op:
                    deps[id(rd)] = rd
            self.readers[wkey] = []
            self.last_w[wkey] = op
        for d in deps.values():
            if d is op:
                continue
            if d.is_mm and op.is_mm:
                continue
            if d.dma is not None and op.dma is d.dma and d.batch == op.batch:
                continue
            d.sig = True
            op.deps.append(d)

    def op(self, eng, fn, reads=(), writes=(), is_mm=False):
        o = _Op(eng, fn, is_mm=is_mm)
        self._track(o, reads, writes)
        self.ops[eng].append(o)
        return o

    def dma(self, eng, grp, out, in_, reads=(), writes=()):
        o = _Op(eng, lambda e: e.dma_start(out=out, in_=in_))
        o.dma = grp
        if grp.batch < 0:
            grp.new_batch()
        o.batch = grp.batch
        grp.ops.append(o)
        self._track(o, reads, writes)
        self.ops[eng].append(o)
        return o

    def mm(self, out, lhsT, rhs, start, stop, reads, writes):
        return self.op("pe", lambda e: e.matmul(out, lhsT, rhs, start=start, stop=stop), reads, writes, is_mm=True)

    def tr(self, out, in_, ident, reads, writes):
        return self.op("pe", lambda e: e.transpose(out, in_, ident), reads, writes, is_mm=True)

    def act(self, out, in_, func, reads, writes, bias=0.0, scale=1.0):
        return self.op("act", lambda e: e.activation(out, in_, func, bias=bias, scale=scale), reads, writes)

    def tt(self, eng, out, in0, in1, op, reads, writes):
        return self.op(eng, lambda e: e.tensor_tensor(out, in0, in1, op), reads, writes)

    def stt(self, eng, out, in0, scalar, in1, op0, op1, reads, writes):
        return self.op(eng, lambda e: e.scalar_tensor_tensor(out, in0, scalar, in1, op0, op1), reads, writes)

    def ts(self, eng, out, in0, s1, s2, op0, op1, reads, writes):
        if s2 is None:
            return self.op(eng, lambda e: e.tensor_scalar(out, in0, s1, None, op0), reads, writes)
        return self.op(eng, lambda e: e.tensor_scalar(out, in0, s1, s2, op0, op1), reads, writes)

    def copy(self, eng, out, in_, reads, writes):
        if eng == "act":
            return self.op("act", lambda e: e.activation(out, in_, AF.Copy), reads, writes)
        return self.op(eng, lambda e: e.tensor_copy(out, in_), reads, writes)

    def fence(self, fn, fams):
        fam = lambda k: k if isinstance(k, str) else k[0]
        keys = [k for k in (set(self.last_w) | set(self.readers)) if fam(k) in fams]
        return self.op("dve", fn, reads=(), writes=keys)

    def emit(self):
        nc = self.nc
        for e in self.ENGS:
            c = 0
            for o in self.ops[e]:
                if o.dma is None and o.sig:
                    c += 1
                    o.count = c
        for g in self.groups:
            n = 0
            for o in g.ops:
                n += 1
                g.cum[o.batch] = n
        with contextlib.ExitStack() as st:
            esem = {e: st.enter_context(nc.semaphore("s_" + e)) for e in self.ENGS}
            for g in self.groups:
                g.sem = st.enter_context(nc.semaphore("d_" + g.name))
            block = st.enter_context(nc.Block())

            def run(ename, eng):
                waited = {}
                for o in self.ops[ename]:
                    for d in o.deps:
                        if d.dma is not None:
                            sem, val = d.dma.sem, 16 * d.dma.cum[d.batch]
                        else:
                            sem, val = esem[d.eng], d.count
                        k = id(sem)
                        if waited.get(k, 0) < val:
                            eng.wait_ge(sem, val)
                            waited[k] = val
                    ins = o.fn(eng)
                    if o.dma is not None:
                        ins.then_inc(o.dma.sem, 16)
                    elif o.sig:
                        ins.then_inc(esem[ename], 1)

            @block.tensor
            def _(eng):
                run("pe", eng)

            @block.scalar
            def _(eng):
                run("act", eng)

            @block.vector
            def _(eng):
                run("dve", eng)

            @block.gpsimd
            def _(eng):
                run("pool", eng)

            @block.sync
            def _(eng):
                run("sp", eng)


class Ring:
    def __init__(self, items):
        self.items = items
        self.i = -1

    def next(self):
        self.i = (self.i + 1) % len(self.items)
        return self.items[self.i]


C_ONES = 0
C_BLK = 128
C_IDENT = 256
C_NEGTRI = 384
C_NEGONES = 512
C_MASK = 640
C_DN = C_MASK + 4 * 512
DN_TRI_LE = 0
DN_TRI_GT = 64
DN_NEG_STRICT = 128
DN_NEG_GET = 192
DN_IDENT64 = 256
DN_NEGONES = 320
DN_NCOL = 384
C_TOTAL = C_DN + DN_NCOL


def _consts():
    c = np.zeros((128, C_TOTAL), np.float32)
    c[:, C_ONES:C_ONES + 128] = 1.0
    c[:64, C_BLK:C_BLK + 64] = 1.0
    c[64:, C_BLK + 64:C_BLK + 128] = 1.0
    c[:, C_IDENT:C_IDENT + 128] = np.eye(128, dtype=np.float32)
    j = np.arange(128)[:, None]
    s = np.arange(128)[None, :]
    c[:, C_NEGTRI:C_NEGTRI + 128] = -(j >= s).astype(np.float32)
    c[:, C_NEGONES:C_NEGONES + 128] = -1.0
    t = np.arange(512)[None, :]
    for jj in range(4):
        c[:, C_MASK + jj * 512:C_MASK + (jj + 1) * 512] = ((j + 128 * jj) < t).astype(np.float32)
    a = np.arange(64)[:, None]
    b = np.arange(64)[None, :]
    for half in range(2):
        r = slice(half * 64, half * 64 + 64)
        o = C_DN
        c[r, o + DN_TRI_LE:o + DN_TRI_LE + 64] = (a <= b)
        c[r, o + DN_TRI_GT:o + DN_TRI_GT + 64] = (a > b)
        c[r, o + DN_NEG_STRICT:o + DN_NEG_STRICT + 64] = ((a > b) - 1.0) * 3e4
        c[r, o + DN_NEG_GET:o + DN_NEG_GET + 64] = ((b >= a) - 1.0) * 3e4
        c[r, o + DN_IDENT64:o + DN_IDENT64 + 64] = (a == b)
        c[r, o + DN_NEGONES:o + DN_NEGONES + 64] = -1.0
    return c


def build(plan=None):
    if plan is None:
        plan = [(l, True, True) for l in range(NL)]
    nc = bass.Bass("TRN2", target_bir_lowering=False)
    dt_in = lambda n, s: nc.dram_tensor(n, s, F32, kind="ExternalInput").ap()
    xT_d = dt_in("xT", [D, T])
    cT_d = dt_in("cT", [128, KC])
    ada_w_d = dt_in("ada_w", [NL, D, 6 * D])
    ada_b_d = dt_in("ada_b", [NL, 6 * D])
    n1g_d = dt_in("n1g", [128, NL * KC])
    n2g_d = dt_in("n2g", [128, NL * KC])
    dn_w_in_d = dt_in("dn_w_in", [2, D, 4112])
    dn_conv_d = dt_in("dn_conv", [128, 2 * 24 * 4])
    dn_alog_d = dt_in("dn_alog", [128, 16])
    dn_dtb_d = dt_in("dn_dtb", [128, 16])
    dn_onorm_d = dt_in("dn_onorm", [128, 2])
    dn_w_out_d = dt_in("dn_w_out", [2, D, D])
    sb_w_qkv_d = dt_in("sb_w_qkv", [2, D, 3 * D])
    sb_qg_d = dt_in("sb_qg", [128, 2])
    sb_kg_d = dt_in("sb_kg", [128, 2])
    sb_w_out_d = dt_in("sb_w_out", [2, D, D])
    ffn_w_in_d = dt_in("ffn_w_in", [NL, D, 2 * DFF])
    ffn_w_out_d = dt_in("ffn_w_out", [NL, DFF, D])
    cst_d = dt_in("cst", [128, C_TOTAL])
    yT_d = nc.dram_tensor("yT", [D, T], F32, kind="ExternalOutput").ap()

    P = Prog(nc)
    with contextlib.ExitStack() as st:
        sb = lambda n, s, d: st.enter_context(nc.sbuf_tensor("sb_" + n, s, d))
        xT = sb("xT", [128, KC, T], F32)
        hT = sb("hT", [128, KC, T], BF16)
        S = sb("S", [128, KC, T], BF16)
        wslots = [sb(f"wslot{i}", [128, 4096], BF16) for i in range(3)]
        wgrp = [P.dma_group(f"w{i}") for i in range(3)]
        wring = Ring(list(range(3)))
        adast = [sb(f"adast{i}", [128, 512], F32R) for i in range(3)]
        adagrp = [P.dma_group(f"a{i}") for i in range(3)]
        adaring = Ring(list(range(3)))
        adab = sb("adab", [1, 512], F32)
        adabgrp = P.dma_group("adab")
        modrow = adab
        modT = sb("modT", [128, NL, NMOD], F32)
        AB = sb("AB", [128, NL, 2, KC], F32)
        smalls = sb("smalls", [128, 2 * NL * KC + 16 + 16 + 2 + 2 + 2 + KC], F32)
        o_n1 = 0
        o_n2 = NL * KC
        o_alog = 2 * NL * KC
        o_dtb = o_alog + 16
        o_onorm = o_dtb + 16
        o_qg = o_onorm + 2
        o_kg = o_qg + 2
        o_c = o_kg + 2
        smalls_conv = sb("convw", [128, 192], F32)
        condr = sb("condr", [128, KC], F32R)
        one11 = sb("one11", [1, 1], F32)
        qgs = sb("qgs", [128, 2], F32)
        ones_bf = sb("ones_bf", [128, 128], BF16)
        blk_bf = sb("blk_bf", [128, 128], BF16)
        ident_bf = sb("ident_bf", [128, 128], BF16)
        negtri_r = sb("negtri_r", [128, 128], F16)
        negones_r = sb("negones_r", [128, 128], F16)
        masks = sb("masks", [128, 4, 512], BF16)
        sq = [sb(f"sq{i}", [128, 512], BF16) for i in range(2)]
        sqring = Ring([0, 1])
        rstd = [sb(f"rstd{i}", [128, 512], F32) for i in range(2)]
        rstdring = Ring([0, 1])
        tmp = [sb(f"tmp{i}", [128, 512], F32) for i in range(2)]
        tmpring = Ring([0, 1])
        qT = sb("qT", [128, T], BF16)
        kT = sb("kT", [128, T], BF16)
        vS = sb("vS", [128, 16, 128], BF16)
        sp4 = sb("sp4", [128, 2048], F16)
        spt = [sp4[:, i * 512:(i + 1) * 512] for i in range(3)]
        spring = Ring([0, 1, 2])
        spsum = sp4[:, 1536:2048]
        aT = [sb(f"aT{i}", [128, 512], BF16) for i in range(2)]
        aring = Ring([0, 1])

        wab = sb("wab", [128, KC, 16], BF16)
        wabgrp = P.dma_group("wab")
        dncst = sb("dncst", [128, DN_NCOL], F32)
        onesf = sb("onesf", [128, 128], F32)
        dummy = sb("dummy", [1, 8], F32)
        S2 = S[:].rearrange("p k t -> p (k t)")
        d_oh = S2[:, 0:2048]
        d_ktok = S2[:, 2176:4224].rearrange("p (a b) -> p a b", a=16)
        d_ktail = S2[:, 4224:6272].rearrange("p (a b) -> p a b", a=16)
        d_nwt = S2[:, 6272:8320]
        d_Y = S2[:, 8320:10368].bitcast(F32).rearrange("p (a b) -> p a b", a=16)
        d_attn = S2[:, 10368:11392].rearrange("p (a b) -> p a b", a=16)
        d_TT = S2[:, 11392:12416].rearrange("p (a b) -> p a b", a=16)
        d_diag = S2[:, 12416:13952].rearrange("p (a b) -> p a b", a=12)
        d_sc = S2[:, 13952:16384].bitcast(F32)
        d_ab = d_sc[:, 0:256].rearrange("p (a b) -> p a b", a=16)
        sc3 = lambda i: d_sc[:, 256 + i * 128:256 + (i + 1) * 128].rearrange("p (a b) -> p a b", a=16)
        d_g, d_beta, d_nbeta, d_beg, d_et, d_egl0, d_egl1 = [sc3(i) for i in range(7)]
        d_nea = d_sc[:, 1152:1160]
        dxbuf = sb("dxbuf", [128, 4, 512], BF16)
        d_pre = sb("pre2", [128, 2176], BF16)[:]
        d_vf = sp4[:].bitcast(BF16)
        qTalt = masks[:].rearrange("p j t -> p (j t)")
        d_X = [dxbuf[:, 0, :].rearrange("p (a b) -> p a b", a=8), dxbuf[:, 1, :].rearrange("p (a b) -> p a b", a=8)]
        d_XT = [dxbuf[:, 2, :].rearrange("p (a b) -> p a b", a=8), dxbuf[:, 3, :].rearrange("p (a b) -> p a b", a=8)]
        d_R = rstd[1][:]
        d_Rb = aT[0][:].rearrange("p (a b) -> p a b", a=8)
        d_S = aT[1][:, 0:256].bitcast(F32)
        d_Sb = aT[1][:, 256:384]
        d_vnb = aT[1][:, 384:512]
        ARENA_FAMS = ("S", "sp", "spsum", "aT", "dn", "masks", "qTb")

        ps = [st.enter_context(nc.psum_tensor(f"ps{i}", [128, 512], F32)) for i in range(8)]
        psP = Ring([0, 1])
        PS_N = 2
        psZ = Ring([3, 4])
        psE = Ring([5, 6])
        psZE = Ring([3, 4, 5, 6])
        PS_O = 7

        g0 = P.dma_group("ld0")
        for kc in range(KC):
            P.dma("sp", g0, xT[:, kc, :], xT_d[kc * 128:(kc + 1) * 128, :],
                  writes=[("x", kc, tb) for tb in range(NTB)])
        gs = P.dma_group("lds")
        P.dma("sp", gs, smalls[:, o_n1:o_n1 + NL * KC], n1g_d, writes=["smalls"])
        P.dma("sp", gs, smalls[:, o_n2:o_n2 + NL * KC], n2g_d, writes=["smalls"])
        P.dma("sp", gs, smalls[:, o_alog:o_alog + 16], dn_alog_d, writes=["smalls"])
        P.dma("sp", gs, smalls[:, o_dtb:o_dtb + 16], dn_dtb_d, writes=["smalls"])
        P.dma("sp", gs, smalls[:, o_onorm:o_onorm + 2], dn_onorm_d, writes=["smalls"])
        P.dma("sp", gs, smalls[:, o_qg:o_qg + 2], sb_qg_d, writes=["smalls"])
        P.dma("sp", gs, smalls[:, o_kg:o_kg + 2], sb_kg_d, writes=["smalls"])
        P.dma("sp", gs, smalls[:, o_c:o_c + KC], cT_d, writes=["smalls"])
        P.dma("sp", gs, smalls_conv[:], dn_conv_d, writes=["convw"])
        gc = P.dma_group("ldc")
        P.dma("pool", gc, ones_bf[:], cst_d[:, C_ONES:C_ONES + 128], writes=["cst"])
        P.dma("pool", gc, blk_bf[:], cst_d[:, C_BLK:C_BLK + 128], writes=["cst"])
        P.dma("pool", gc, ident_bf[:], cst_d[:, C_IDENT:C_IDENT + 128], writes=["cst"])
        P.dma("pool", gc, negtri_r[:], cst_d[:, C_NEGTRI:C_NEGTRI + 128], writes=["cst"])
        P.dma("pool", gc, negones_r[:], cst_d[:, C_NEGONES:C_NEGONES + 128], writes=["cst"])
        mgrp = P.dma_group("masks")
        P.op("dve", lambda e: e.memset(one11[:], 1.0), writes=["one11"])
        P.dma("sp", gs, dncst[:], cst_d[:, C_DN:C_DN + DN_NCOL], writes=["smalls"])
        P.dma("sp", gs, onesf[:], cst_d[:, C_ONES:C_ONES + 128], writes=["smalls"])
        P.act(condr[:], smalls[:, o_c:o_c + KC], AF.Silu, reads=["smalls"], writes=["condr"])
        P.ts("dve", qgs[:], smalls[:, o_qg:o_qg + 2], 0.125, None, ALU.mult, None, reads=["smalls"], writes=["qgs"])

        def wload(pieces):
            s = wring.next()
            wgrp[s].new_batch()
            for (off, kcs, ncols, src) in pieces:
                dst = wslots[s][:, off:off + kcs * ncols].rearrange("p (k n) -> p k n", k=kcs)
                P.dma("pool", wgrp[s], dst, src.rearrange("(k p) n -> p k n", p=128), writes=[("w", s)])
            return s

        def wview(s, off, kcs, ncols):
            return wslots[s][:, off:off + kcs * ncols].rearrange("p (k n) -> p k n", k=kcs)

        def ada_block(l, nb):
            pso = ps[PS_O]
            for kc in range(KC):
                a = adaring.next()
                adagrp[a].new_batch()
                P.dma("pool", adagrp[a], adast[a][:], ada_w_d[l, kc * 128:(kc + 1) * 128, nb * 512:(nb + 1) * 512],
                      writes=[("adast", a)])
                P.mm(pso[0:1, :], condr[:, kc:kc + 1], adast[a][:], kc == 0, kc == KC - 1,
                     reads=[("adast", a), "condr"], writes=[("ps", PS_O)])
            adabgrp.new_batch()
            P.dma("sp", adabgrp, adab[:], ada_b_d[l:l + 1, nb * 512:(nb + 1) * 512], writes=["adab"])
            P.tt("dve", modrow[:], pso[0:1, :], adab[:], ALU.add, reads=[("ps", PS_O), "adab"], writes=["adab"])
            for j in range(4):
                col = nb * 4 + j
                P.mm(ps[PS_N][:, col:col + 1], modrow[0:1, j * 128:(j + 1) * 128], one11[0:1, 0:1], True, True,
                     reads=["adab", "one11"], writes=[("ps", PS_N)])
            P.copy("dve", modT[:, l, nb * 4:(nb + 1) * 4], ps[PS_N][:, nb * 4:(nb + 1) * 4], reads=[("ps", PS_N)], writes=[("modT", l)])
            if nb == 11:
                P.stt("dve", AB[:, l, 0, :], modT[:, l, 8:16], 1.0, smalls[:, o_n1 + l * KC:o_n1 + (l + 1) * KC],
                      ALU.add, ALU.mult, reads=[("modT", l), "smalls"], writes=[("AB", l)])
                P.stt("dve", AB[:, l, 1, :], modT[:, l, 32:40], 1.0, smalls[:, o_n2 + l * KC:o_n2 + (l + 1) * KC],
                      ALU.add, ALU.mult, reads=[("modT", l), "smalls"], writes=[("AB", l)])

        def norm_mod(l, which):
            sh = 0 if which == 0 else 24
            for tb in range(NTB):
                tsl = slice(tb * 512, (tb + 1) * 512)
                for kc in range(KC):
                    i = sqring.next()
                    P.act(sq[i][:], xT[:, kc, tsl], AF.Square, reads=[("x", kc, tb)], writes=[("sq", i)])
                    P.mm(ps[PS_N][:], ones_bf[:], sq[i][:], kc == 0, kc == KC - 1,
                         reads=[("sq", i), "cst"], writes=[("ps", PS_N)])
                r = rstdring.next()
                P.act(rstd[r][:], ps[PS_N][:], AF.Ln, reads=[("ps", PS_N)], writes=[("rstd", r)], bias=EPS, scale=1.0 / D)
                P.act(rstd[r][:], rstd[r][:], AF.Exp, reads=[("rstd", r)], writes=[("rstd", r)], scale=-0.5)
                for kc in range(KC):
                    i = tmpring.next()
                    P.tt("dve", tmp[i][:], xT[:, kc, tsl], rstd[r][:], ALU.mult,
                         reads=[("x", kc, tb), ("rstd", r)], writes=[("tmp", i)])
                    P.act(hT[:, kc, tsl], tmp[i][:], AF.Identity, reads=[("tmp", i), ("AB", l), ("modT", l)],
                          writes=[("h", kc, tb)], bias=modT[:, l, sh + kc:sh + kc + 1], scale=AB[:, l, which, kc:kc + 1])

        def out_proj(l, w_d, nkc, gate_off, skeys_fn, s_view_fn):
            for mb in range(2 if nkc <= 8 else 4):
                ncols = 512 if nkc <= 8 else 256
                s = wload([(0, nkc, ncols, w_d[:, mb * ncols:(mb + 1) * ncols])])
                wv = wview(s, 0, nkc, ncols)
                for mc in range(ncols // 128):
                    m = mb * (ncols // 128) + mc
                    for tb in range(NTB):
                        tsl = slice(tb * 512, (tb + 1) * 512)
                        p = psP.next()
                        for kc in range(nkc):
                            P.mm(ps[p][:], wv[:, kc, mc * 128:(mc + 1) * 128], s_view_fn(kc, tsl), kc == 0, kc == nkc - 1,
                                 reads=[("w", s), skeys_fn(kc, tb)], writes=[("ps", p)])
                        P.stt("dve", xT[:, m, tsl], ps[p][:], modT[:, l, gate_off + m:gate_off + m + 1], xT[:, m, tsl],
                              ALU.mult, ALU.add, reads=[("ps", p), ("modT", l), ("x", m, tb)], writes=[("x", m, tb)])

        def ffn(l, ada_next):
            groups = [(0, 8), (8, 7), (15, 7)]
            for gi, (c0, ncg) in enumerate(groups):
                j = 0
                while j < ncg:
                    nsub = min(4, ncg - j)
                    cc = c0 + j
                    sg = wload([(0, KC, nsub * 128, ffn_w_in_d[l, :, cc * 128:(cc + nsub) * 128])])
                    su = wload([(0, KC, nsub * 128, ffn_w_in_d[l, :, DFF + cc * 128:DFF + (cc + nsub) * 128])])
                    wg = wview(sg, 0, KC, nsub * 128)
                    wu = wview(su, 0, KC, nsub * 128)
                    for q in range(nsub):
                        for tb in range(NTB):
                            tsl = slice(tb * 512, (tb + 1) * 512)
                            pg = psZ.next()
                            pu = psE.next()
                            for kc in range(KC):
                                P.mm(ps[pg][:], wg[:, kc, q * 128:(q + 1) * 128], hT[:, kc, tsl], kc == 0, kc == KC - 1,
                                     reads=[("w", sg), ("h", kc, tb)], writes=[("ps", pg)])
                            for kc in range(KC):
                                P.mm(ps[pu][:], wu[:, kc, q * 128:(q + 1) * 128], hT[:, kc, tsl], kc == 0, kc == KC - 1,
                                     reads=[("w", su), ("h", kc, tb)], writes=[("ps", pu)])
                            i = tmpring.next()
                            P.act(tmp[i][:], ps[pg][:], AF.Silu, reads=[("ps", pg)], writes=[("tmp", i)])
                            P.tt("dve", S[:, j + q, tsl], tmp[i][:], ps[pu][:], ALU.mult,
                                 reads=[("tmp", i), ("ps", pu)], writes=[("S", j + q, tb)])
                    j += nsub
                out_proj(l, ffn_w_out_d[l, c0 * 128:(c0 + ncg) * 128, :], ncg, 40,
                         lambda kc, tb: ("S", kc, tb), lambda kc, tsl: S[:, kc, tsl])
                if ada_next is not None:
                    for nb in range(gi * 4, gi * 4 + 4):
                        ada_block(ada_next, nb)

        ada_hook = [None]

        def run_hook(i):
            if ada_hook[0] is not None and 1 <= i <= 4:
                for nb in range((i - 1) * 3, (i - 1) * 3 + 3):
                    ada_block(ada_hook[0], nb)

        def sb_mixer(l):
            jl = l // 2
            w_d = sb_w_qkv_d[jl]
            mgrp.new_batch()
            P.dma("pool", mgrp, masks[:], cst_d[:, C_MASK:C_MASK + 2048].rearrange("p (j t) -> p j t", j=4), writes=["masks"])
            for hp in range(8):
                s = wload([(0, KC, 128, w_d[:, hp * 128:(hp + 1) * 128]),
                           (1024, KC, 128, w_d[:, D + hp * 128:D + (hp + 1) * 128]),
                           (2048, KC, 128, w_d[:, 2 * D + hp * 128:2 * D + (hp + 1) * 128])])
                wq = wview(s, 0, KC, 128)
                wk = wview(s, 1024, KC, 128)
                wv = wview(s, 2048, KC, 128)
                for (wmat, dst, dkey, gcol) in ((wq, qT, "qT", qgs[:, jl:jl + 1]),
                                                (wk, kT, "kT", smalls[:, o_kg + jl:o_kg + jl + 1])):
                    for tb in range(NTB):
                        tsl = slice(tb * 512, (tb + 1) * 512)
                        p = psP.next()
                        for kc in range(KC):
                            P.mm(ps[p][:], wmat[:, kc, :], hT[:, kc, tsl], kc == 0, kc == KC - 1,
                                 reads=[("w", s), ("h", kc, tb)], writes=[("ps", p)])
                        i = sqring.next()
                        P.act(sq[i][:], ps[p][:], AF.Square, reads=[("ps", p)], writes=[("sq", i)])
                        P.mm(ps[PS_N][:], blk_bf[:], sq[i][:], True, True, reads=[("sq", i), "cst"], writes=[("ps", PS_N)])
                        r = rstdring.next()
                        P.act(rstd[r][:], ps[PS_N][:], AF.Ln, reads=[("ps", PS_N)], writes=[("rstd", r)], bias=EPS, scale=1.0 / 64)
                        P.act(rstd[r][:], rstd[r][:], AF.Exp, reads=[("rstd", r)], writes=[("rstd", r)], scale=-0.5)
                        P.stt("dve", dst[:, tsl], ps[p][:], gcol, rstd[r][:], ALU.mult, ALU.mult,
                              reads=[("ps", p), ("rstd", r), "qgs", "smalls"], writes=[(dkey, tb)])
                for t4 in range(4):
                    p = psP.next()
                    for jj in range(4):
                        tt_ = t4 * 4 + jj
                        for kc in range(KC):
                            P.mm(ps[p][:, jj * 128:(jj + 1) * 128], hT[:, kc, tt_ * 128:(tt_ + 1) * 128], wv[:, kc, :],
                                 kc == 0, kc == KC - 1, reads=[("w", s), ("h", kc, t4)], writes=[("ps", p)])
                    P.copy("act", vS[:, t4 * 4:(t4 + 1) * 4, :], ps[p][:].rearrange("p (j n) -> p j n", j=4),
                           reads=[("ps", p)], writes=[("vS", t4)])
                tiles = []
                for hd in range(2):
                    for qb in range(NTB):
                        nkb = 4 * (qb + 1)
                        for kb in range(nkb - 1, -1, -1):
                            tiles.append(dict(hd=hd, qb=qb, kb=kb, first=(kb == nkb - 1), last=(kb == 0),
                                              diag=(kb >= 4 * qb), jm=kb - 4 * qb))

                def stA(t):
                    r0 = t["hd"] * 64
                    ksl = slice(t["kb"] * 128, (t["kb"] + 1) * 128)
                    c0 = 128 * t["jm"] if t["diag"] else 0
                    t["c0"] = c0
                    qsl = slice(t["qb"] * 512 + c0, (t["qb"] + 1) * 512)
                    cs = slice(c0, 512)
                    pz = psZ.next()
                    P.mm(ps[pz][:, cs], kT[r0:r0 + 64, ksl], qT[r0:r0 + 64, qsl], True, True,
                         reads=[("kT", t["kb"] // 4), ("qT", t["qb"])], writes=[("ps", pz)])
                    si = spring.next()
                    t["si"] = si
                    P.act(spt[si][:, cs], ps[pz][:, cs], AF.Exp, reads=[("ps", pz)], writes=[("sp", si)])
                    P.act(spt[si][:, cs], spt[si][:, cs], AF.Ln, reads=[("sp", si)], writes=[("sp", si)], bias=1.0)
                    if t["diag"]:
                        P.tt("dve", spt[si][:, cs], spt[si][:, cs], masks[:, t["jm"], cs], ALU.mult,
                             reads=[("sp", si), "masks"], writes=[("sp", si)])

                def stB(t):
                    r0 = t["hd"] * 64
                    ksl = slice(t["kb"] * 128, (t["kb"] + 1) * 128)
                    c0 = t["c0"]
                    qsl = slice(t["qb"] * 512 + c0, (t["qb"] + 1) * 512)
                    cs = slice(c0, 512)
                    si = t["si"]
                    pe_ = psE.next()
                    P.mm(ps[pe_][:, cs], kT[r0:r0 + 64, ksl], qT[r0:r0 + 64, qsl], True, False,
                         reads=[("kT", t["kb"] // 4), ("qT", t["qb"])], writes=[("ps", pe_)])
                    P.mm(ps[pe_][:, cs], negtri_r[:], spt[si][:, cs], False, t["first"],
                         reads=[("sp", si), "cst"], writes=[("ps", pe_)])
                    if not t["first"]:
                        P.mm(ps[pe_][:, cs], negones_r[:], spsum[:, cs], False, True,
                             reads=["spsum", "cst"], writes=[("ps", pe_)])
                    if not t["last"]:
                        if t["first"]:
                            if c0 > 0:
                                P.ts("dve", spsum[:, 0:c0], masks[:, 0, 0:c0], 0.0, None, ALU.mult, None,
                                     reads=["masks"], writes=["spsum"])
                            P.copy("dve", spsum[:, cs], spt[si][:, cs], reads=[("sp", si)], writes=["spsum"])
                        else:
                            P.tt("dve", spsum[:, cs], spsum[:, cs], spt[si][:, cs], ALU.add,
                                 reads=[("sp", si), "spsum"], writes=["spsum"])
                    ai = aring.next()
                    t["ai"] = ai
                    if t["first"] and c0 > 0:
                        P.op("dve", lambda e, ai=ai, c0=c0: e.memset(aT[ai][:, 0:c0], 0.0), writes=[("aT", ai)])
                    P.act(aT[ai][:, cs], ps[pe_][:, cs], AF.Exp, reads=[("ps", pe_)], writes=[("aT", ai)])
                    if t["diag"]:
                        P.tt("dve", aT[ai][:, cs], aT[ai][:, cs], masks[:, t["jm"], cs], ALU.mult,
                             reads=[("aT", ai), "masks"], writes=[("aT", ai)])

                def stC(t):
                    r0 = t["hd"] * 64
                    qsl = slice(t["qb"] * 512, (t["qb"] + 1) * 512)
                    ai = t["ai"]
                    cs = slice(0, 512) if t["first"] else slice(t["c0"], 512)
                    P.mm(ps[PS_O][:, cs], vS[:, t["kb"], :], aT[ai][:, cs], t["first"], t["last"],
                         reads=[("vS", t["kb"] // 4), ("aT", ai)], writes=[("ps", PS_O)])
                    if t["last"]:
                        P.copy("act", S[r0:r0 + 64, hp, qsl], ps[PS_O][r0:r0 + 64, :],
                               reads=[("ps", PS_O)], writes=[("S", hp, t["qb"])])

                nt = len(tiles)
                for idx in range(nt + 2):
                    if idx < nt:
                        stA(tiles[idx])
                    if 0 <= idx - 1 < nt:
                        stB(tiles[idx - 1])
                    if 0 <= idx - 2 < nt:
                        stC(tiles[idx - 2])
                run_hook(hp)
            out_proj(l, sb_w_out_d[jl], KC, 16, lambda kc, tb: ("S", kc, tb), lambda kc, tsl: S[:, kc, tsl])

        class _Stop(Exception):
            pass

        def stage(n):
            import os
            if float(os.environ.get("KDN", "99")) < n:
                raise _Stop()

        def dn_mixer(l):
            try:
                dn_mixer_(l)
            except _Stop:
                P.fence(lambda e: e.memset(dummy[0:1, 1:2], 0.0), ARENA_FAMS)

        def dn_mixer_(l):
            jl = l // 2
            w_d = dn_w_in_d[jl]
            K = lambda *a: ("dn",) + a
            tri_le = dncst[:, DN_TRI_LE:DN_TRI_LE + 64]
            tri_gt = dncst[:, DN_TRI_GT:DN_TRI_GT + 64]
            neg_strict = dncst[:, DN_NEG_STRICT:DN_NEG_STRICT + 64]
            neg_get = dncst[:, DN_NEG_GET:DN_NEG_GET + 64]
            ident64 = dncst[:, DN_IDENT64:DN_IDENT64 + 64]
            negones = dncst[:, DN_NEGONES:DN_NEGONES + 64]
            HS = [slice(0, 64), slice(64, 128)]
            P.fence(lambda e: e.memset(dummy[0:1, 0:1], 0.0), ARENA_FAMS)
            wabgrp.new_batch()
            P.dma("pool", wabgrp, wab[:], w_d[:, 4096:4112].rearrange("(k p) n -> p k n", p=128), writes=["wab"])
            for tt_ in range(16):
                for kc in range(KC):
                    P.mm(ps[PS_O][:, tt_ * 16:(tt_ + 1) * 16], hT[:, kc, tt_ * 128:(tt_ + 1) * 128], wab[:, kc, :],
                         kc == 0, kc == KC - 1, reads=["wab", ("h", kc, tt_ // 4)], writes=[("ps", PS_O)])
            P.copy("dve", d_ab, ps[PS_O][:, 0:256].rearrange("p (a b) -> p a b", a=16), reads=[("ps", PS_O)], writes=[K("ab")])
            P.act(d_nea, smalls[:, o_alog + jl * 8:o_alog + jl * 8 + 8], AF.Exp, reads=["smalls"], writes=[K("nea")])
            P.ts("dve", d_nea, d_nea, -1.0, None, ALU.mult, None, reads=[K("nea")], writes=[K("nea")])
            dtb_bc = smalls[:, o_dtb + jl * 8:o_dtb + jl * 8 + 8][:, None, :].broadcast_to([128, 16, 8])
            P.tt("dve", d_g, d_ab[:, :, 0:8], dtb_bc, ALU.add, reads=[K("ab"), "smalls"], writes=[K("g")])
            P.act(d_g, d_g, AF.Exp, reads=[K("g")], writes=[K("g")])
            P.act(d_g, d_g, AF.Ln, reads=[K("g")], writes=[K("g")], bias=1.0)
            P.tt("dve", d_g, d_g, d_nea[:, None, :].broadcast_to([128, 16, 8]), ALU.mult, reads=[K("g"), K("nea")], writes=[K("g")])
            P.act(d_beta, d_ab[:, :, 8:16], AF.Exp, reads=[K("ab")], writes=[K("beta")], scale=-1.0)
            P.ts("dve", d_beta, d_beta, 1.0, None, ALU.add, None, reads=[K("beta")], writes=[K("beta")])
            P.op("dve", lambda e: e.reciprocal(d_beta, d_beta), reads=[K("beta")], writes=[K("beta")])
            P.ts("dve", d_nbeta, d_beta, -1.0, None, ALU.mult, None, reads=[K("beta")], writes=[K("nbeta")])
            g2d = d_g.rearrange("p a b -> p (a b)")
            for par in range(2):
                hs = HS[par]
                P.mm(ps[PS_N][hs, 0:128], tri_le[hs, :], g2d[hs, :], True, True, reads=[K("g"), "smalls"], writes=[("ps", PS_N)])
                P.mm(ps[PS_N][hs, 128:256], tri_gt[hs, :], g2d[hs, :], True, True, reads=[K("g"), "smalls"], writes=[("ps", PS_N)])
            pgl = [PS_O, psP.next()]
            for par in range(2):
                P.mm(ps[pgl[par]][:, 0:128], onesf[HS[par], :], g2d[HS[par], :], True, True,
                     reads=[K("g"), "smalls"], writes=[("ps", pgl[par])])
            f2 = lambda v: v.rearrange("p a b -> p (a b)")
            P.act(f2(d_beg), ps[PS_N][:, 0:128], AF.Exp, reads=[("ps", PS_N)], writes=[K("beg")])
            P.tt("dve", f2(d_beg), f2(d_beg), f2(d_beta), ALU.mult, reads=[K("beg"), K("beta")], writes=[K("beg")])
            P.act(f2(d_et), ps[PS_N][:, 128:256], AF.Exp, reads=[("ps", PS_N)], writes=[K("et")])
            P.act(f2(d_egl0), ps[pgl[0]][:, 0:128], AF.Exp, reads=[("ps", pgl[0])], writes=[K("egl")])
            P.act(f2(d_egl1), ps[pgl[1]][:, 0:128], AF.Exp, reads=[("ps", pgl[1])], writes=[K("egl")])
            d_egl = [d_egl0, d_egl1]

            hinfo = {}

            def prep1(h):
                qb, qk = (qT, "qT") if h % 2 == 0 else (qTalt, "qTb")
                s = wload([(0, KC, 128, w_d[:, h * 128:(h + 1) * 128]),
                           (1024, KC, 128, w_d[:, D + h * 128:D + (h + 1) * 128]),
                           (2048, KC, 128, w_d[:, 2 * D + h * 128:2 * D + (h + 1) * 128]),
                           (3072, KC, 128, w_d[:, 3 * D + h * 128:3 * D + (h + 1) * 128])])
                wz = wview(s, 3072, KC, 128)
                for which in range(3):
                    ch = which * 8 + h
                    for tap in range(4):
                        col = smalls_conv[:, ((jl * 24 + ch) * 4 + tap):((jl * 24 + ch) * 4 + tap) + 1]
                        P.ts("dve", d_diag[:, which * 4 + tap, :], ident_bf[:], col, None, ALU.mult, None,
                             reads=["cst", "convw"], writes=[K("diag", which)])
                yield
                for which in range(3):
                    wmat = wview(s, which * 1024, KC, 128)
                    P.op("dve", lambda e: e.memset(d_pre[:, 0:3], 0.0), writes=[K("pre", 0)])
                    for tb in range(NTB):
                        tsl = slice(tb * 512, (tb + 1) * 512)
                        p = psP.next()
                        for kc in range(KC):
                            P.mm(ps[p][:], wmat[:, kc, :], hT[:, kc, tsl], kc == 0, kc == KC - 1,
                                 reads=[("w", s), ("h", kc, tb)], writes=[("ps", p)])
                            if kc == 3:
                                yield
                        P.copy("act", d_pre[:, 3 + tb * 512:3 + (tb + 1) * 512], ps[p][:], reads=[("ps", p)], writes=[K("pre", tb), K("pre", tb + 1)])
                        yield
                    for tb in range(NTB):
                        tsl = slice(tb * 512, (tb + 1) * 512)
                        pc = psZ.next()
                        for tap in range(4):
                            P.mm(ps[pc][:], d_diag[:, which * 4 + tap, :], d_pre[:, tb * 512 + tap:tb * 512 + tap + 512],
                                 tap == 0, tap == 3, reads=[K("diag", which), K("pre", tb), K("pre", tb + 1)], writes=[("ps", pc)])
                        yield
                        if which == 2:
                            P.act(d_vf[:, tsl], ps[pc][:], AF.Silu, reads=[("ps", pc)], writes=[K("vf", tb)])
                        else:
                            dst, dkey = (qb, qk) if which == 0 else (kT, "kT")
                            i = tmpring.next()
                            P.act(tmp[i][:], ps[pc][:], AF.Silu, reads=[("ps", pc)], writes=[("tmp", i)])
                            j = sqring.next()
                            P.act(sq[j][:], tmp[i][:], AF.Square, reads=[("tmp", i)], writes=[("sq", j)])
                            P.mm(ps[PS_N][:], ones_bf[:], sq[j][:], True, True, reads=[("sq", j), "cst"], writes=[("ps", PS_N)])
                            r = rstdring.next()
                            P.act(rstd[r][:], ps[PS_N][:], AF.Ln, reads=[("ps", PS_N)], writes=[("rstd", r)], bias=EPS, scale=1.0)
                            P.act(rstd[r][:], rstd[r][:], AF.Exp, reads=[("rstd", r)], writes=[("rstd", r)], scale=-0.5)
                            P.stt("dve", dst[:, tsl], tmp[i][:], (128.0 ** -0.5) if which == 0 else 1.0, rstd[r][:], ALU.mult, ALU.mult,
                                  reads=[("tmp", i), ("rstd", r)], writes=[(dkey, tb)])
                        yield
                hinfo[h] = (s, wz)

            def part2(h):
                qb, qk = (qT, "qT") if h % 2 == 0 else (qTalt, "qTb")
                for (src, skey, dst, dkey) in ((kT, "kT", d_ktok, "ktok"), (d_vf, None, vS, "vS")):
                    for t4 in range(4):
                        p = psP.next()
                        for jj in range(4):
                            tt_ = t4 * 4 + jj
                            rk = (skey, t4) if skey else K("vf", t4)
                            P.mm(ps[p][:, jj * 128:(jj + 1) * 128], src[:, tt_ * 128:(tt_ + 1) * 128], ident_bf[:], True, True,
                                 reads=[rk, "cst"], writes=[("ps", p)])
                        wk_ = K("ktok", t4) if dkey == "ktok" else ("vS", t4)
                        P.copy("act", dst[:, t4 * 4:(t4 + 1) * 4, :], ps[p][:].rearrange("p (j n) -> p j n", j=4),
                               reads=[("ps", p)], writes=[wk_])
                allk = [K("ktok", t4) for t4 in range(4)]
                allv = [("vS", t4) for t4 in range(4)]
                bc = lambda v: v[:, :, h:h + 1].broadcast_to([128, 16, 128])
                P.tt("dve", vS[:], vS[:], bc(d_beta), ALU.mult, reads=allv + [K("beta")], writes=allv)
                P.tt("dve", d_ktail, d_ktok, bc(d_et), ALU.mult, reads=allk + [K("et")], writes=[K("ktail")])
                P.tt("dve", d_ktok, d_ktok, bc(d_beg), ALU.mult, reads=allk + [K("beg")], writes=allk)
                P.tt("dve", d_Y, tri_le[:, None, :].broadcast_to([128, 16, 64]), d_g[:, :, h:h + 1].broadcast_to([128, 16, 64]),
                     ALU.mult, reads=[K("g"), "smalls"], writes=[K("Y")])
                def grp(gq):
                    Rg = rstd[1][:] if gq == 0 else rstd[0][:]
                    rk_ = ("rstd", 1) if gq == 0 else ("rstd", 0)
                    Rbg = aT[0][:].rearrange("p (a b) -> p a b", a=8) if gq == 0 else sq[0][:].rearrange("p (a b) -> p a b", a=8)
                    rbk_ = K("Rb") if gq == 0 else ("sq", 0)
                    def blocks():
                        for t8 in range(8):
                            for par in range(2):
                                tt_ = gq * 8 + t8
                                yield t8, tt_, HS[par], par, tt_ * 128 + par * 64, slice(t8 * 64, (t8 + 1) * 64)
                    kkeys = [("kT", gq * 2), ("kT", gq * 2 + 1)]
                    qkeys = [(qk, gq * 2), (qk, gq * 2 + 1)]
                    yield
                    pk = psZ.next()
                    for t8, tt_, hs, par, c0, cs in blocks():
                        P.mm(ps[pk][hs, cs], kT[:, c0:c0 + 64], kT[:, c0:c0 + 64], True, True, reads=kkeys, writes=[("ps", pk)])
                    yield
                    pd = psE.next()
                    for t8, tt_, hs, par, c0, cs in blocks():
                        P.mm(ps[pd][hs, cs], d_Y[hs, tt_, :], onesf[hs, 0:64], True, False, reads=[K("Y"), "smalls"], writes=[("ps", pd)])
                        P.mm(ps[pd][hs, cs], negones[hs, :], d_Y[hs, tt_, :], False, True, reads=[K("Y"), "smalls"], writes=[("ps", pd)])
                    i = tmpring.next()
                    v3 = lambda ap: ap.rearrange("p (a b) -> p a b", a=8)
                    P.tt("dve", v3(tmp[i][:]), v3(ps[pd][:]), neg_strict[:, None, :].broadcast_to([128, 8, 64]), ALU.add,
                         reads=[("ps", pd), "smalls"], writes=[("tmp", i)])
                    P.act(tmp[i][:], tmp[i][:], AF.Exp, reads=[("tmp", i)], writes=[("tmp", i)])
                    P.tt("dve", tmp[i][:], tmp[i][:], ps[pk][:], ALU.mult, reads=[("tmp", i), ("ps", pk)], writes=[("tmp", i)])
                    cur = nxt = gq
                    nb_bc = d_nbeta[:, gq * 8:(gq + 1) * 8, h:h + 1].broadcast_to([128, 8, 64])
                    P.tt("dve", d_XT[cur], v3(tmp[i][:]), nb_bc, ALU.mult, reads=[("tmp", i), K("nbeta")], writes=[K("XT", cur)])
                    yield
                    px = psZ.next()
                    for t8, tt_, hs, par, c0, cs in blocks():
                        P.mm(ps[px][hs, cs], d_XT[cur][hs, t8, :], ident_bf[hs, par * 64:par * 64 + 64], True, True,
                             reads=[K("XT", cur), "cst"], writes=[("ps", px)])
                    P.copy("act", d_X[cur], v3(ps[px][:]), reads=[("ps", px)], writes=[K("X", cur)])
                    P.tt("dve", v3(Rg), v3(ps[px][:]), ident64[:, None, :].broadcast_to([128, 8, 64]), ALU.add,
                         reads=[("ps", px), "smalls"], writes=[rk_])
                    P.copy("act", Rbg, v3(Rg), reads=[rk_], writes=[rbk_])
                    for lvl in range(1, 6):
                        if lvl < 5:
                            yield
                            p2 = psZ.next()
                            for t8, tt_, hs, par, c0, cs in blocks():
                                P.mm(ps[p2][hs, cs], d_XT[cur][hs, t8, :], d_X[cur][hs, t8, :], True, True,
                                     reads=[K("XT", cur), K("X", cur)], writes=[("ps", p2)])
                        yield
                        p2t = psE.next()
                        for t8, tt_, hs, par, c0, cs in blocks():
                            P.mm(ps[p2t][hs, cs], d_X[cur][hs, t8, :], d_XT[cur][hs, t8, :], True, True,
                                 reads=[K("XT", cur), K("X", cur)], writes=[("ps", p2t)])
                        if lvl < 5:
                            P.copy("act", d_X[nxt], v3(ps[p2][:]), reads=[("ps", p2)], writes=[K("X", nxt)])
                        P.copy("dve", d_XT[nxt], v3(ps[p2t][:]), reads=[("ps", p2t)], writes=[K("XT", nxt)])
                        yield
                        pr = psP.next()
                        for t8, tt_, hs, par, c0, cs in blocks():
                            P.mm(ps[pr][hs, cs], d_XT[nxt][hs, t8, :], Rbg[hs, t8, :], True, True,
                                 reads=[K("XT", nxt), rbk_], writes=[("ps", pr)])
                        P.tt("dve", Rg, Rg, ps[pr][:], ALU.add, reads=[rk_, ("ps", pr)], writes=[rk_])
                        if lvl < 5:
                            P.copy("act", Rbg, v3(Rg), reads=[rk_], writes=[rbk_])
                        else:
                            P.copy("act", d_TT[:, gq * 8:(gq + 1) * 8, :], v3(Rg), reads=[rk_], writes=[K("TT", gq)])
                    yield
                    pq = psZ.next()
                    for t8, tt_, hs, par, c0, cs in blocks():
                        P.mm(ps[pq][hs, cs], kT[:, c0:c0 + 64], qb[:, c0:c0 + 64], True, True, reads=kkeys + qkeys, writes=[("ps", pq)])
                    yield
                    pdt = psE.next()
                    for t8, tt_, hs, par, c0, cs in blocks():
                        P.mm(ps[pdt][hs, cs], onesf[hs, 0:64], d_Y[hs, tt_, :], True, False, reads=[K("Y"), "smalls"], writes=[("ps", pdt)])
                        P.mm(ps[pdt][hs, cs], d_Y[hs, tt_, :], negones[hs, :], False, True, reads=[K("Y"), "smalls"], writes=[("ps", pdt)])
                    i = tmpring.next()
                    P.tt("dve", v3(tmp[i][:]), v3(ps[pdt][:]), neg_get[:, None, :].broadcast_to([128, 8, 64]), ALU.add,
                         reads=[("ps", pdt), "smalls"], writes=[("tmp", i)])
                    P.act(tmp[i][:], tmp[i][:], AF.Exp, reads=[("tmp", i)], writes=[("tmp", i)])
                    P.tt("dve", d_attn[:, gq * 8:(gq + 1) * 8, :], v3(tmp[i][:]), v3(ps[pq][:]), ALU.mult,
                         reads=[("tmp", i), ("ps", pq)], writes=[K("attn", gq)])
                gens_ = [grp(0), grp(1)]
                while gens_:
                    for g_ in list(gens_):
                        try:
                            next(g_)
                        except StopIteration:
                            gens_.remove(g_)
                for tb in range(NTB):
                    tsl = slice(tb * 512, (tb + 1) * 512)
                    pab = [psE.next(), psE.next()]
                    for par in range(2):
                        for t4 in range(4):
                            tt_ = tb * 4 + t4
                            P.mm(ps[pab[par]][:, t4 * 64:(t4 + 1) * 64], onesf[HS[par], :], d_Y[HS[par], tt_, :], True, True,
                                 reads=[K("Y"), "smalls"], writes=[("ps", pab[par])])
                    i = tmpring.next()
                    tv = tmp[i][:].rearrange("p (t q i) -> p t q i", t=4, q=2)
                    for par in range(2):
                        P.act(tv[:, :, par, :], ps[pab[par]][:, 0:256].rearrange("p (t i) -> p t i", t=4), AF.Exp,
                              reads=[("ps", pab[par])], writes=[("tmp", i)])
                    P.tt("dve", qb[:, tsl], qb[:, tsl], tmp[i][:], ALU.mult, reads=[("tmp", i), (qk, tb)], writes=[(qk, tb)])
                    pwb = [psP.next(), psP.next()]
                    for par in range(2):
                        for t4 in range(4):
                            tt_ = tb * 4 + t4
                            P.mm(ps[pwb[par]][:, t4 * 64:(t4 + 1) * 64], d_ktok[HS[par], tt_, :], d_TT[HS[par], tt_, :], True, True,
                                 reads=[K("ktok", tb), K("TT", tb // 2)], writes=[("ps", pwb[par])])
                    nv = d_nwt[:, tsl].rearrange("p (t q i) -> p t q i", t=4, q=2)
                    for par in range(2):
                        P.act(nv[:, :, par, :], ps[pwb[par]][:, 0:256].rearrange("p (t i) -> p t i", t=4), AF.Identity,
                              reads=[("ps", pwb[par])], writes=[K("nwt", tb)], scale=-1.0)

            def scan(h, gen):
                qb, qk = (qT, "qT") if h % 2 == 0 else (qTalt, "qTb")
                s, wz = hinfo[h]

                def pull(n):
                    if gen is None:
                        return
                    for _ in range(n):
                        try:
                            next(gen)
                        except StopIteration:
                            return

                P.op("dve", lambda e: e.memset(d_S, 0.0), writes=[K("S")])
                P.op("dve", lambda e: e.memset(d_Sb, 0.0), writes=[K("Sb")])
                for n in range(32):
                    tt_, par = n // 2, n % 2
                    hs = HS[par]
                    c0 = n * 64
                    tb = n // 8
                    oc = slice((n % 8) * 64, (n % 8) * 64 + 64)
                    pv = psZ.next()
                    P.mm(ps[pv][hs, 0:128], d_TT[hs, tt_, :], vS[hs, tt_, :], True, False,
                         reads=[K("TT", tt_ // 8), ("vS", tt_ // 4)], writes=[("ps", pv)])
                    P.mm(ps[pv][hs, 0:128], d_nwt[:, c0:c0 + 64], d_Sb, False, True, reads=[K("nwt", tb), K("Sb")], writes=[("ps", pv)])
                    P.mm(ps[PS_O][:, oc], d_Sb, qb[:, c0:c0 + 64], True, False, reads=[K("Sb"), (qk, tb)], writes=[("ps", PS_O)])
                    P.copy("act", d_vnb[hs, :], ps[pv][hs, 0:128], reads=[("ps", pv)], writes=[K("vnb")])
                    pull(1)
                    P.mm(ps[PS_O][:, oc], d_vnb[hs, :], d_attn[hs, tt_, :], False, True, reads=[K("vnb"), K("attn", tt_ // 8)], writes=[("ps", PS_O)])
                    pS = psE.next()
                    P.mm(ps[pS][:, 0:128], d_ktail[hs, tt_, :], d_vnb[hs, :], True, True, reads=[K("ktail"), K("vnb")], writes=[("ps", pS)])
                    P.stt("dve", d_S, d_S, d_egl[par][:, tt_, h:h + 1], ps[pS][:, 0:128], ALU.mult, ALU.add,
                          reads=[K("S"), K("egl"), ("ps", pS)], writes=[K("S")])
                    P.copy("dve", d_Sb, d_S, reads=[K("S")], writes=[K("Sb")])
                    pull(1)
                    if n % 8 == 7:
                        tsl = slice(tb * 512, (tb + 1) * 512)
                        j = sqring.next()
                        P.act(sq[j][:], ps[PS_O][:], AF.Square, reads=[("ps", PS_O)], writes=[("sq", j)])
                        P.mm(ps[PS_N][:], ones_bf[:], sq[j][:], True, True, reads=[("sq", j), "cst"], writes=[("ps", PS_N)])
                        r = rstdring.next()
                        P.act(rstd[r][:], ps[PS_N][:], AF.Ln, reads=[("ps", PS_N)], writes=[("rstd", r)], bias=EPS, scale=1.0 / 128)
                        P.act(rstd[r][:], rstd[r][:], AF.Exp, reads=[("rstd", r)], writes=[("rstd", r)], scale=-0.5)
                        p = psP.next()
                        for kc in range(KC):
                            P.mm(ps[p][:], wz[:, kc, :], hT[:, kc, tsl], kc == 0, kc == KC - 1,
                                 reads=[("w", s), ("h", kc, tb)], writes=[("ps", p)])
                        i = tmpring.next()
                        P.act(tmp[i][:], ps[p][:], AF.Silu, reads=[("ps", p)], writes=[("tmp", i)])
                        i2 = tmpring.next()
                        P.stt("dve", tmp[i2][:], ps[PS_O][:], smalls[:, o_onorm + jl:o_onorm + jl + 1], rstd[r][:], ALU.mult, ALU.mult,
                              reads=[("ps", PS_O), ("rstd", r), "smalls"], writes=[("tmp", i2)])
                        P.tt("dve", d_oh[:, tsl], tmp[i2][:], tmp[i][:], ALU.mult, reads=[("tmp", i), ("tmp", i2)], writes=[K("oh", tb)])

            def outp(h):
                so = wload([(0, 1, 1024, dn_w_out_d[jl][h * 128:(h + 1) * 128, :])])
                wo = wview(so, 0, 1, 1024)
                for m in range(KC):
                    for tb in range(NTB):
                        tsl = slice(tb * 512, (tb + 1) * 512)
                        p = psP.next()
                        P.mm(ps[p][:], wo[:, 0, m * 128:(m + 1) * 128], d_oh[:, tsl], True, True,
                             reads=[("w", so), K("oh", tb)], writes=[("ps", p)])
                        P.stt("dve", xT[:, m, tsl], ps[p][:], modT[:, l, 16 + m:17 + m], xT[:, m, tsl], ALU.mult, ALU.add,
                              reads=[("ps", p), ("modT", l), ("x", m, tb)], writes=[("x", m, tb)])

            for _ in prep1(0):
                pass
            for h in range(8):
                part2(h)
                gen = prep1(h + 1) if h < 7 else None
                scan(h, gen)
                if gen is not None:
                    for _ in gen:
                        pass
                outp(h)
                run_hook(h)
            P.fence(lambda e: e.memset(dummy[0:1, 1:2], 0.0), ARENA_FAMS)

        import os
        DBG = os.environ.get("KDBG", "")
        first_layer = plan[0][0]
        for nb in range(0 if "noada" in DBG else 12):
            ada_block(first_layer, nb)
        for pi, (l, do_mix, do_ffn) in enumerate(plan):
            nxt = plan[pi + 1][0] if pi + 1 < len(plan) else None
            if do_mix:
                ada_hook[0] = nxt
                norm_mod(l, 0)
                if l % 2 == 0:
                    dn_mixer(l)
                else:
                    sb_mixer(l)
                ada_hook[0] = None
            if do_ffn:
                norm_mod(l, 1)
                ffn(l, None if do_mix else nxt)
            elif nxt is not None and not do_mix:
                for nb in range(12):
                    ada_block(nxt, nb)

        go = P.dma_group("st")
        for kc in range(KC):
            P.dma("sp", go, yT_d[kc * 128:(kc + 1) * 128, :], xT[:, kc, :],
                  reads=[("x", kc, tb) for tb in range(NTB)], writes=["yT"])
        P.op("sp", lambda e: e.nop(), reads=["yT"])
        P.emit()
    return nc


def _layout(inputs):
    f = lambda a: np.ascontiguousarray(np.asarray(a, dtype=np.float32))
    x = f(inputs["x"])
    c = f(inputs["c"])
    B = x.shape[0]
    shared = {
        "ada_w": f(inputs["ada_w"]),
        "ada_b": f(inputs["ada_b"]),
        "n1g": f(np.asarray(inputs["norm1_g"]).reshape(NL, KC, 128).transpose(2, 0, 1).reshape(128, NL * KC)),
        "n2g": f(np.asarray(inputs["norm2_g"]).reshape(NL, KC, 128).transpose(2, 0, 1).reshape(128, NL * KC)),
        "dn_w_in": f(inputs["dn_w_in"]),
        "dn_conv": f(np.asarray(inputs["dn_conv_w"]).reshape(2, 4, 24, 128).transpose(3, 0, 2, 1).reshape(128, 192)),
        "dn_alog": f(np.broadcast_to(np.asarray(inputs["dn_a_log"]).reshape(1, 16), (128, 16))),
        "dn_dtb": f(np.broadcast_to(np.asarray(inputs["dn_dt_bias"]).reshape(1, 16), (128, 16))),
        "dn_onorm": f(np.asarray(inputs["dn_onorm_g"]).T),
        "dn_w_out": f(inputs["dn_w_out"]),
        "sb_w_qkv": f(inputs["sb_w_qkv"]),
        "sb_qg": f(np.tile(np.asarray(inputs["sb_q_norm_g"]).T, (2, 1))),
        "sb_kg": f(np.tile(np.asarray(inputs["sb_k_norm_g"]).T, (2, 1))),
        "sb_w_out": f(inputs["sb_w_out"]),
        "ffn_w_in": f(inputs["ffn_w_in"]),
        "ffn_w_out": f(inputs["ffn_w_out"]),
        "cst": _consts(),
    }
    maps = []
    for b in range(B):
        m = dict(shared)
        m["xT"] = f(x[b].T)
        m["cT"] = f(c[b].reshape(KC, 128).T)
        maps.append(m)
    return maps


def kernel(**inputs):
    maps = _layout(inputs)
    nc = build()
    res = run_bass_kernel_spmd(nc, maps, core_ids=list(range(len(maps))))
    out = np.stack([np.ascontiguousarray(r["yT"].T) for r in res.results], axis=0)
    return out.astype(np.float32)
```

```python
import contextlib
import numpy as np
import concourse.bass as bass
import concourse.mybir as mybir
from concourse.bass_utils import run_bass_kernel_spmd

F32 = mybir.dt.float32
F32R = mybir.dt.float32r
BF16 = mybir.dt.bfloat16
F16 = mybir.dt.float16
AF = mybir.ActivationFunctionType
ALU = mybir.AluOpType

D = 1024
T = 2048
DFF = 2816
NL = 4
KC = 8
NTB = 4
EPS = 1e-6
NMOD = 48


class _Op:
    __slots__ = ("eng", "fn", "deps", "sig", "count", "dma", "batch", "is_mm")

    def __init__(self, eng, fn, is_mm=False):
        self.eng = eng
        self.fn = fn
        self.deps = []
        self.sig = False
        self.count = 0
        self.dma = None
        self.batch = 0
        self.is_mm = is_mm


class DmaGroup:
    def __init__(self, name):
        self.name = name
        self.ops = []
        self.batch = -1
        self.sem = None
        self.cum = {}

    def new_batch(self):
        self.batch += 1


class Prog:
    ENGS = ("pe", "act", "dve", "pool", "sp")

    def __init__(self, nc):
        self.nc = nc
        self.ops = {e: [] for e in self.ENGS}
        self.last_w = {}
        self.readers = {}
        self.groups = []

    def dma_group(self, name):
        g = DmaGroup(name)
        self.groups.append(g)
        return g

    def _track(self, op, reads, writes):
        psr = [r for r in reads if isinstance(r, tuple) and r[0] == "ps"]
        if psr:
            reads = [r for r in reads if not (isinstance(r, tuple) and r[0] == "ps")]
            writes = list(writes) + psr
        deps = {}
        for r in reads:
            w = self.last_w.get(r)
            if w is not None:
                deps[id(w)] = w
            self.readers.setdefault(r, []).append(op)
        for wkey in writes:
            w = self.last_w.get(wkey)
            if w is not None:
                deps[id(w)] = w
            for rd in self.readers.get(wkey, ()):
                if rd is not op:
                    deps[id(rd)] = rd
            self.readers[wkey] = []
            self.last_w[wkey] = op
        for d in deps.values():
            if d is op:
                continue
            if d.is_mm and op.is_mm:
                continue
            if d.dma is not None and op.dma is d.dma and d.batch == op.batch:
                continue
            d.sig = True
            op.deps.append(d)

    def op(self, eng, fn, reads=(), writes=(), is_mm=False):
        o = _Op(eng, fn, is_mm=is_mm)
        self._track(o, reads, writes)
        self.ops[eng].append(o)
        return o

    def dma(self, eng, grp, out, in_, reads=(), writes=()):
        o = _Op(eng, lambda e: e.dma_start(out=out, in_=in_))
        o.dma = grp
        if grp.batch < 0:
            grp.new_batch()
        o.batch = grp.batch
        grp.ops.append(o)
        self._track(o, reads, writes)
        self.ops[eng].append(o)
        return o

    def mm(self, out, lhsT, rhs, start, stop, reads, writes):
        return self.op("pe", lambda e: e.matmul(out, lhsT, rhs, start=start, stop=stop), reads, writes, is_mm=True)

    def tr(self, out, in_, ident, reads, writes):
        return self.op("pe", lambda e: e.transpose(out, in_, ident), reads, writes, is_mm=True)

    def act(self, out, in_, func, reads, writes, bias=0.0, scale=1.0):
        return self.op("act", lambda e: e.activation(out, in_, func, bias=bias, scale=scale), reads, writes)

    def tt(self, eng, out, in0, in1, op, reads, writes):
        return self.op(eng, lambda e: e.tensor_tensor(out, in0, in1, op), reads, writes)

    def stt(self, eng, out, in0, scalar, in1, op0, op1, reads, writes):
        return self.op(eng, lambda e: e.scalar_tensor_tensor(out, in0, scalar, in1, op0, op1), reads, writes)

    def ts(self, eng, out, in0, s1, s2, op0, op1, reads, writes):
        if s2 is None:
            return self.op(eng, lambda e: e.tensor_scalar(out, in0, s1, None, op0), reads, writes)
        return self.op(eng, lambda e: e.tensor_scalar(out, in0, s1, s2, op0, op1), reads, writes)

    def copy(self, eng, out, in_, reads, writes):
        if eng == "act":
            return self.op("act", lambda e: e.activation(out, in_, AF.Copy), reads, writes)
        return self.op(eng, lambda e: e.tensor_copy(out, in_), reads, writes)

    def fence(self, fn, fams):
        fam = lambda k: k if isinstance(k, str) else k[0]
        keys = [k for k in (set(self.last_w) | set(self.readers)) if fam(k) in fams]
        return self.op("dve", fn, reads=(), writes=keys)

    def emit(self):
        nc = self.nc
        for e in self.ENGS:
            c = 0
            for o in self.ops[e]:
                if o.dma is None and o.sig:
                    c += 1
                    o.count = c
        for g in self.groups:
            n = 0
            for o in g.ops:
                n += 1
                g.cum[o.batch] = n
        with contextlib.ExitStack() as st:
            esem = {e: st.enter_context(nc.semaphore("s_" + e)) for e in self.ENGS}
            for g in self.groups:
                g.sem = st.enter_context(nc.semaphore("d_" + g.name))
            block = st.enter_context(nc.Block())

            def run(ename, eng):
                waited = {}
                for o in self.ops[ename]:
                    for d in o.deps:
                        if d.dma is not None:
                            sem, val = d.dma.sem, 16 * d.dma.cum[d.batch]
                        else:
                            sem, val = esem[d.eng], d.count
                        k = id(sem)
                        if waited.get(k, 0) < val:
                            eng.wait_ge(sem, val)
                            waited[k] = val
                    ins = o.fn(eng)
                    if o.dma is not None:
                        ins.then_inc(o.dma.sem, 16)
                    elif o.sig:
                        ins.then_inc(esem[ename], 1)

            @block.tensor
            def _(eng):
                run("pe", eng)

            @block.scalar
            def _(eng):
                run("act", eng)

            @block.vector
            def _(eng):
                run("dve", eng)

            @block.gpsimd
            def _(eng):
                run("pool", eng)

            @block.sync
            def _(eng):
                run("sp", eng)


class Ring:
    def __init__(self, items):
        self.items = items
        self.i = -1

    def next(self):
        self.i = (self.i + 1) % len(self.items)
        return self.items[self.i]


C_ONES = 0
C_BLK = 128
C_IDENT = 256
C_NEGTRI = 384
C_NEGONES = 512
C_MASK = 640
C_DN = C_MASK + 4 * 512
DN_TRI_LE = 0
DN_TRI_GT = 64
DN_NEG_STRICT = 128
DN_NEG_GET = 192
DN_IDENT64 = 256
DN_NEGONES = 320
DN_NCOL = 384
C_TOTAL = C_DN + DN_NCOL


def _consts():
    c = np.zeros((128, C_TOTAL), np.float32)
    c[:, C_ONES:C_ONES + 128] = 1.0
    c[:64, C_BLK:C_BLK + 64] = 1.0
    c[64:, C_BLK + 64:C_BLK + 128] = 1.0
    c[:, C_IDENT:C_IDENT + 128] = np.eye(128, dtype=np.float32)
    j = np.arange(128)[:, None]
    s = np.arange(128)[None, :]
    c[:, C_NEGTRI:C_NEGTRI + 128] = -(j >= s).astype(np.float32)
    c[:, C_NEGONES:C_NEGONES + 128] = -1.0
    t = np.arange(512)[None, :]
    for jj in range(4):
        c[:, C_MASK + jj * 512:C_MASK + (jj + 1) * 512] = ((j + 128 * jj) < t).astype(np.float32)
    a = np.arange(64)[:, None]
    b = np.arange(64)[None, :]
    for half in range(2):
        r = slice(half * 64, half * 64 + 64)
        o = C_DN
        c[r, o + DN_TRI_LE:o + DN_TRI_LE + 64] = (a <= b)
        c[r, o + DN_TRI_GT:o + DN_TRI_GT + 64] = (a > b)
        c[r, o + DN_NEG_STRICT:o + DN_NEG_STRICT + 64] = ((a > b) - 1.0) * 3e4
        c[r, o + DN_NEG_GET:o + DN_NEG_GET + 64] = ((b >= a) - 1.0) * 3e4
        c[r, o + DN_IDENT64:o + DN_IDENT64 + 64] = (a == b)
        c[r, o + DN_NEGONES:o + DN_NEGONES + 64] = -1.0
    return c


def build(plan=None):
    if plan is None:
        plan = [(l, True, True) for l in range(NL)]
    nc = bass.Bass("TRN2", target_bir_lowering=False)
    dt_in = lambda n, s: nc.dram_tensor(n, s, F32, kind="ExternalInput").ap()
    xT_d = dt_in("xT", [D, T])
    cT_d = dt_in("cT", [128, KC])
    ada_w_d = dt_in("ada_w", [NL, D, 6 * D])
    ada_b_d = dt_in("ada_b", [NL, 6 * D])
    n1g_d = dt_in("n1g", [128, NL * KC])
    n2g_d = dt_in("n2g", [128, NL * KC])
    dn_w_in_d = dt_in("dn_w_in", [2, D, 4112])
    dn_conv_d = dt_in("dn_conv", [128, 2 * 24 * 4])
    dn_alog_d = dt_in("dn_alog", [128, 16])
    dn_dtb_d = dt_in("dn_dtb", [128, 16])
    dn_onorm_d = dt_in("dn_onorm", [128, 2])
    dn_w_out_d = dt_in("dn_w_out", [2, D, D])
    sb_w_qkv_d = dt_in("sb_w_qkv", [2, D, 3 * D])
    sb_qg_d = dt_in("sb_qg", [128, 2])
    sb_kg_d = dt_in("sb_kg", [128, 2])
    sb_w_out_d = dt_in("sb_w_out", [2, D, D])
    ffn_w_in_d = dt_in("ffn_w_in", [NL, D, 2 * DFF])
    ffn_w_out_d = dt_in("ffn_w_out", [NL, DFF, D])
    cst_d = dt_in("cst", [128, C_TOTAL])
    yT_d = nc.dram_tensor("yT", [D, T], F32, kind="ExternalOutput").ap()

    P = Prog(nc)
    with contextlib.ExitStack() as st:
        sb = lambda n, s, d: st.enter_context(nc.sbuf_tensor("sb_" + n, s, d))
        xT = sb("xT", [128, KC, T], F32)
        hT = sb("hT", [128, KC, T], BF16)
        S = sb("S", [128, KC, T], BF16)
        wslots = [sb(f"wslot{i}", [128, 4096], BF16) for i in range(3)]
        wgrp = [P.dma_group(f"w{i}") for i in range(3)]
        wring = Ring(list(range(3)))
        adast = [sb(f"adast{i}", [128, 512], F32R) for i in range(3)]
        adagrp = [P.dma_group(f"a{i}") for i in range(3)]
        adaring = Ring(list(range(3)))
        adab = sb("adab", [1, 512], F32)
        adabgrp = P.dma_group("adab")
        modrow = adab
        modT = sb("modT", [128, NL, NMOD], F32)
        AB = sb("AB", [128, NL, 2, KC], F32)
        smalls = sb("smalls", [128, 2 * NL * KC + 16 + 16 + 2 + 2 + 2 + KC], F32)
        o_n1 = 0
        o_n2 = NL * KC
        o_alog = 2 * NL * KC
        o_dtb = o_alog + 16
        o_onorm = o_dtb + 16
        o_qg = o_onorm + 2
        o_kg = o_qg + 2
        o_c = o_kg + 2
        smalls_conv = sb("convw", [128, 192], F32)
        condr = sb("condr", [128, KC], F32R)
        one11 = sb("one11", [1, 1], F32)
        qgs = sb("qgs", [128, 2], F32)
        ones_bf = sb("ones_bf", [128, 128], BF16)
        blk_bf = sb("blk_bf", [128, 128], BF16)
        ident_bf = sb("ident_bf", [128, 128], BF16)
        negtri_r = sb("negtri_r", [128, 128], F16)
        negones_r = sb("negones_r", [128, 128], F16)
        masks = sb("masks", [128, 4, 512], BF16)
        sq = [sb(f"sq{i}", [128, 512], BF16) for i in range(2)]
        sqring = Ring([0, 1])
        rstd = [sb(f"rstd{i}", [128, 512], F32) for i in range(2)]
        rstdring = Ring([0, 1])
        tmp = [sb(f"tmp{i}", [128, 512], F32) for i in range(2)]
        tmpring = Ring([0, 1])
        qT = sb("qT", [128, T], BF16)
        kT = sb("kT", [128, T], BF16)
        vS = sb("vS", [128, 16, 128], BF16)
        sp4 = sb("sp4", [128, 2048], F16)
        spt = [sp4[:, i * 512:(i + 1) * 512] for i in range(3)]
        spring = Ring([0, 1, 2])
        spsum = sp4[:, 1536:2048]
        spsumB = sb("spsumB", [128, 512], F16)[:]
        aT = [sb(f"aT{i}", [128, 512], BF16) for i in range(2)]
        aring = Ring([0, 1])

        wab = sb("wab", [128, KC, 16], BF16)
        wabgrp = P.dma_group("wab")
        dncst = sb("dncst", [128, DN_NCOL], F32)
        onesf = sb("onesf", [128, 128], F32)
        dummy = sb("dummy", [1, 8], F32)
        S2 = S[:].rearrange("p k t -> p (k t)")
        d_oh = S2[:, 0:2048]
        d_ktok = S2[:, 2176:4224].rearrange("p (a b) -> p a b", a=16)
        d_ktail = S2[:, 4224:6272].rearrange("p (a b) -> p a b", a=16)
        d_nwt = S2[:, 6272:8320]
        d_Y = S2[:, 8320:10368].bitcast(F32).rearrange("p (a b) -> p a b", a=16)
        d_attn = S2[:, 10368:11392].rearrange("p (a b) -> p a b", a=16)
        d_TT = S2[:, 11392:12416].rearrange("p (a b) -> p a b", a=16)
        d_diag = S2[:, 12416:13952].rearrange("p (a b) -> p a b", a=12)
        d_sc = S2[:, 13952:16384].bitcast(F32)
        d_ab = d_sc[:, 0:256].rearrange("p (a b) -> p a b", a=16)
        sc3 = lambda i: d_sc[:, 256 + i * 128:256 + (i + 1) * 128].rearrange("p (a b) -> p a b", a=16)
        d_g, d_beta, d_nbeta, d_beg, d_et, d_egl0, d_egl1 = [sc3(i) for i in range(7)]
        d_nea = d_sc[:, 1152:1160]
        dxbuf = sb("dxbuf", [128, 4, 512], BF16)
        d_pre = sb("pre2", [128, 2176], BF16)[:]
        d_vf = sp4[:].bitcast(BF16)
        qTalt = masks[:].rearrange("p j t -> p (j t)")
        d_X = [dxbuf[:, 0, :].rearrange("p (a b) -> p a b", a=8), dxbuf[:, 1, :].rearrange("p (a b) -> p a b", a=8)]
        d_XT = [dxbuf[:, 2, :].rearrange("p (a b) -> p a b", a=8), dxbuf[:, 3, :].rearrange("p (a b) -> p a b", a=8)]
        d_R = rstd[1][:]
        d_Rb = aT[0][:].rearrange("p (a b) -> p a b", a=8)
        d_S = aT[1][:, 0:256].bitcast(F32)
        d_Sb = aT[1][:, 256:384]
        d_vnb = aT[1][:, 384:512]
        ARENA_FAMS = ("S", "sp", "spsum", "aT", "dn", "masks", "qTb")

        ps = [st.enter_context(nc.psum_tensor(f"ps{i}", [128, 512], F32)) for i in range(8)]
        psP = Ring([0, 1])
        PS_N = 2
        psZ = Ring([3, 4])
        psE = Ring([5, 6])
        psZE = Ring([3, 4, 5, 6])
        PS_O = 7

        g0 = P.dma_group("ld0")
        for kc in range(KC):
            P.dma("sp", g0, xT[:, kc, :], xT_d[kc * 128:(kc + 1) * 128, :],
                  writes=[("x", kc, tb) for tb in range(NTB)])
        gs = P.dma_group("lds")
        P.dma("sp", gs, smalls[:, o_n1:o_n1 + NL * KC], n1g_d, writes=["smalls"])
        P.dma("sp", gs, smalls[:, o_n2:o_n2 + NL * KC], n2g_d, writes=["smalls"])
        P.dma("sp", gs, smalls[:, o_alog:o_alog + 16], dn_alog_d, writes=["smalls"])
        P.dma("sp", gs, smalls[:, o_dtb:o_dtb + 16], dn_dtb_d, writes=["smalls"])
        P.dma("sp", gs, smalls[:, o_onorm:o_onorm + 2], dn_onorm_d, writes=["smalls"])
        P.dma("sp", gs, smalls[:, o_qg:o_qg + 2], sb_qg_d, writes=["smalls"])
        P.dma("sp", gs, smalls[:, o_kg:o_kg + 2], sb_kg_d, writes=["smalls"])
        P.dma("sp", gs, smalls[:, o_c:o_c + KC], cT_d, writes=["smalls"])
        P.dma("sp", gs, smalls_conv[:], dn_conv_d, writes=["convw"])
        gc = P.dma_group("ldc")
        P.dma("pool", gc, ones_bf[:], cst_d[:, C_ONES:C_ONES + 128], writes=["cst"])
        P.dma("pool", gc, blk_bf[:], cst_d[:, C_BLK:C_BLK + 128], writes=["cst"])
        P.dma("pool", gc, ident_bf[:], cst_d[:, C_IDENT:C_IDENT + 128], writes=["cst"])
        P.dma("pool", gc, negtri_r[:], cst_d[:, C_NEGTRI:C_NEGTRI + 128], writes=["cst"])
        P.dma("pool", gc, negones_r[:], cst_d[:, C_NEGONES:C_NEGONES + 128], writes=["cst"])
        mgrp = P.dma_group("masks")
        P.dma("pool", mgrp, masks[:], cst_d[:, C_MASK:C_MASK + 2048].rearrange("p (j t) -> p j t", j=4), writes=["masks"])
        P.op("dve", lambda e: e.memset(dummy[0:1, 2:3], 0.0),
             writes=[("sp", 0), ("sp", 1), ("sp", 2), "spsum", ("aT", 0), ("aT", 1)]
             + [("S", c_, t_) for c_ in range(KC) for t_ in range(NTB)])
        P.op("dve", lambda e: e.memset(one11[:], 1.0), writes=["one11"])
        P.dma("sp", gs, dncst[:], cst_d[:, C_DN:C_DN + DN_NCOL], writes=["smalls"])
        P.dma("sp", gs, onesf[:], cst_d[:, C_ONES:C_ONES + 128], writes=["smalls"])
        P.act(condr[:], smalls[:, o_c:o_c + KC], AF.Silu, reads=["smalls"], writes=["condr"])
        P.ts("dve", qgs[:], smalls[:, o_qg:o_qg + 2], 0.125, None, ALU.mult, None, reads=["smalls"], writes=["qgs"])

        def wload(pieces):
            s = wring.next()
            wgrp[s].new_batch()
            for (off, kcs, ncols, src) in pieces:
                dst = wslots[s][:, off:off + kcs * ncols].rearrange("p (k n) -> p k n", k=kcs)
                P.dma("pool", wgrp[s], dst, src.rearrange("(k p) n -> p k n", p=128), writes=[("w", s)])
            return s

        def wview(s, off, kcs, ncols):
            return wslots[s][:, off:off + kcs * ncols].rearrange("p (k n) -> p k n", k=kcs)

        def ada_block(l, nb):
            pso = ps[PS_O]
            for kc in range(KC):
                a = adaring.next()
                adagrp[a].new_batch()
                P.dma("pool", adagrp[a], adast[a][:], ada_w_d[l, kc * 128:(kc + 1) * 128, nb * 512:(nb + 1) * 512],
                      writes=[("adast", a)])
                P.mm(pso[0:1, :], condr[:, kc:kc + 1], adast[a][:], kc == 0, kc == KC - 1,
                     reads=[("adast", a), "condr"], writes=[("ps", PS_O)])
            adabgrp.new_batch()
            P.dma("sp", adabgrp, adab[:], ada_b_d[l:l + 1, nb * 512:(nb + 1) * 512], writes=["adab"])
            P.tt("dve", modrow[:], pso[0:1, :], adab[:], ALU.add, reads=[("ps", PS_O), "adab"], writes=["adab"])
            for j in range(4):
                col = nb * 4 + j
                P.mm(ps[PS_N][:, col:col + 1], modrow[0:1, j * 128:(j + 1) * 128], one11[0:1, 0:1], True, True,
                     reads=["adab", "one11"], writes=[("ps", PS_N)])
            P.copy("dve", modT[:, l, nb * 4:(nb + 1) * 4], ps[PS_N][:, nb * 4:(nb + 1) * 4], reads=[("ps", PS_N)], writes=[("modT", l)])
            if nb == 11:
                P.stt("dve", AB[:, l, 0, :], modT[:, l, 8:16], 1.0, smalls[:, o_n1 + l * KC:o_n1 + (l + 1) * KC],
                      ALU.add, ALU.mult, reads=[("modT", l), "smalls"], writes=[("AB", l)])
                P.stt("dve", AB[:, l, 1, :], modT[:, l, 32:40], 1.0, smalls[:, o_n2 + l * KC:o_n2 + (l + 1) * KC],
                      ALU.add, ALU.mult, reads=[("modT", l), "smalls"], writes=[("AB", l)])

        def norm_mod(l, which):
            sh = 0 if which == 0 else 24
            for tb in range(NTB):
                tsl = slice(tb * 512, (tb + 1) * 512)
                for kc in range(KC):
                    i = sqring.next()
                    P.act(sq[i][:], xT[:, kc, tsl], AF.Square, reads=[("x", kc, tb)], writes=[("sq", i)])
                    P.mm(ps[PS_N][:], ones_bf[:], sq[i][:], kc == 0, kc == KC - 1,
                         reads=[("sq", i), "cst"], writes=[("ps", PS_N)])
                r = rstdring.next()
                P.act(rstd[r][:], ps[PS_N][:], AF.Ln, reads=[("ps", PS_N)], writes=[("rstd", r)], bias=EPS, scale=1.0 / D)
                P.act(rstd[r][:], rstd[r][:], AF.Exp, reads=[("rstd", r)], writes=[("rstd", r)], scale=-0.5)
                for kc in range(KC):
                    i = tmpring.next()
                    P.tt("dve", tmp[i][:], xT[:, kc, tsl], rstd[r][:], ALU.mult,
                         reads=[("x", kc, tb), ("rstd", r)], writes=[("tmp", i)])
                    P.act(hT[:, kc, tsl], tmp[i][:], AF.Identity, reads=[("tmp", i), ("AB", l), ("modT", l)],
                          writes=[("h", kc, tb)], bias=modT[:, l, sh + kc:sh + kc + 1], scale=AB[:, l, which, kc:kc + 1])

        def out_proj(l, w_d, nkc, gate_off, skeys_fn, s_view_fn):
            for mb in range(2 if nkc <= 8 else 4):
                ncols = 512 if nkc <= 8 else 256
                s = wload([(0, nkc, ncols, w_d[:, mb * ncols:(mb + 1) * ncols])])
                wv = wview(s, 0, nkc, ncols)
                for mc in range(ncols // 128):
                    m = mb * (ncols // 128) + mc
                    for tb in range(NTB):
                        tsl = slice(tb * 512, (tb + 1) * 512)
                        p = psP.next()
                        for kc in range(nkc):
                            P.mm(ps[p][:], wv[:, kc, mc * 128:(mc + 1) * 128], s_view_fn(kc, tsl), kc == 0, kc == nkc - 1,
                                 reads=[("w", s), skeys_fn(kc, tb)], writes=[("ps", p)])
                        P.stt("dve", xT[:, m, tsl], ps[p][:], modT[:, l, gate_off + m:gate_off + m + 1], xT[:, m, tsl],
                              ALU.mult, ALU.add, reads=[("ps", p), ("modT", l), ("x", m, tb)], writes=[("x", m, tb)])

        def ffn(l, ada_next):
            groups = [(0, 8), (8, 7), (15, 7)]
            for gi, (c0, ncg) in enumerate(groups):
                j = 0
                while j < ncg:
                    nsub = min(4, ncg - j)
                    cc = c0 + j
                    sg = wload([(0, KC, nsub * 128, ffn_w_in_d[l, :, cc * 128:(cc + nsub) * 128])])
                    su = wload([(0, KC, nsub * 128, ffn_w_in_d[l, :, DFF + cc * 128:DFF + (cc + nsub) * 128])])
                    wg = wview(sg, 0, KC, nsub * 128)
                    wu = wview(su, 0, KC, nsub * 128)
                    for q in range(nsub):
                        for tb in range(NTB):
                            tsl = slice(tb * 512, (tb + 1) * 512)
                            pg = psZ.next()
                            pu = psE.next()
                            for kc in range(KC):
                                P.mm(ps[pg][:], wg[:, kc, q * 128:(q + 1) * 128], hT[:, kc, tsl], kc == 0, kc == KC - 1,
                                     reads=[("w", sg), ("h", kc, tb)], writes=[("ps", pg)])
                            for kc in range(KC):
                                P.mm(ps[pu][:], wu[:, kc, q * 128:(q + 1) * 128], hT[:, kc, tsl], kc == 0, kc == KC - 1,
                                     reads=[("w", su), ("h", kc, tb)], writes=[("ps", pu)])
                            i = tmpring.next()
                            P.act(tmp[i][:], ps[pg][:], AF.Silu, reads=[("ps", pg)], writes=[("tmp", i)])
                            P.tt("dve", S[:, j + q, tsl], tmp[i][:], ps[pu][:], ALU.mult,
                                 reads=[("tmp", i), ("ps", pu)], writes=[("S", j + q, tb)])
                    j += nsub
                out_proj(l, ffn_w_out_d[l, c0 * 128:(c0 + ncg) * 128, :], ncg, 40,
                         lambda kc, tb: ("S", kc, tb), lambda kc, tsl: S[:, kc, tsl])
                if ada_next is not None:
                    for nb in range(gi * 4, gi * 4 + 4):
                        ada_block(ada_next, nb)

        ada_hook = [None]

        def run_hook(i):
            if ada_hook[0] is not None and 1 <= i <= 4:
                for nb in range((i - 1) * 3, (i - 1) * 3 + 3):
                    ada_block(ada_hook[0], nb)

        def sb_mixer(l):
            jl = l // 2
            w_d = sb_w_qkv_d[jl]
            mgrp.new_batch()
            P.dma("pool", mgrp, masks[:], cst_d[:, C_MASK:C_MASK + 2048].rearrange("p (j t) -> p j t", j=4), writes=["masks"])
            for hp in range(8):
                s = wload([(0, KC, 128, w_d[:, hp * 128:(hp + 1) * 128]),
                           (1024, KC, 128, w_d[:, D + hp * 128:D + (hp + 1) * 128]),
                           (2048, KC, 128, w_d[:, 2 * D + hp * 128:2 * D + (hp + 1) * 128])])
                wq = wview(s, 0, KC, 128)
                wk = wview(s, 1024, KC, 128)
                wv = wview(s, 2048, KC, 128)
                for (wmat, dst, dkey, gcol) in ((wq, qT, "qT", qgs[:, jl:jl + 1]),
                                                (wk, kT, "kT", smalls[:, o_kg + jl:o_kg + jl + 1])):
                    for tb in range(NTB):
                        tsl = slice(tb * 512, (tb + 1) * 512)
                        p = psP.next()
                        for kc in range(KC):
                            P.mm(ps[p][:], wmat[:, kc, :], hT[:, kc, tsl], kc == 0, kc == KC - 1,
                                 reads=[("w", s), ("h", kc, tb)], writes=[("ps", p)])
                        i = sqring.next()
                        P.act(sq[i][:], ps[p][:], AF.Square, reads=[("ps", p)], writes=[("sq", i)])
                        P.mm(ps[PS_N][:], blk_bf[:], sq[i][:], True, True, reads=[("sq", i), "cst"], writes=[("ps", PS_N)])
                        r = rstdring.next()
                        P.act(rstd[r][:], ps[PS_N][:], AF.Ln, reads=[("ps", PS_N)], writes=[("rstd", r)], bias=EPS, scale=1.0 / 64)
                        P.act(rstd[r][:], rstd[r][:], AF.Exp, reads=[("rstd", r)], writes=[("rstd", r)], scale=-0.5)
                        P.stt("dve", dst[:, tsl], ps[p][:], gcol, rstd[r][:], ALU.mult, ALU.mult,
                              reads=[("ps", p), ("rstd", r), "qgs", "smalls"], writes=[(dkey, tb)])
                for t4 in range(4):
                    p = psP.next()
                    for jj in range(4):
                        tt_ = t4 * 4 + jj
                        for kc in range(KC):
                            P.mm(ps[p][:, jj * 128:(jj + 1) * 128], hT[:, kc, tt_ * 128:(tt_ + 1) * 128], wv[:, kc, :],
                                 kc == 0, kc == KC - 1, reads=[("w", s), ("h", kc, t4)], writes=[("ps", p)])
                    P.copy("dve", vS[:, t4 * 4:(t4 + 1) * 4, :], ps[p][:].rearrange("p (j n) -> p j n", j=4),
                           reads=[("ps", p)], writes=[("vS", t4)])
                tiles = []
                for qb in range(NTB):
                    nkb = 4 * (qb + 1)
                    for kb in range(nkb - 1, -1, -1):
                        for hd in range(2):
                            tiles.append(dict(hd=hd, qb=qb, kb=kb, first=(kb == nkb - 1), last=(kb == 0),
                                              diag=(kb >= 4 * qb), jm=kb - 4 * qb))
                SPS = [(spsum, "spsum"), (spsumB, "spsumB")]
                PSO = [PS_O, 0]

                def stA(t):
                    r0 = t["hd"] * 64
                    ksl = slice(t["kb"] * 128, (t["kb"] + 1) * 128)
                    c0 = 128 * t["jm"] if t["diag"] else 0
                    t["c0"] = c0
                    qsl = slice(t["qb"] * 512 + c0, (t["qb"] + 1) * 512)
                    cs = slice(c0, 512)
                    pz = psZ.next()
                    P.mm(ps[pz][:, cs], kT[r0:r0 + 64, ksl], qT[r0:r0 + 64, qsl], True, True,
                         reads=[("kT", t["kb"] // 4), ("qT", t["qb"])], writes=[("ps", pz)])
                    si = spring.next()
                    t["si"] = si
                    P.act(spt[si][:, cs], ps[pz][:, cs], AF.Exp, reads=[("ps", pz)], writes=[("sp", si)])
                    P.act(spt[si][:, cs], spt[si][:, cs], AF.Ln, reads=[("sp", si)], writes=[("sp", si)], bias=1.0)
                    if t["diag"]:
                        P.tt("dve", spt[si][:, cs], spt[si][:, cs], masks[:, t["jm"], cs], ALU.mult,
                             reads=[("sp", si), "masks"], writes=[("sp", si)])

                def stB(t):
                    r0 = t["hd"] * 64
                    ksl = slice(t["kb"] * 128, (t["kb"] + 1) * 128)
                    c0 = t["c0"]
                    qsl = slice(t["qb"] * 512 + c0, (t["qb"] + 1) * 512)
                    cs = slice(c0, 512)
                    si = t["si"]
                    spsum_, spk_ = SPS[t["hd"]]
                    pe_ = psE.next()
                    P.mm(ps[pe_][:, cs], kT[r0:r0 + 64, ksl], qT[r0:r0 + 64, qsl], True, False,
                         reads=[("kT", t["kb"] // 4), ("qT", t["qb"])], writes=[("ps", pe_)])
                    P.mm(ps[pe_][:, cs], negtri_r[:], spt[si][:, cs], False, t["first"],
                         reads=[("sp", si), "cst"], writes=[("ps", pe_)])
                    if not t["first"]:
                        P.mm(ps[pe_][:, cs], negones_r[:], spsum_[:, cs], False, True,
                             reads=[spk_, "cst"], writes=[("ps", pe_)])
                    if not t["last"]:
                        if t["first"]:
                            if c0 > 0:
                                P.ts("dve", spsum_[:, 0:c0], masks[:, 0, 0:c0], 0.0, None, ALU.mult, None,
                                     reads=["masks"], writes=[spk_])
                            P.copy("dve", spsum_[:, cs], spt[si][:, cs], reads=[("sp", si)], writes=[spk_])
                        else:
                            P.tt("dve", spsum_[:, cs], spsum_[:, cs], spt[si][:, cs], ALU.add,
                                 reads=[("sp", si), spk_], writes=[spk_])
                    ai = aring.next()
                    t["ai"] = ai
                    if t["first"] and c0 > 0:
                        P.op("dve", lambda e, ai=ai, c0=c0: e.memset(aT[ai][:, 0:c0], 0.0), writes=[("aT", ai)])
                    P.act(aT[ai][:, cs], ps[pe_][:, cs], AF.Exp, reads=[("ps", pe_)], writes=[("aT", ai)])
                    if t["diag"]:
                        P.tt("dve", aT[ai][:, cs], aT[ai][:, cs], masks[:, t["jm"], cs], ALU.mult,
                             reads=[("aT", ai), "masks"], writes=[("aT", ai)])

                def stC(t):
                    r0 = t["hd"] * 64
                    qsl = slice(t["qb"] * 512, (t["qb"] + 1) * 512)
                    ai = t["ai"]
                    po = PSO[t["hd"]]
                    cs = slice(0, 512) if t["first"] else slice(t["c0"], 512)
                    P.mm(ps[po][:, cs], vS[:, t["kb"], :], aT[ai][:, cs], t["first"], t["last"],
                         reads=[("vS", t["kb"] // 4), ("aT", ai)], writes=[("ps", po)])
                    if t["last"]:
                        P.copy("dve", S[r0:r0 + 64, hp, qsl], ps[po][r0:r0 + 64, :],
                               reads=[("ps", po)], writes=[("S", hp, t["qb"])])

                nt = len(tiles)
                for idx in range(nt + 2):
                    if idx < nt:
                        stA(tiles[idx])
                    if 0 <= idx - 1 < nt:
                        stB(tiles[idx - 1])
                    if 0 <= idx - 2 < nt:
                        stC(tiles[idx - 2])
                run_hook(hp)
            out_proj(l, sb_w_out_d[jl], KC, 16, lambda kc, tb: ("S", kc, tb), lambda kc, tsl: S[:, kc, tsl])

        class _Stop(Exception):
            pass

        def stage(n):
            import os
            if float(os.environ.get("KDN", "99")) < n:
                raise _Stop()

        def dn_mixer(l):
            try:
                dn_mixer_(l)
            except _Stop:
                P.fence(lambda e: e.memset(dummy[0:1, 1:2], 0.0), ARENA_FAMS)

        def dn_mixer_(l):
            jl = l // 2
            w_d = dn_w_in_d[jl]
            K = lambda *a: ("dn",) + a
            tri_le = dncst[:, DN_TRI_LE:DN_TRI_LE + 64]
            tri_gt = dncst[:, DN_TRI_GT:DN_TRI_GT + 64]
            neg_strict = dncst[:, DN_NEG_STRICT:DN_NEG_STRICT + 64]
            neg_get = dncst[:, DN_NEG_GET:DN_NEG_GET + 64]
            ident64 = dncst[:, DN_IDENT64:DN_IDENT64 + 64]
            negones = dncst[:, DN_NEGONES:DN_NEGONES + 64]
            HS = [slice(0, 64), slice(64, 128)]
            P.fence(lambda e: e.memset(dummy[0:1, 0:1], 0.0), ARENA_FAMS)
            wabgrp.new_batch()
            P.dma("pool", wabgrp, wab[:], w_d[:, 4096:4112].rearrange("(k p) n -> p k n", p=128), writes=["wab"])
            for tt_ in range(16):
                for kc in range(KC):
                    P.mm(ps[PS_O][:, tt_ * 16:(tt_ + 1) * 16], hT[:, kc, tt_ * 128:(tt_ + 1) * 128], wab[:, kc, :],
                         kc == 0, kc == KC - 1, reads=["wab", ("h", kc, tt_ // 4)], writes=[("ps", PS_O)])
            P.copy("dve", d_ab, ps[PS_O][:, 0:256].rearrange("p (a b) -> p a b", a=16), reads=[("ps", PS_O)], writes=[K("ab")])
            P.act(d_nea, smalls[:, o_alog + jl * 8:o_alog + jl * 8 + 8], AF.Exp, reads=["smalls"], writes=[K("nea")])
            P.ts("dve", d_nea, d_nea, -1.0, None, ALU.mult, None, reads=[K("nea")], writes=[K("nea")])
            dtb_bc = smalls[:, o_dtb + jl * 8:o_dtb + jl * 8 + 8][:, None, :].broadcast_to([128, 16, 8])
            P.tt("dve", d_g, d_ab[:, :, 0:8], dtb_bc, ALU.add, reads=[K("ab"), "smalls"], writes=[K("g")])
            P.act(d_g, d_g, AF.Exp, reads=[K("g")], writes=[K("g")])
            P.act(d_g, d_g, AF.Ln, reads=[K("g")], writes=[K("g")], bias=1.0)
            P.tt("dve", d_g, d_g, d_nea[:, None, :].broadcast_to([128, 16, 8]), ALU.mult, reads=[K("g"), K("nea")], writes=[K("g")])
            P.act(d_beta, d_ab[:, :, 8:16], AF.Exp, reads=[K("ab")], writes=[K("beta")], scale=-1.0)
            P.ts("dve", d_beta, d_beta, 1.0, None, ALU.add, None, reads=[K("beta")], writes=[K("beta")])
            P.op("dve", lambda e: e.reciprocal(d_beta, d_beta), reads=[K("beta")], writes=[K("beta")])
            P.ts("dve", d_nbeta, d_beta, -1.0, None, ALU.mult, None, reads=[K("beta")], writes=[K("nbeta")])
            g2d = d_g.rearrange("p a b -> p (a b)")
            for par in range(2):
                hs = HS[par]
                P.mm(ps[PS_N][hs, 0:128], tri_le[hs, :], g2d[hs, :], True, True, reads=[K("g"), "smalls"], writes=[("ps", PS_N)])
                P.mm(ps[PS_N][hs, 128:256], tri_gt[hs, :], g2d[hs, :], True, True, reads=[K("g"), "smalls"], writes=[("ps", PS_N)])
            pgl = [PS_O, psP.next()]
            for par in range(2):
                P.mm(ps[pgl[par]][:, 0:128], onesf[HS[par], :], g2d[HS[par], :], True, True,
                     reads=[K("g"), "smalls"], writes=[("ps", pgl[par])])
            f2 = lambda v: v.rearrange("p a b -> p (a b)")
            P.act(f2(d_beg), ps[PS_N][:, 0:128], AF.Exp, reads=[("ps", PS_N)], writes=[K("beg")])
            P.tt("dve", f2(d_beg), f2(d_beg), f2(d_beta), ALU.mult, reads=[K("beg"), K("beta")], writes=[K("beg")])
            P.act(f2(d_et), ps[PS_N][:, 128:256], AF.Exp, reads=[("ps", PS_N)], writes=[K("et")])
            P.act(f2(d_egl0), ps[pgl[0]][:, 0:128], AF.Exp, reads=[("ps", pgl[0])], writes=[K("egl")])
            P.act(f2(d_egl1), ps[pgl[1]][:, 0:128], AF.Exp, reads=[("ps", pgl[1])], writes=[K("egl")])
            d_egl = [d_egl0, d_egl1]

            hinfo = {}

            def prep1(h):
                qb, qk = (qT, "qT") if h % 2 == 0 else (qTalt, "qTb")
                s = wload([(0, KC, 128, w_d[:, h * 128:(h + 1) * 128]),
                           (1024, KC, 128, w_d[:, D + h * 128:D + (h + 1) * 128]),
                           (2048, KC, 128, w_d[:, 2 * D + h * 128:2 * D + (h + 1) * 128]),
                           (3072, KC, 128, w_d[:, 3 * D + h * 128:3 * D + (h + 1) * 128])])
                wz = wview(s, 3072, KC, 128)
                for which in range(3):
                    ch = which * 8 + h
                    for tap in range(4):
                        col = smalls_conv[:, ((jl * 24 + ch) * 4 + tap):((jl * 24 + ch) * 4 + tap) + 1]
                        P.ts("dve", d_diag[:, which * 4 + tap, :], ident_bf[:], col, None, ALU.mult, None,
                             reads=["cst", "convw"], writes=[K("diag", which)])
                yield
                for which in range(3):
                    wmat = wview(s, which * 1024, KC, 128)
                    P.op("dve", lambda e: e.memset(d_pre[:, 0:3], 0.0), writes=[K("pre", 0)])
                    for tb in range(NTB):
                        tsl = slice(tb * 512, (tb + 1) * 512)
                        p = psP.next()
                        for kc in range(KC):
                            P.mm(ps[p][:], wmat[:, kc, :], hT[:, kc, tsl], kc == 0, kc == KC - 1,
                                 reads=[("w", s), ("h", kc, tb)], writes=[("ps", p)])
                            if kc == 3:
                                yield
                        P.copy("act", d_pre[:, 3 + tb * 512:3 + (tb + 1) * 512], ps[p][:], reads=[("ps", p)], writes=[K("pre", tb), K("pre", tb + 1)])
                        yield
                    for tb in range(NTB):
                        tsl = slice(tb * 512, (tb + 1) * 512)
                        pc = psZ.next()
                        for tap in range(4):
                            P.mm(ps[pc][:], d_diag[:, which * 4 + tap, :], d_pre[:, tb * 512 + tap:tb * 512 + tap + 512],
                                 tap == 0, tap == 3, reads=[K("diag", which), K("pre", tb), K("pre", tb + 1)], writes=[("ps", pc)])
                        yield
                        if which == 2:
                            P.act(d_vf[:, tsl], ps[pc][:], AF.Silu, reads=[("ps", pc)], writes=[K("vf", tb)])
                        else:
                            dst, dkey = (qb, qk) if which == 0 else (kT, "kT")
                            i = tmpring.next()
                            P.act(tmp[i][:], ps[pc][:], AF.Silu, reads=[("ps", pc)], writes=[("tmp", i)])
                            j = sqring.next()
                            P.act(sq[j][:], tmp[i][:], AF.Square, reads=[("tmp", i)], writes=[("sq", j)])
                            P.mm(ps[PS_N][:], ones_bf[:], sq[j][:], True, True, reads=[("sq", j), "cst"], writes=[("ps", PS_N)])
                            r = rstdring.next()
                            P.act(rstd[r][:], ps[PS_N][:], AF.Ln, reads=[("ps", PS_N)], writes=[("rstd", r)], bias=EPS, scale=1.0)
                            P.act(rstd[r][:], rstd[r][:], AF.Exp, reads=[("rstd", r)], writes=[("rstd", r)], scale=-0.5)
                            P.stt("dve", dst[:, tsl], tmp[i][:], (128.0 ** -0.5) if which == 0 else 1.0, rstd[r][:], ALU.mult, ALU.mult,
                                  reads=[("tmp", i), ("rstd", r)], writes=[(dkey, tb)])
                        yield
                hinfo[h] = (s, wz)

            def part2(h):
                qb, qk = (qT, "qT") if h % 2 == 0 else (qTalt, "qTb")
                for (src, skey, dst, dkey) in ((kT, "kT", d_ktok, "ktok"), (d_vf, None, vS, "vS")):
                    for t4 in range(4):
                        p = psP.next()
                        for jj in range(4):
                            tt_ = t4 * 4 + jj
                            rk = (skey, t4) if skey else K("vf", t4)
                            P.mm(ps[p][:, jj * 128:(jj + 1) * 128], src[:, tt_ * 128:(tt_ + 1) * 128], ident_bf[:], True, True,
                                 reads=[rk, "cst"], writes=[("ps", p)])
                        wk_ = K("ktok", t4) if dkey == "ktok" else ("vS", t4)
                        P.copy("act", dst[:, t4 * 4:(t4 + 1) * 4, :], ps[p][:].rearrange("p (j n) -> p j n", j=4),
                               reads=[("ps", p)], writes=[wk_])
                allk = [K("ktok", t4) for t4 in range(4)]
                allv = [("vS", t4) for t4 in range(4)]
                bc = lambda v: v[:, :, h:h + 1].broadcast_to([128, 16, 128])
                P.tt("dve", vS[:], vS[:], bc(d_beta), ALU.mult, reads=allv + [K("beta")], writes=allv)
                P.tt("dve", d_ktail, d_ktok, bc(d_et), ALU.mult, reads=allk + [K("et")], writes=[K("ktail")])
                P.tt("dve", d_ktok, d_ktok, bc(d_beg), ALU.mult, reads=allk + [K("beg")], writes=allk)
                P.tt("dve", d_Y, tri_le[:, None, :].broadcast_to([128, 16, 64]), d_g[:, :, h:h + 1].broadcast_to([128, 16, 64]),
                     ALU.mult, reads=[K("g"), "smalls"], writes=[K("Y")])
                def grp(gq):
                    Rg = rstd[1][:] if gq == 0 else rstd[0][:]
                    rk_ = ("rstd", 1) if gq == 0 else ("rstd", 0)
                    Rbg = aT[0][:].rearrange("p (a b) -> p a b", a=8) if gq == 0 else sq[0][:].rearrange("p (a b) -> p a b", a=8)
                    rbk_ = K("Rb") if gq == 0 else ("sq", 0)
                    def blocks():
                        for t8 in range(8):
                            for par in range(2):
                                tt_ = gq * 8 + t8
                                yield t8, tt_, HS[par], par, tt_ * 128 + par * 64, slice(t8 * 64, (t8 + 1) * 64)
                    kkeys = [("kT", gq * 2), ("kT", gq * 2 + 1)]
                    qkeys = [(qk, gq * 2), (qk, gq * 2 + 1)]
                    yield
                    pk = psZ.next()
                    for t8, tt_, hs, par, c0, cs in blocks():
                        P.mm(ps[pk][hs, cs], kT[:, c0:c0 + 64], kT[:, c0:c0 + 64], True, True, reads=kkeys, writes=[("ps", pk)])
                    yield
                    pd = psE.next()
                    for t8, tt_, hs, par, c0, cs in blocks():
                        P.mm(ps[pd][hs, cs], d_Y[hs, tt_, :], onesf[hs, 0:64], True, False, reads=[K("Y"), "smalls"], writes=[("ps", pd)])
                        P.mm(ps[pd][hs, cs], negones[hs, :], d_Y[hs, tt_, :], False, True, reads=[K("Y"), "smalls"], writes=[("ps", pd)])
                    i = tmpring.next()
                    v3 = lambda ap: ap.rearrange("p (a b) -> p a b", a=8)
                    P.tt("dve", v3(tmp[i][:]), v3(ps[pd][:]), neg_strict[:, None, :].broadcast_to([128, 8, 64]), ALU.add,
                         reads=[("ps", pd), "smalls"], writes=[("tmp", i)])
                    P.act(tmp[i][:], tmp[i][:], AF.Exp, reads=[("tmp", i)], writes=[("tmp", i)])
                    P.tt("dve", tmp[i][:], tmp[i][:], ps[pk][:], ALU.mult, reads=[("tmp", i), ("ps", pk)], writes=[("tmp", i)])
                    cur = nxt = gq
                    nb_bc = d_nbeta[:, gq * 8:(gq + 1) * 8, h:h + 1].broadcast_to([128, 8, 64])
                    P.tt("dve", d_XT[cur], v3(tmp[i][:]), nb_bc, ALU.mult, reads=[("tmp", i), K("nbeta")], writes=[K("XT", cur)])
                    yield
                    px = psZ.next()
                    for t8, tt_, hs, par, c0, cs in blocks():
                        P.mm(ps[px][hs, cs], d_XT[cur][hs, t8, :], ident_bf[hs, par * 64:par * 64 + 64], True, True,
                             reads=[K("XT", cur), "cst"], writes=[("ps", px)])
                    P.copy("act", d_X[cur], v3(ps[px][:]), reads=[("ps", px)], writes=[K("X", cur)])
                    P.tt("dve", v3(Rg), v3(ps[px][:]), ident64[:, None, :].broadcast_to([128, 8, 64]), ALU.add,
                         reads=[("ps", px), "smalls"], writes=[rk_])
                    P.copy("act", Rbg, v3(Rg), reads=[rk_], writes=[rbk_])
                    for lvl in range(1, 6):
                        if lvl < 5:
                            yield
                            p2 = psZ.next()
                            for t8, tt_, hs, par, c0, cs in blocks():
                                P.mm(ps[p2][hs, cs], d_XT[cur][hs, t8, :], d_X[cur][hs, t8, :], True, True,
                                     reads=[K("XT", cur), K("X", cur)], writes=[("ps", p2)])
                        yield
                        p2t = psE.next()
                        for t8, tt_, hs, par, c0, cs in blocks():
                            P.mm(ps[p2t][hs, cs], d_X[cur][hs, t8, :], d_XT[cur][hs, t8, :], True, True,
                                 reads=[K("XT", cur), K("X", cur)], writes=[("ps", p2t)])
                        if lvl < 5:
                            P.copy("act", d_X[nxt], v3(ps[p2][:]), reads=[("ps", p2)], writes=[K("X", nxt)])
                        P.copy("dve", d_XT[nxt], v3(ps[p2t][:]), reads=[("ps", p2t)], writes=[K("XT", nxt)])
                        yield
                        pr = psP.next()
                        for t8, tt_, hs, par, c0, cs in blocks():
                            P.mm(ps[pr][hs, cs], d_XT[nxt][hs, t8, :], Rbg[hs, t8, :], True, True,
                                 reads=[K("XT", nxt), rbk_], writes=[("ps", pr)])
                        P.tt("dve", Rg, Rg, ps[pr][:], ALU.add, reads=[rk_, ("ps", pr)], writes=[rk_])
                        if lvl < 5:
                            P.copy("act", Rbg, v3(Rg), reads=[rk_], writes=[rbk_])
                        else:
                            P.copy("act", d_TT[:, gq * 8:(gq + 1) * 8, :], v3(Rg), reads=[rk_], writes=[K("TT", gq)])
                    yield
                    pq = psZ.next()
                    for t8, tt_, hs, par, c0, cs in blocks():
                        P.mm(ps[pq][hs, cs], kT[:, c0:c0 + 64], qb[:, c0:c0 + 64], True, True, reads=kkeys + qkeys, writes=[("ps", pq)])
                    yield
                    pdt = psE.next()
                    for t8, tt_, hs, par, c0, cs in blocks():
                        P.mm(ps[pdt][hs, cs], onesf[hs, 0:64], d_Y[hs, tt_, :], True, False, reads=[K("Y"), "smalls"], writes=[("ps", pdt)])
                        P.mm(ps[pdt][hs, cs], d_Y[hs, tt_, :], negones[hs, :], False, True, reads=[K("Y"), "smalls"], writes=[("ps", pdt)])
                    i = tmpring.next()
                    P.tt("dve", v3(tmp[i][:]), v3(ps[pdt][:]), neg_get[:, None, :].broadcast_to([128, 8, 64]), ALU.add,
                         reads=[("ps", pdt), "smalls"], writes=[("tmp", i)])
                    P.act(tmp[i][:], tmp[i][:], AF.Exp, reads=[("tmp", i)], writes=[("tmp", i)])
                    P.tt("dve", d_attn[:, gq * 8:(gq + 1) * 8, :], v3(tmp[i][:]), v3(ps[pq][:]), ALU.mult,
                         reads=[("tmp", i), ("ps", pq)], writes=[K("attn", gq)])
                gens_ = [grp(0), grp(1)]
                while gens_:
                    for g_ in list(gens_):
                        try:
                            next(g_)
                        except StopIteration:
                            gens_.remove(g_)
                for tb in range(NTB):
                    tsl = slice(tb * 512, (tb + 1) * 512)
                    pab = [psE.next(), psE.next()]
                    for par in range(2):
                        for t4 in range(4):
                            tt_ = tb * 4 + t4
                            P.mm(ps[pab[par]][:, t4 * 64:(t4 + 1) * 64], onesf[HS[par], :], d_Y[HS[par], tt_, :], True, True,
                                 reads=[K("Y"), "smalls"], writes=[("ps", pab[par])])
                    i = tmpring.next()
                    tv = tmp[i][:].rearrange("p (t q i) -> p t q i", t=4, q=2)
                    for par in range(2):
                        P.act(tv[:, :, par, :], ps[pab[par]][:, 0:256].rearrange("p (t i) -> p t i", t=4), AF.Exp,
                              reads=[("ps", pab[par])], writes=[("tmp", i)])
                    P.tt("dve", qb[:, tsl], qb[:, tsl], tmp[i][:], ALU.mult, reads=[("tmp", i), (qk, tb)], writes=[(qk, tb)])
                    pwb = [psP.next(), psP.next()]
                    for par in range(2):
                        for t4 in range(4):
                            tt_ = tb * 4 + t4
                            P.mm(ps[pwb[par]][:, t4 * 64:(t4 + 1) * 64], d_ktok[HS[par], tt_, :], d_TT[HS[par], tt_, :], True, True,
                                 reads=[K("ktok", tb), K("TT", tb // 2)], writes=[("ps", pwb[par])])
                    nv = d_nwt[:, tsl].rearrange("p (t q i) -> p t q i", t=4, q=2)
                    for par in range(2):
                        P.act(nv[:, :, par, :], ps[pwb[par]][:, 0:256].rearrange("p (t i) -> p t i", t=4), AF.Identity,
                              reads=[("ps", pwb[par])], writes=[K("nwt", tb)], scale=-1.0)

            def scan(h, gen):
                qb, qk = (qT, "qT") if h % 2 == 0 else (qTalt, "qTb")
                s, wz = hinfo[h]

                def pull(n):
                    if gen is None:
                        return
                    for _ in range(n):
                        try:
                            next(gen)
                        except StopIteration:
                            return

                P.op("dve", lambda e: e.memset(d_S, 0.0), writes=[K("S")])
                P.op("dve", lambda e: e.memset(d_Sb, 0.0), writes=[K("Sb")])
                for n in range(32):
                    tt_, par = n // 2, n % 2
                    hs = HS[par]
                    c0 = n * 64
                    tb = n // 8
                    oc = slice((n % 8) * 64, (n % 8) * 64 + 64)
                    pv = psZ.next()
                    P.mm(ps[pv][hs, 0:128], d_TT[hs, tt_, :], vS[hs, tt_, :], True, False,
                         reads=[K("TT", tt_ // 8), ("vS", tt_ // 4)], writes=[("ps", pv)])
                    P.mm(ps[pv][hs, 0:128], d_nwt[:, c0:c0 + 64], d_Sb, False, True, reads=[K("nwt", tb), K("Sb")], writes=[("ps", pv)])
                    P.mm(ps[PS_O][:, oc], d_Sb, qb[:, c0:c0 + 64], True, False, reads=[K("Sb"), (qk, tb)], writes=[("ps", PS_O)])
                    P.copy("act", d_vnb[hs, :], ps[pv][hs, 0:128], reads=[("ps", pv)], writes=[K("vnb")])
                    pull(1)
                    P.mm(ps[PS_O][:, oc], d_vnb[hs, :], d_attn[hs, tt_, :], False, True, reads=[K("vnb"), K("attn", tt_ // 8)], writes=[("ps", PS_O)])
                    pS = psE.next()
                    P.mm(ps[pS][:, 0:128], d_ktail[hs, tt_, :], d_vnb[hs, :], True, True, reads=[K("ktail"), K("vnb")], writes=[("ps", pS)])
                    P.stt("dve", d_Sb, d_S, d_egl[par][:, tt_, h:h + 1], ps[pS][:, 0:128], ALU.mult, ALU.add,
                          reads=[K("S"), K("egl"), ("ps", pS)], writes=[K("Sb")])
                    P.stt("dve", d_S, d_S, d_egl[par][:, tt_, h:h + 1], ps[pS][:, 0:128], ALU.mult, ALU.add,
                          reads=[K("S"), K("egl"), ("ps", pS)], writes=[K("S")])
                    pull(1)
                    if n % 8 == 7:
                        tsl = slice(tb * 512, (tb + 1) * 512)
                        j = sqring.next()
                        P.act(sq[j][:], ps[PS_O][:], AF.Square, reads=[("ps", PS_O)], writes=[("sq", j)])
                        P.mm(ps[PS_N][:], ones_bf[:], sq[j][:], True, True, reads=[("sq", j), "cst"], writes=[("ps", PS_N)])
                        r = rstdring.next()
                        P.act(rstd[r][:], ps[PS_N][:], AF.Ln, reads=[("ps", PS_N)], writes=[("rstd", r)], bias=EPS, scale=1.0 / 128)
                        P.act(rstd[r][:], rstd[r][:], AF.Exp, reads=[("rstd", r)], writes=[("rstd", r)], scale=-0.5)
                        p = psP.next()
                        for kc in range(KC):
                            P.mm(ps[p][:], wz[:, kc, :], hT[:, kc, tsl], kc == 0, kc == KC - 1,
                                 reads=[("w", s), ("h", kc, tb)], writes=[("ps", p)])
                        i = tmpring.next()
                        P.act(tmp[i][:], ps[p][:], AF.Silu, reads=[("ps", p)], writes=[("tmp", i)])
                        i2 = tmpring.next()
                        P.stt("dve", tmp[i2][:], ps[PS_O][:], smalls[:, o_onorm + jl:o_onorm + jl + 1], rstd[r][:], ALU.mult, ALU.mult,
                              reads=[("ps", PS_O), ("rstd", r), "smalls"], writes=[("tmp", i2)])
                        P.tt("dve", d_oh[:, tsl], tmp[i2][:], tmp[i][:], ALU.mult, reads=[("tmp", i), ("tmp", i2)], writes=[K("oh", tb)])

            def outp(h):
                so = wload([(0, 1, 1024, dn_w_out_d[jl][h * 128:(h + 1) * 128, :])])
                wo = wview(so, 0, 1, 1024)
                for m in range(KC):
                    for tb in range(NTB):
                        tsl = slice(tb * 512, (tb + 1) * 512)
                        p = psP.next()
                        P.mm(ps[p][:], wo[:, 0, m * 128:(m + 1) * 128], d_oh[:, tsl], True, True,
                             reads=[("w", so), K("oh", tb)], writes=[("ps", p)])
                        P.stt("dve", xT[:, m, tsl], ps[p][:], modT[:, l, 16 + m:17 + m], xT[:, m, tsl], ALU.mult, ALU.add,
                              reads=[("ps", p), ("modT", l), ("x", m, tb)], writes=[("x", m, tb)])

            for _ in prep1(0):
                pass
            for h in range(8):
                part2(h)
                gen = prep1(h + 1) if h < 7 else None
                scan(h, gen)
                if gen is not None:
                    for _ in gen:
                        pass
                outp(h)
                run_hook(h)
            P.fence(lambda e: e.memset(dummy[0:1, 1:2], 0.0), ARENA_FAMS)

        import os
        DBG = os.environ.get("KDBG", "")
        first_layer = plan[0][0]
        for nb in range(0 if "noada" in DBG else 12):
            ada_block(first_layer, nb)
        for pi, (l, do_mix, do_ffn) in enumerate(plan):
            nxt = plan[pi + 1][0] if pi + 1 < len(plan) else None
            if do_mix:
                ada_hook[0] = nxt
                norm_mod(l, 0)
                if l % 2 == 0:
                    dn_mixer(l)
                else:
                    sb_mixer(l)
                ada_hook[0] = None
            if do_ffn:
                norm_mod(l, 1)
                ffn(l, None if do_mix else nxt)
            elif nxt is not None and not do_mix:
                for nb in range(12):
                    ada_block(nxt, nb)

        go = P.dma_group("st")
        for kc in range(KC):
            P.dma("sp", go, yT_d[kc * 128:(kc + 1) * 128, :], xT[:, kc, :],
                  reads=[("x", kc, tb) for tb in range(NTB)], writes=["yT"])
        P.op("sp", lambda e: e.nop(), reads=["yT"])
        P.emit()
    return nc


def _layout(inputs):
    f = lambda a: np.ascontiguousarray(np.asarray(a, dtype=np.float32))
    x = f(inputs["x"])
    c = f(inputs["c"])
    B = x.shape[0]
    shared = {
        "ada_w": f(inputs["ada_w"]),
        "ada_b": f(inputs["ada_b"]),
        "n1g": f(np.asarray(inputs["norm1_g"]).reshape(NL, KC, 128).transpose(2, 0, 1).reshape(128, NL * KC)),
        "n2g": f(np.asarray(inputs["norm2_g"]).reshape(NL, KC, 128).transpose(2, 0, 1).reshape(128, NL * KC)),
        "dn_w_in": f(inputs["dn_w_in"]),
        "dn_conv": f(np.asarray(inputs["dn_conv_w"]).reshape(2, 4, 24, 128).transpose(3, 0, 2, 1).reshape(128, 192)),
        "dn_alog": f(np.broadcast_to(np.asarray(inputs["dn_a_log"]).reshape(1, 16), (128, 16))),
        "dn_dtb": f(np.broadcast_to(np.asarray(inputs["dn_dt_bias"]).reshape(1, 16), (128, 16))),
        "dn_onorm": f(np.asarray(inputs["dn_onorm_g"]).T),
        "dn_w_out": f(inputs["dn_w_out"]),
        "sb_w_qkv": f(inputs["sb_w_qkv"]),
        "sb_qg": f(np.tile(np.asarray(inputs["sb_q_norm_g"]).T, (2, 1))),
        "sb_kg": f(np.tile(np.asarray(inputs["sb_k_norm_g"]).T, (2, 1))),
        "sb_w_out": f(inputs["sb_w_out"]),
        "ffn_w_in": f(inputs["ffn_w_in"]),
        "ffn_w_out": f(inputs["ffn_w_out"]),
        "cst": _consts(),
    }
    maps = []
    for b in range(B):
        m = dict(shared)
        m["xT"] = f(x[b].T)
        m["cT"] = f(c[b].reshape(KC, 128).T)
        maps.append(m)
    return maps


def kernel(**inputs):
    maps = _layout(inputs)
    nc = build()
    res = run_bass_kernel_spmd(nc, maps, core_ids=list(range(len(maps))))
    out = np.stack([np.ascontiguousarray(r["yT"].T) for r in res.results], axis=0)
    return out.astype(np.float32)
```

```python
import contextlib
import numpy as np
import concourse.bass as bass
import concourse.mybir as mybir
from concourse.bass_utils import run_bass_kernel_spmd

F32 = mybir.dt.float32
F32R = mybir.dt.float32r
BF16 = mybir.dt.bfloat16
F16 = mybir.dt.float16
AF = mybir.ActivationFunctionType
ALU = mybir.AluOpType

D = 1024
T = 2048
DFF = 2816
NL = 4
KC = 8
NTB = 4
EPS = 1e-6
NMOD = 48


class _Op:
    __slots__ = ("eng", "fn", "deps", "sig", "count", "dma", "batch", "is_mm")

    def __init__(self, eng, fn, is_mm=False):
        self.eng = eng
        self.fn = fn
        self.deps = []
        self.sig = False
        self.count = 0
        self.dma = None
        self.batch = 0
        self.is_mm = is_mm


class DmaGroup:
    def __init__(self, name):
        self.name = name
        self.ops = []
        self.batch = -1
        self.sem = None
        self.cum = {}

    def new_batch(self):
        self.batch += 1


class Prog:
    ENGS = ("pe", "act", "dve", "pool", "sp")

    def __init__(self, nc):
        self.nc = nc
        self.ops = {e: [] for e in self.ENGS}
        self.last_w = {}
        self.readers = {}
        self.groups = []

    def dma_group(self, name):
        g = DmaGroup(name)
        self.groups.append(g)
        return g

    def _track(self, op, reads, writes):
        psr = [r for r in reads if isinstance(r, tuple) and r[0] == "ps"]
        if psr:
            reads = [r for r in reads if not (isinstance(r, tuple) and r[0] == "ps")]
            writes = list(writes) + psr
        deps = {}
        for r in reads:
            w = self.last_w.get(r)
            if w is not None:
                deps[id(w)] = w
            self.readers.setdefault(r, []).append(op)
        for wkey in writes:
            w = self.last_w.get(wkey)
            if w is not None:
                deps[id(w)] = w
            for rd in self.readers.get(wkey, ()):
                if rd is not op:
                    deps[id(rd)] = rd
            self.readers[wkey] = []
            self.last_w[wkey] = op
        for d in deps.values():
            if d is op:
                continue
            if d.is_mm and op.is_mm:
                continue
            if d.dma is not None and op.dma is d.dma and d.batch == op.batch:
                continue
            d.sig = True
            op.deps.append(d)

    def op(self, eng, fn, reads=(), writes=(), is_mm=False):
        o = _Op(eng, fn, is_mm=is_mm)
        self._track(o, reads, writes)
        self.ops[eng].append(o)
        return o

    def dma(self, eng, grp, out, in_, reads=(), writes=()):
        o = _Op(eng, lambda e: e.dma_start(out=out, in_=in_))
        o.dma = grp
        if grp.batch < 0:
            grp.new_batch()
        o.batch = grp.batch
        grp.ops.append(o)
        self._track(o, reads, writes)
        self.ops[eng].append(o)
        return o

    def mm(self, out, lhsT, rhs, start, stop, reads, writes):
        return self.op("pe", lambda e: e.matmul(out, lhsT, rhs, start=start, stop=stop), reads, writes, is_mm=True)

    def tr(self, out, in_, ident, reads, writes):
        return self.op("pe", lambda e: e.transpose(out, in_, ident), reads, writes, is_mm=True)

    def act(self, out, in_, func, reads, writes, bias=0.0, scale=1.0):
        return self.op("act", lambda e: e.activation(out, in_, func, bias=bias, scale=scale), reads, writes)

    def tt(self, eng, out, in0, in1, op, reads, writes):
        return self.op(eng, lambda e: e.tensor_tensor(out, in0, in1, op), reads, writes)

    def stt(self, eng, out, in0, scalar, in1, op0, op1, reads, writes):
        return self.op(eng, lambda e: e.scalar_tensor_tensor(out, in0, scalar, in1, op0, op1), reads, writes)

    def ts(self, eng, out, in0, s1, s2, op0, op1, reads, writes):
        if s2 is None:
            return self.op(eng, lambda e: e.tensor_scalar(out, in0, s1, None, op0), reads, writes)
        return self.op(eng, lambda e: e.tensor_scalar(out, in0, s1, s2, op0, op1), reads, writes)

    def copy(self, eng, out, in_, reads, writes):
        if eng == "act":
            return self.op("act", lambda e: e.activation(out, in_, AF.Copy), reads, writes)
        return self.op(eng, lambda e: e.tensor_copy(out, in_), reads, writes)

    def fence(self, fn, fams):
        fam = lambda k: k if isinstance(k, str) else k[0]
        keys = [k for k in (set(self.last_w) | set(self.readers)) if fam(k) in fams]
        return self.op("dve", fn, reads=(), writes=keys)

    def emit(self):
        nc = self.nc
        for e in self.ENGS:
            c = 0
            for o in self.ops[e]:
                if o.dma is None and o.sig:
                    c += 1
                    o.count = c
        for g in self.groups:
            n = 0
            for o in g.ops:
                n += 1
                g.cum[o.batch] = n
        with contextlib.ExitStack() as st:
            esem = {e: st.enter_context(nc.semaphore("s_" + e)) for e in self.ENGS}
            for g in self.groups:
                g.sem = st.enter_context(nc.semaphore("d_" + g.name))
            block = st.enter_context(nc.Block())

            def run(ename, eng):
                waited = {}
                for o in self.ops[ename]:
                    for d in o.deps:
                        if d.dma is not None:
                            sem, val = d.dma.sem, 16 * d.dma.cum[d.batch]
                        else:
                            sem, val = esem[d.eng], d.count
                        k = id(sem)
                        if waited.get(k, 0) < val:
                            eng.wait_ge(sem, val)
                            waited[k] = val
                    ins = o.fn(eng)
                    if o.dma is not None:
                        ins.then_inc(o.dma.sem, 16)
                    elif o.sig:
                        ins.then_inc(esem[ename], 1)

            @block.tensor
            def _(eng):
                run("pe", eng)

            @block.scalar
            def _(eng):
                run("act", eng)

            @block.vector
            def _(eng):
                run("dve", eng)

            @block.gpsimd
            def _(eng):
                run("pool", eng)

            @block.sync
            def _(eng):
                run("sp", eng)


class Ring:
    def __init__(self, items):
        self.items = items
        self.i = -1

    def next(self):
        self.i = (self.i + 1) % len(self.items)
        return self.items[self.i]


C_ONES = 0
C_BLK = 128
C_IDENT = 256
C_NEGTRI = 384
C_NEGONES = 512
C_MASK = 640
C_DN = C_MASK + 4 * 512
DN_TRI_LE = 0
DN_TRI_GT = 64
DN_NEG_STRICT = 128
DN_NEG_GET = 192
DN_IDENT64 = 256
DN_NEGONES = 320
DN_NCOL = 384
C_TOTAL = C_DN + DN_NCOL


def _consts():
    c = np.zeros((128, C_TOTAL), np.float32)
    c[:, C_ONES:C_ONES + 128] = 1.0
    c[:64, C_BLK:C_BLK + 64] = 1.0
    c[64:, C_BLK + 64:C_BLK + 128] = 1.0
    c[:, C_IDENT:C_IDENT + 128] = np.eye(128, dtype=np.float32)
    j = np.arange(128)[:, None]
    s = np.arange(128)[None, :]
    c[:, C_NEGTRI:C_NEGTRI + 128] = -(j >= s).astype(np.float32)
    c[:, C_NEGONES:C_NEGONES + 128] = -1.0
    t = np.arange(512)[None, :]
    for jj in range(4):
        c[:, C_MASK + jj * 512:C_MASK + (jj + 1) * 512] = ((j + 128 * jj) < t).astype(np.float32)
    a = np.arange(64)[:, None]
    b = np.arange(64)[None, :]
    for half in range(2):
        r = slice(half * 64, half * 64 + 64)
        o = C_DN
        c[r, o + DN_TRI_LE:o + DN_TRI_LE + 64] = (a <= b)
        c[r, o + DN_TRI_GT:o + DN_TRI_GT + 64] = (a > b)
        c[r, o + DN_NEG_STRICT:o + DN_NEG_STRICT + 64] = ((a > b) - 1.0) * 3e4
        c[r, o + DN_NEG_GET:o + DN_NEG_GET + 64] = ((b >= a) - 1.0) * 3e4
        c[r, o + DN_IDENT64:o + DN_IDENT64 + 64] = (a == b)
        c[r, o + DN_NEGONES:o + DN_NEGONES + 64] = -1.0
    return c


def build(plan=None):
    if plan is None:
        plan = [(l, True, True) for l in range(NL)]
    nc = bass.Bass("TRN2", target_bir_lowering=False)
    dt_in = lambda n, s: nc.dram_tensor(n, s, F32, kind="ExternalInput").ap()
    xT_d = dt_in("xT", [D, T])
    cT_d = dt_in("cT", [128, KC])
    ada_w_d = dt_in("ada_w", [NL, D, 6 * D])
    ada_b_d = dt_in("ada_b", [NL, 6 * D])
    n1g_d = dt_in("n1g", [128, NL * KC])
    n2g_d = dt_in("n2g", [128, NL * KC])
    dn_w_in_d = dt_in("dn_w_in", [2, D, 4112])
    dn_conv_d = dt_in("dn_conv", [128, 2 * 24 * 4])
    dn_alog_d = dt_in("dn_alog", [128, 16])
    dn_dtb_d = dt_in("dn_dtb", [128, 16])
    dn_onorm_d = dt_in("dn_onorm", [128, 2])
    dn_w_out_d = dt_in("dn_w_out", [2, D, D])
    sb_w_qkv_d = dt_in("sb_w_qkv", [2, D, 3 * D])
    sb_qg_d = dt_in("sb_qg", [128, 2])
    sb_kg_d = dt_in("sb_kg", [128, 2])
    sb_w_out_d = dt_in("sb_w_out", [2, D, D])
    ffn_w_in_d = dt_in("ffn_w_in", [NL, D, 2 * DFF])
    ffn_w_out_d = dt_in("ffn_w_out", [NL, DFF, D])
    cst_d = dt_in("cst", [128, C_TOTAL])
    yT_d = nc.dram_tensor("yT", [D, T], F32, kind="ExternalOutput").ap()

    P = Prog(nc)
    with contextlib.ExitStack() as st:
        sb = lambda n, s, d: st.enter_context(nc.sbuf_tensor("sb_" + n, s, d))
        xT = sb("xT", [128, KC, T], F32)
        hT = sb("hT", [128, KC, T], BF16)
        S = sb("S", [128, KC, T], BF16)
        wslots = [sb(f"wslot{i}", [128, 4096], BF16) for i in range(3)]
        wgrp = [P.dma_group(f"w{i}") for i in range(3)]
        wring = Ring(list(range(3)))
        adast = [sb(f"adast{i}", [128, 512], F32R) for i in range(3)]
        adagrp = [P.dma_group(f"a{i}") for i in range(3)]
        adaring = Ring(list(range(3)))
        adab = sb("adab", [1, 512], F32)
        adabgrp = P.dma_group("adab")
        modrow = adab
        modT = sb("modT", [128, NL, NMOD], F32)
        AB = sb("AB", [128, NL, 2, KC], F32)
        smalls = sb("smalls", [128, 2 * NL * KC + 16 + 16 + 2 + 2 + 2 + KC], F32)
        o_n1 = 0
        o_n2 = NL * KC
        o_alog = 2 * NL * KC
        o_dtb = o_alog + 16
        o_onorm = o_dtb + 16
        o_qg = o_onorm + 2
        o_kg = o_qg + 2
        o_c = o_kg + 2
        smalls_conv = sb("convw", [128, 192], F32)
        condr = sb("condr", [128, KC], F32R)
        one11 = sb("one11", [1, 1], F32)
        qgs = sb("qgs", [128, 2], F32)
        ones_bf = sb("ones_bf", [128, 128], BF16)
        blk_bf = sb("blk_bf", [128, 128], BF16)
        ident_bf = sb("ident_bf", [128, 128], BF16)
        negtri_r = sb("negtri_r", [128, 128], F16)
        negones_r = sb("negones_r", [128, 128], F16)
        masks = sb("masks", [128, 4, 512], BF16)
        sq = [sb(f"sq{i}", [128, 512], BF16) for i in range(2)]
        sqring = Ring([0, 1])
        rstd = [sb(f"rstd{i}", [128, 512], F32) for i in range(2)]
        rstdring = Ring([0, 1])
        tmp = [sb(f"tmp{i}", [128, 512], F32) for i in range(2)]
        tmpring = Ring([0, 1])
        qT = sb("qT", [128, T], BF16)
        kT = sb("kT", [128, T], BF16)
        vS = sb("vS", [128, 16, 128], BF16)
        sp4 = sb("sp4", [128, 2048], F16)
        spt = [sp4[:, i * 512:(i + 1) * 512] for i in range(3)]
        spring = Ring([0, 1, 2])
        spsum = sp4[:, 1536:2048]
        spsumB = sb("spsumB", [128, 512], F16)[:]
        aT = [sb(f"aT{i}", [128, 512], BF16) for i in range(2)]
        aring = Ring([0, 1])

        wab = sb("wab", [128, KC, 16], BF16)
        wabgrp = P.dma_group("wab")
        dncst = sb("dncst", [128, DN_NCOL], F32)
        onesf = sb("onesf", [128, 128], F32)
        dummy = sb("dummy", [1, 8], F32)
        S2 = S[:].rearrange("p k t -> p (k t)")
        d_oh = S2[:, 0:2048]
        d_ktok = S2[:, 2176:4224].rearrange("p (a b) -> p a b", a=16)
        d_ktail = S2[:, 4224:6272].rearrange("p (a b) -> p a b", a=16)
        d_nwt = S2[:, 6272:8320]
        d_Y = S2[:, 8320:10368].bitcast(F32).rearrange("p (a b) -> p a b", a=16)
        d_attn = S2[:, 10368:11392].rearrange("p (a b) -> p a b", a=16)
        d_TT = S2[:, 11392:12416].rearrange("p (a b) -> p a b", a=16)
        d_diag = S2[:, 12416:13952].rearrange("p (a b) -> p a b", a=12)
        d_sc = S2[:, 13952:16384].bitcast(F32)
        d_ab = d_sc[:, 0:256].rearrange("p (a b) -> p a b", a=16)
        sc3 = lambda i: d_sc[:, 256 + i * 128:256 + (i + 1) * 128].rearrange("p (a b) -> p a b", a=16)
        d_g, d_beta, d_nbeta, d_beg, d_et, d_egl0, d_egl1 = [sc3(i) for i in range(7)]
        d_nea = d_sc[:, 1152:1160]
        dxbuf = sb("dxbuf", [128, 4, 512], BF16)
        d_pre = sb("pre2", [128, 2176], BF16)[:]
        d_vf = sp4[:].bitcast(BF16)
        qTalt = masks[:].rearrange("p j t -> p (j t)")
        d_X = [dxbuf[:, 0, :].rearrange("p (a b) -> p a b", a=8), dxbuf[:, 1, :].rearrange("p (a b) -> p a b", a=8)]
        d_XT = [dxbuf[:, 2, :].rearrange("p (a b) -> p a b", a=8), dxbuf[:, 3, :].rearrange("p (a b) -> p a b", a=8)]
        d_R = rstd[1][:]
        d_Rb = aT[0][:].rearrange("p (a b) -> p a b", a=8)
        d_S = aT[1][:, 0:256].bitcast(F32)
        d_Sb = aT[1][:, 256:384]
        d_vnb = aT[1][:, 384:512]
        ARENA_FAMS = ("S", "sp", "spsum", "aT", "dn", "masks", "qTb")

        ps = [st.enter_context(nc.psum_tensor(f"ps{i}", [128, 512], F32)) for i in range(8)]
        psP = Ring([0, 1])
        PS_N = 2
        psZ = Ring([3, 4])
        psE = Ring([5, 6])
        psZE = Ring([3, 4, 5, 6])
        PS_O = 7

        g0 = P.dma_group("ld0")
        for kc in range(KC):
            P.dma("sp", g0, xT[:, kc, :], xT_d[kc * 128:(kc + 1) * 128, :],
                  writes=[("x", kc, tb) for tb in range(NTB)])
        gs = P.dma_group("lds")
        P.dma("sp", gs, smalls[:, o_n1:o_n1 + NL * KC], n1g_d, writes=["smalls"])
        P.dma("sp", gs, smalls[:, o_n2:o_n2 + NL * KC], n2g_d, writes=["smalls"])
        P.dma("sp", gs, smalls[:, o_alog:o_alog + 16], dn_alog_d, writes=["smalls"])
        P.dma("sp", gs, smalls[:, o_dtb:o_dtb + 16], dn_dtb_d, writes=["smalls"])
        P.dma("sp", gs, smalls[:, o_onorm:o_onorm + 2], dn_onorm_d, writes=["smalls"])
        P.dma("sp", gs, smalls[:, o_qg:o_qg + 2], sb_qg_d, writes=["smalls"])
        P.dma("sp", gs, smalls[:, o_kg:o_kg + 2], sb_kg_d, writes=["smalls"])
        P.dma("sp", gs, smalls[:, o_c:o_c + KC], cT_d, writes=["smalls"])
        P.dma("sp", gs, smalls_conv[:], dn_conv_d, writes=["convw"])
        gc = P.dma_group("ldc")
        P.dma("pool", gc, ones_bf[:], cst_d[:, C_ONES:C_ONES + 128], writes=["cst"])
        P.dma("pool", gc, blk_bf[:], cst_d[:, C_BLK:C_BLK + 128], writes=["cst"])
        P.dma("pool", gc, ident_bf[:], cst_d[:, C_IDENT:C_IDENT + 128], writes=["cst"])
        P.dma("pool", gc, negtri_r[:], cst_d[:, C_NEGTRI:C_NEGTRI + 128], writes=["cst"])
        P.dma("pool", gc, negones_r[:], cst_d[:, C_NEGONES:C_NEGONES + 128], writes=["cst"])
        mgrp = P.dma_group("masks")
        P.dma("pool", mgrp, masks[:], cst_d[:, C_MASK:C_MASK + 2048].rearrange("p (j t) -> p j t", j=4), writes=["masks"])
        P.op("dve", lambda e: e.memset(dummy[0:1, 2:3], 0.0),
             writes=[("sp", 0), ("sp", 1), ("sp", 2), "spsum", ("aT", 0), ("aT", 1)]
             + [("S", c_, t_) for c_ in range(KC) for t_ in range(NTB)])
        P.op("dve", lambda e: e.memset(one11[:], 1.0), writes=["one11"])
        P.dma("sp", gs, dncst[:], cst_d[:, C_DN:C_DN + DN_NCOL], writes=["smalls"])
        P.dma("sp", gs, onesf[:], cst_d[:, C_ONES:C_ONES + 128], writes=["smalls"])
        P.act(condr[:], smalls[:, o_c:o_c + KC], AF.Silu, reads=["smalls"], writes=["condr"])
        P.ts("dve", qgs[:], smalls[:, o_qg:o_qg + 2], 0.125, None, ALU.mult, None, reads=["smalls"], writes=["qgs"])

        def wload(pieces):
            s = wring.next()
            wgrp[s].new_batch()
            for (off, kcs, ncols, src) in pieces:
                dst = wslots[s][:, off:off + kcs * ncols].rearrange("p (k n) -> p k n", k=kcs)
                P.dma("pool", wgrp[s], dst, src.rearrange("(k p) n -> p k n", p=128), writes=[("w", s)])
            return s

        def wview(s, off, kcs, ncols):
            return wslots[s][:, off:off + kcs * ncols].rearrange("p (k n) -> p k n", k=kcs)

        def ada_block(l, nb):
            pso = ps[PS_O]
            for kc in range(KC):
                a = adaring.next()
                adagrp[a].new_batch()
                P.dma("pool", adagrp[a], adast[a][:], ada_w_d[l, kc * 128:(kc + 1) * 128, nb * 512:(nb + 1) * 512],
                      writes=[("adast", a)])
                P.mm(pso[0:1, :], condr[:, kc:kc + 1], adast[a][:], kc == 0, kc == KC - 1,
                     reads=[("adast", a), "condr"], writes=[("ps", PS_O)])
            adabgrp.new_batch()
            P.dma("sp", adabgrp, adab[:], ada_b_d[l:l + 1, nb * 512:(nb + 1) * 512], writes=["adab"])
            P.tt("dve", modrow[:], pso[0:1, :], adab[:], ALU.add, reads=[("ps", PS_O), "adab"], writes=["adab"])
            for j in range(4):
                col = nb * 4 + j
                P.mm(ps[PS_N][:, col:col + 1], modrow[0:1, j * 128:(j + 1) * 128], one11[0:1, 0:1], True, True,
                     reads=["adab", "one11"], writes=[("ps", PS_N)])
            P.copy("dve", modT[:, l, nb * 4:(nb + 1) * 4], ps[PS_N][:, nb * 4:(nb + 1) * 4], reads=[("ps", PS_N)], writes=[("modT", l)])
            if nb == 11:
                P.stt("dve", AB[:, l, 0, :], modT[:, l, 8:16], 1.0, smalls[:, o_n1 + l * KC:o_n1 + (l + 1) * KC],
                      ALU.add, ALU.mult, reads=[("modT", l), "smalls"], writes=[("AB", l)])
                P.stt("dve", AB[:, l, 1, :], modT[:, l, 32:40], 1.0, smalls[:, o_n2 + l * KC:o_n2 + (l + 1) * KC],
                      ALU.add, ALU.mult, reads=[("modT", l), "smalls"], writes=[("AB", l)])

        def norm_mod(l, which):
            sh = 0 if which == 0 else 24
            for tb in range(NTB):
                tsl = slice(tb * 512, (tb + 1) * 512)
                for kc in range(KC):
                    i = sqring.next()
                    P.act(sq[i][:], xT[:, kc, tsl], AF.Square, reads=[("x", kc, tb)], writes=[("sq", i)])
                    P.mm(ps[PS_N][:], ones_bf[:], sq[i][:], kc == 0, kc == KC - 1,
                         reads=[("sq", i), "cst"], writes=[("ps", PS_N)])
                r = rstdring.next()
                P.act(rstd[r][:], ps[PS_N][:], AF.Ln, reads=[("ps", PS_N)], writes=[("rstd", r)], bias=EPS, scale=1.0 / D)
                P.act(rstd[r][:], rstd[r][:], AF.Exp, reads=[("rstd", r)], writes=[("rstd", r)], scale=-0.5)
                for kc in range(KC):
                    i = tmpring.next()
                    P.tt("dve", tmp[i][:], xT[:, kc, tsl], rstd[r][:], ALU.mult,
                         reads=[("x", kc, tb), ("rstd", r)], writes=[("tmp", i)])
                    P.act(hT[:, kc, tsl], tmp[i][:], AF.Identity, reads=[("tmp", i), ("AB", l), ("modT", l)],
                          writes=[("h", kc, tb)], bias=modT[:, l, sh + kc:sh + kc + 1], scale=AB[:, l, which, kc:kc + 1])

        def out_proj(l, w_d, nkc, gate_off, skeys_fn, s_view_fn):
            for mb in range(2 if nkc <= 8 else 4):
                ncols = 512 if nkc <= 8 else 256
                s = wload([(0, nkc, ncols, w_d[:, mb * ncols:(mb + 1) * ncols])])
                wv = wview(s, 0, nkc, ncols)
                for mc in range(ncols // 128):
                    m = mb * (ncols // 128) + mc
                    for tb in range(NTB):
                        tsl = slice(tb * 512, (tb + 1) * 512)
                        p = psP.next()
                        for kc in range(nkc):
                            P.mm(ps[p][:], wv[:, kc, mc * 128:(mc + 1) * 128], s_view_fn(kc, tsl), kc == 0, kc == nkc - 1,
                                 reads=[("w", s), skeys_fn(kc, tb)], writes=[("ps", p)])
                        P.stt("dve", xT[:, m, tsl], ps[p][:], modT[:, l, gate_off + m:gate_off + m + 1], xT[:, m, tsl],
                              ALU.mult, ALU.add, reads=[("ps", p), ("modT", l), ("x", m, tb)], writes=[("x", m, tb)])

        def ffn(l, ada_next):
            groups = [(0, 8), (8, 7), (15, 7)]
            for gi, (c0, ncg) in enumerate(groups):
                j = 0
                while j < ncg:
                    nsub = min(4, ncg - j)
                    cc = c0 + j
                    sg = wload([(0, KC, nsub * 128, ffn_w_in_d[l, :, cc * 128:(cc + nsub) * 128])])
                    su = wload([(0, KC, nsub * 128, ffn_w_in_d[l, :, DFF + cc * 128:DFF + (cc + nsub) * 128])])
                    wg = wview(sg, 0, KC, nsub * 128)
                    wu = wview(su, 0, KC, nsub * 128)
                    for q in range(nsub):
                        for tb in range(NTB):
                            tsl = slice(tb * 512, (tb + 1) * 512)
                            pg = psZ.next()
                            pu = psE.next()
                            for kc in range(KC):
                                P.mm(ps[pg][:], wg[:, kc, q * 128:(q + 1) * 128], hT[:, kc, tsl], kc == 0, kc == KC - 1,
                                     reads=[("w", sg), ("h", kc, tb)], writes=[("ps", pg)])
                            for kc in range(KC):
                                P.mm(ps[pu][:], wu[:, kc, q * 128:(q + 1) * 128], hT[:, kc, tsl], kc == 0, kc == KC - 1,
                                     reads=[("w", su), ("h", kc, tb)], writes=[("ps", pu)])
                            i = tmpring.next()
                            P.act(tmp[i][:], ps[pg][:], AF.Silu, reads=[("ps", pg)], writes=[("tmp", i)])
                            P.tt("dve", S[:, j + q, tsl], tmp[i][:], ps[pu][:], ALU.mult,
                                 reads=[("tmp", i), ("ps", pu)], writes=[("S", j + q, tb)])
                    j += nsub
                out_proj(l, ffn_w_out_d[l, c0 * 128:(c0 + ncg) * 128, :], ncg, 40,
                         lambda kc, tb: ("S", kc, tb), lambda kc, tsl: S[:, kc, tsl])
                if ada_next is not None:
                    for nb in range(gi * 4, gi * 4 + 4):
                        ada_block(ada_next, nb)

        ada_hook = [None]

        def run_hook(i):
            if ada_hook[0] is not None and 1 <= i <= 4:
                for nb in range((i - 1) * 3, (i - 1) * 3 + 3):
                    ada_block(ada_hook[0], nb)

        def sb_mixer(l):
            jl = l // 2
            w_d = sb_w_qkv_d[jl]
            mgrp.new_batch()
            P.dma("pool", mgrp, masks[:], cst_d[:, C_MASK:C_MASK + 2048].rearrange("p (j t) -> p j t", j=4), writes=["masks"])
            for hp in range(8):
                s = wload([(0, KC, 128, w_d[:, hp * 128:(hp + 1) * 128]),
                           (1024, KC, 128, w_d[:, D + hp * 128:D + (hp + 1) * 128]),
                           (2048, KC, 128, w_d[:, 2 * D + hp * 128:2 * D + (hp + 1) * 128])])
                wq = wview(s, 0, KC, 128)
                wk = wview(s, 1024, KC, 128)
                wv = wview(s, 2048, KC, 128)
                for (wmat, dst, dkey, gcol) in ((wq, qT, "qT", qgs[:, jl:jl + 1]),
                                                (wk, kT, "kT", smalls[:, o_kg + jl:o_kg + jl + 1])):
                    for tb in range(NTB):
                        tsl = slice(tb * 512, (tb + 1) * 512)
                        p = psP.next()
                        for kc in range(KC):
                            P.mm(ps[p][:], wmat[:, kc, :], hT[:, kc, tsl], kc == 0, kc == KC - 1,
                                 reads=[("w", s), ("h", kc, tb)], writes=[("ps", p)])
                        i = sqring.next()
                        P.act(sq[i][:], ps[p][:], AF.Square, reads=[("ps", p)], writes=[("sq", i)])
                        P.mm(ps[PS_N][:], blk_bf[:], sq[i][:], True, True, reads=[("sq", i), "cst"], writes=[("ps", PS_N)])
                        r = rstdring.next()
                        P.act(rstd[r][:], ps[PS_N][:], AF.Ln, reads=[("ps", PS_N)], writes=[("rstd", r)], bias=EPS, scale=1.0 / 64)
                        P.act(rstd[r][:], rstd[r][:], AF.Exp, reads=[("rstd", r)], writes=[("rstd", r)], scale=-0.5)
                        P.stt("dve", dst[:, tsl], ps[p][:], gcol, rstd[r][:], ALU.mult, ALU.mult,
                              reads=[("ps", p), ("rstd", r), "qgs", "smalls"], writes=[(dkey, tb)])
                for t4 in range(4):
                    p = psP.next()
                    for jj in range(4):
                        tt_ = t4 * 4 + jj
                        for kc in range(KC):
                            P.mm(ps[p][:, jj * 128:(jj + 1) * 128], hT[:, kc, tt_ * 128:(tt_ + 1) * 128], wv[:, kc, :],
                                 kc == 0, kc == KC - 1, reads=[("w", s), ("h", kc, t4)], writes=[("ps", p)])
                    P.copy("dve", vS[:, t4 * 4:(t4 + 1) * 4, :], ps[p][:].rearrange("p (j n) -> p j n", j=4),
                           reads=[("ps", p)], writes=[("vS", t4)])
                tiles = []
                for qb in range(NTB):
                    nkb = 4 * (qb + 1)
                    for kb in range(nkb - 1, -1, -1):
                        for hd in range(2):
                            tiles.append(dict(hd=hd, qb=qb, kb=kb, first=(kb == nkb - 1), last=(kb == 0),
                                              diag=(kb >= 4 * qb), jm=kb - 4 * qb))
                SPS = [(spsum, "spsum"), (spsumB, "spsumB")]
                PSO = [PS_O, 0]

                def stA(t):
                    r0 = t["hd"] * 64
                    ksl = slice(t["kb"] * 128, (t["kb"] + 1) * 128)
                    c0 = 128 * t["jm"] if t["diag"] else 0
                    t["c0"] = c0
                    qsl = slice(t["qb"] * 512 + c0, (t["qb"] + 1) * 512)
                    cs = slice(c0, 512)
                    pz = psZ.next()
                    P.mm(ps[pz][:, cs], kT[r0:r0 + 64, ksl], qT[r0:r0 + 64, qsl], True, True,
                         reads=[("kT", t["kb"] // 4), ("qT", t["qb"])], writes=[("ps", pz)])
                    si = spring.next()
                    t["si"] = si
                    P.act(spt[si][:, cs], ps[pz][:, cs], AF.Exp, reads=[("ps", pz)], writes=[("sp", si)])
                    P.act(spt[si][:, cs], spt[si][:, cs], AF.Ln, reads=[("sp", si)], writes=[("sp", si)], bias=1.0)
                    if t["diag"]:
                        P.tt("dve", spt[si][:, cs], spt[si][:, cs], masks[:, t["jm"], cs], ALU.mult,
                             reads=[("sp", si), "masks"], writes=[("sp", si)])

                def stB(t):
                    r0 = t["hd"] * 64
                    ksl = slice(t["kb"] * 128, (t["kb"] + 1) * 128)
                    c0 = t["c0"]
                    qsl = slice(t["qb"] * 512 + c0, (t["qb"] + 1) * 512)
                    cs = slice(c0, 512)
                    si = t["si"]
                    spsum_, spk_ = SPS[t["hd"]]
                    pe_ = psE.next()
                    P.mm(ps[pe_][:, cs], kT[r0:r0 + 64, ksl], qT[r0:r0 + 64, qsl], True, False,
                         reads=[("kT", t["kb"] // 4), ("qT", t["qb"])], writes=[("ps", pe_)])
                    P.mm(ps[pe_][:, cs], negtri_r[:], spt[si][:, cs], False, t["first"],
                         reads=[("sp", si), "cst"], writes=[("ps", pe_)])
                    if not t["first"]:
                        P.mm(ps[pe_][:, cs], negones_r[:], spsum_[:, cs], False, True,
                             reads=[spk_, "cst"], writes=[("ps", pe_)])
                    if not t["last"]:
                        if t["first"]:
                            if c0 > 0:
                                P.ts("dve", spsum_[:, 0:c0], masks[:, 0, 0:c0], 0.0, None, ALU.mult, None,
                                     reads=["masks"], writes=[spk_])
                            P.copy("dve", spsum_[:, cs], spt[si][:, cs], reads=[("sp", si)], writes=[spk_])
                        else:
                            P.tt("dve", spsum_[:, cs], spsum_[:, cs], spt[si][:, cs], ALU.add,
                                 reads=[("sp", si), spk_], writes=[spk_])
                    ai = aring.next()
                    t["ai"] = ai
                    if t["first"] and c0 > 0:
                        P.op("dve", lambda e, ai=ai, c0=c0: e.memset(aT[ai][:, 0:c0], 0.0), writes=[("aT", ai)])
                    P.act(aT[ai][:, cs], ps[pe_][:, cs], AF.Exp, reads=[("ps", pe_)], writes=[("aT", ai)])
                    if t["diag"]:
                        P.tt("dve", aT[ai][:, cs], aT[ai][:, cs], masks[:, t["jm"], cs], ALU.mult,
                             reads=[("aT", ai), "masks"], writes=[("aT", ai)])

                def stC(t):
                    r0 = t["hd"] * 64
                    qsl = slice(t["qb"] * 512, (t["qb"] + 1) * 512)
                    ai = t["ai"]
                    po = PSO[t["hd"]]
                    cs = slice(0, 512) if t["first"] else slice(t["c0"], 512)
                    P.mm(ps[po][:, cs], vS[:, t["kb"], :], aT[ai][:, cs], t["first"], t["last"],
                         reads=[("vS", t["kb"] // 4), ("aT", ai)], writes=[("ps", po)])
                    if t["last"]:
                        P.copy("dve", S[r0:r0 + 64, hp, qsl], ps[po][r0:r0 + 64, :],
                               reads=[("ps", po)], writes=[("S", hp, t["qb"])])

                nt = len(tiles)
                for idx in range(nt + 2):
                    if idx < nt:
                        stA(tiles[idx])
                    if 0 <= idx - 1 < nt:
                        stB(tiles[idx - 1])
                    if 0 <= idx - 2 < nt:
                        stC(tiles[idx - 2])
                run_hook(hp)
            out_proj(l, sb_w_out_d[jl], KC, 16, lambda kc, tb: ("S", kc, tb), lambda kc, tsl: S[:, kc, tsl])

        class _Stop(Exception):
            pass

        def stage(n):
            import os
            if float(os.environ.get("KDN", "99")) < n:
                raise _Stop()

        def dn_mixer(l):
            try:
                dn_mixer_(l)
            except _Stop:
                P.fence(lambda e: e.memset(dummy[0:1, 1:2], 0.0), ARENA_FAMS)

        def dn_mixer_(l):
            jl = l // 2
            w_d = dn_w_in_d[jl]
            K = lambda *a: ("dn",) + a
            tri_le = dncst[:, DN_TRI_LE:DN_TRI_LE + 64]
            tri_gt = dncst[:, DN_TRI_GT:DN_TRI_GT + 64]
            neg_strict = dncst[:, DN_NEG_STRICT:DN_NEG_STRICT + 64]
            neg_get = dncst[:, DN_NEG_GET:DN_NEG_GET + 64]
            ident64 = dncst[:, DN_IDENT64:DN_IDENT64 + 64]
            negones = dncst[:, DN_NEGONES:DN_NEGONES + 64]
            HS = [slice(0, 64), slice(64, 128)]
            P.fence(lambda e: e.memset(dummy[0:1, 0:1], 0.0), ARENA_FAMS)
            wabgrp.new_batch()
            P.dma("pool", wabgrp, wab[:], w_d[:, 4096:4112].rearrange("(k p) n -> p k n", p=128), writes=["wab"])
            for tt_ in range(16):
                for kc in range(KC):
                    P.mm(ps[PS_O][:, tt_ * 16:(tt_ + 1) * 16], hT[:, kc, tt_ * 128:(tt_ + 1) * 128], wab[:, kc, :],
                         kc == 0, kc == KC - 1, reads=["wab", ("h", kc, tt_ // 4)], writes=[("ps", PS_O)])
            P.copy("dve", d_ab, ps[PS_O][:, 0:256].rearrange("p (a b) -> p a b", a=16), reads=[("ps", PS_O)], writes=[K("ab")])
            P.act(d_nea, smalls[:, o_alog + jl * 8:o_alog + jl * 8 + 8], AF.Exp, reads=["smalls"], writes=[K("nea")])
            P.ts("dve", d_nea, d_nea, -1.0, None, ALU.mult, None, reads=[K("nea")], writes=[K("nea")])
            dtb_bc = smalls[:, o_dtb + jl * 8:o_dtb + jl * 8 + 8][:, None, :].broadcast_to([128, 16, 8])
            P.tt("dve", d_g, d_ab[:, :, 0:8], dtb_bc, ALU.add, reads=[K("ab"), "smalls"], writes=[K("g")])
            P.act(d_g, d_g, AF.Exp, reads=[K("g")], writes=[K("g")])
            P.act(d_g, d_g, AF.Ln, reads=[K("g")], writes=[K("g")], bias=1.0)
            P.tt("dve", d_g, d_g, d_nea[:, None, :].broadcast_to([128, 16, 8]), ALU.mult, reads=[K("g"), K("nea")], writes=[K("g")])
            P.act(d_beta, d_ab[:, :, 8:16], AF.Exp, reads=[K("ab")], writes=[K("beta")], scale=-1.0)
            P.ts("dve", d_beta, d_beta, 1.0, None, ALU.add, None, reads=[K("beta")], writes=[K("beta")])
            P.op("dve", lambda e: e.reciprocal(d_beta, d_beta), reads=[K("beta")], writes=[K("beta")])
            P.ts("dve", d_nbeta, d_beta, -1.0, None, ALU.mult, None, reads=[K("beta")], writes=[K("nbeta")])
            g2d = d_g.rearrange("p a b -> p (a b)")
            for par in range(2):
                hs = HS[par]
                P.mm(ps[PS_N][hs, 0:128], tri_le[hs, :], g2d[hs, :], True, True, reads=[K("g"), "smalls"], writes=[("ps", PS_N)])
                P.mm(ps[PS_N][hs, 128:256], tri_gt[hs, :], g2d[hs, :], True, True, reads=[K("g"), "smalls"], writes=[("ps", PS_N)])
            pgl = [PS_O, psP.next()]
            for par in range(2):
                P.mm(ps[pgl[par]][:, 0:128], onesf[HS[par], :], g2d[HS[par], :], True, True,
                     reads=[K("g"), "smalls"], writes=[("ps", pgl[par])])
            f2 = lambda v: v.rearrange("p a b -> p (a b)")
            P.act(f2(d_beg), ps[PS_N][:, 0:128], AF.Exp, reads=[("ps", PS_N)], writes=[K("beg")])
            P.tt("dve", f2(d_beg), f2(d_beg), f2(d_beta), ALU.mult, reads=[K("beg"), K("beta")], writes=[K("beg")])
            P.act(f2(d_et), ps[PS_N][:, 128:256], AF.Exp, reads=[("ps", PS_N)], writes=[K("et")])
            P.act(f2(d_egl0), ps[pgl[0]][:, 0:128], AF.Exp, reads=[("ps", pgl[0])], writes=[K("egl")])
            P.act(f2(d_egl1), ps[pgl[1]][:, 0:128], AF.Exp, reads=[("ps", pgl[1])], writes=[K("egl")])
            d_egl = [d_egl0, d_egl1]

            hinfo = {}

            def prep1(h):
                qb, qk = (qT, "qT") if h % 2 == 0 else (qTalt, "qTb")
                s = wload([(0, KC, 128, w_d[:, h * 128:(h + 1) * 128]),
                           (1024, KC, 128, w_d[:, D + h * 128:D + (h + 1) * 128]),
                           (2048, KC, 128, w_d[:, 2 * D + h * 128:2 * D + (h + 1) * 128]),
                           (3072, KC, 128, w_d[:, 3 * D + h * 128:3 * D + (h + 1) * 128])])
                wz = wview(s, 3072, KC, 128)
                for which in range(3):
                    ch = which * 8 + h
                    for tap in range(4):
                        col = smalls_conv[:, ((jl * 24 + ch) * 4 + tap):((jl * 24 + ch) * 4 + tap) + 1]
                        P.ts("dve", d_diag[:, which * 4 + tap, :], ident_bf[:], col, None, ALU.mult, None,
                             reads=["cst", "convw"], writes=[K("diag", which)])
                yield
                for which in range(3):
                    wmat = wview(s, which * 1024, KC, 128)
                    P.op("dve", lambda e: e.memset(d_pre[:, 0:3], 0.0), writes=[K("pre", 0)])
                    for tb in range(NTB):
                        tsl = slice(tb * 512, (tb + 1) * 512)
                        p = psP.next()
                        for kc in range(KC):
                            P.mm(ps[p][:], wmat[:, kc, :], hT[:, kc, tsl], kc == 0, kc == KC - 1,
                                 reads=[("w", s), ("h", kc, tb)], writes=[("ps", p)])
                            if kc == 3:
                                yield
                        P.copy("act", d_pre[:, 3 + tb * 512:3 + (tb + 1) * 512], ps[p][:], reads=[("ps", p)], writes=[K("pre", tb), K("pre", tb + 1)])
                        yield
                    for tb in range(NTB):
                        tsl = slice(tb * 512, (tb + 1) * 512)
                        pc = psZ.next()
                        for tap in range(4):
                            P.mm(ps[pc][:], d_diag[:, which * 4 + tap, :], d_pre[:, tb * 512 + tap:tb * 512 + tap + 512],
                                 tap == 0, tap == 3, reads=[K("diag", which), K("pre", tb), K("pre", tb + 1)], writes=[("ps", pc)])
                        yield
                        if which == 2:
                            P.act(d_vf[:, tsl], ps[pc][:], AF.Silu, reads=[("ps", pc)], writes=[K("vf", tb)])
                        else:
                            dst, dkey = (qb, qk) if which == 0 else (kT, "kT")
                            i = tmpring.next()
                            P.act(tmp[i][:], ps[pc][:], AF.Silu, reads=[("ps", pc)], writes=[("tmp", i)])
                            j = sqring.next()
                            P.act(sq[j][:], tmp[i][:], AF.Square, reads=[("tmp", i)], writes=[("sq", j)])
                            P.mm(ps[PS_N][:], ones_bf[:], sq[j][:], True, True, reads=[("sq", j), "cst"], writes=[("ps", PS_N)])
                            r = rstdring.next()
                            P.act(rstd[r][:], ps[PS_N][:], AF.Ln, reads=[("ps", PS_N)], writes=[("rstd", r)], bias=EPS, scale=1.0)
                            P.act(rstd[r][:], rstd[r][:], AF.Exp, reads=[("rstd", r)], writes=[("rstd", r)], scale=-0.5)
                            P.stt("dve", dst[:, tsl], tmp[i][:], (128.0 ** -0.5) if which == 0 else 1.0, rstd[r][:], ALU.mult, ALU.mult,
                                  reads=[("tmp", i), ("rstd", r)], writes=[(dkey, tb)])
                        yield
                hinfo[h] = (s, wz)

            def part2(h):
                qb, qk = (qT, "qT") if h % 2 == 0 else (qTalt, "qTb")
                for (src, skey, dst, dkey) in ((kT, "kT", d_ktok, "ktok"), (d_vf, None, vS, "vS")):
                    for t4 in range(4):
                        p = psP.next()
                        for jj in range(4):
                            tt_ = t4 * 4 + jj
                            rk = (skey, t4) if skey else K("vf", t4)
                            P.mm(ps[p][:, jj * 128:(jj + 1) * 128], src[:, tt_ * 128:(tt_ + 1) * 128], ident_bf[:], True, True,
                                 reads=[rk, "cst"], writes=[("ps", p)])
                        wk_ = K("ktok", t4) if dkey == "ktok" else ("vS", t4)
                        P.copy("act", dst[:, t4 * 4:(t4 + 1) * 4, :], ps[p][:].rearrange("p (j n) -> p j n", j=4),
                               reads=[("ps", p)], writes=[wk_])
                allk = [K("ktok", t4) for t4 in range(4)]
                allv = [("vS", t4) for t4 in range(4)]
                bc = lambda v: v[:, :, h:h + 1].broadcast_to([128, 16, 128])
                P.tt("dve", vS[:], vS[:], bc(d_beta), ALU.mult, reads=allv + [K("beta")], writes=allv)
                P.tt("dve", d_ktail, d_ktok, bc(d_et), ALU.mult, reads=allk + [K("et")], writes=[K("ktail")])
                P.tt("dve", d_ktok, d_ktok, bc(d_beg), ALU.mult, reads=allk + [K("beg")], writes=allk)
                P.tt("dve", d_Y, tri_le[:, None, :].broadcast_to([128, 16, 64]), d_g[:, :, h:h + 1].broadcast_to([128, 16, 64]),
                     ALU.mult, reads=[K("g"), "smalls"], writes=[K("Y")])
                def grp(gq):
                    Rg = rstd[1][:] if gq == 0 else rstd[0][:]
                    rk_ = ("rstd", 1) if gq == 0 else ("rstd", 0)
                    Rbg = aT[0][:].rearrange("p (a b) -> p a b", a=8) if gq == 0 else sq[0][:].rearrange("p (a b) -> p a b", a=8)
                    rbk_ = K("Rb") if gq == 0 else ("sq", 0)
                    def blocks():
                        for t8 in range(8):
                            for par in range(2):
                                tt_ = gq * 8 + t8
                                yield t8, tt_, HS[par], par, tt_ * 128 + par * 64, slice(t8 * 64, (t8 + 1) * 64)
                    kkeys = [("kT", gq * 2), ("kT", gq * 2 + 1)]
                    qkeys = [(qk, gq * 2), (qk, gq * 2 + 1)]
                    yield
                    pk = psZ.next()
                    for t8, tt_, hs, par, c0, cs in blocks():
                        P.mm(ps[pk][hs, cs], kT[:, c0:c0 + 64], kT[:, c0:c0 + 64], True, True, reads=kkeys, writes=[("ps", pk)])
                    yield
                    pd = psE.next()
                    for t8, tt_, hs, par, c0, cs in blocks():
                        P.mm(ps[pd][hs, cs], d_Y[hs, tt_, :], onesf[hs, 0:64], True, False, reads=[K("Y"), "smalls"], writes=[("ps", pd)])
                        P.mm(ps[pd][hs, cs], negones[hs, :], d_Y[hs, tt_, :], False, True, reads=[K("Y"), "smalls"], writes=[("ps", pd)])
                    i = tmpring.next()
                    v3 = lambda ap: ap.rearrange("p (a b) -> p a b", a=8)
                    P.tt("dve", v3(tmp[i][:]), v3(ps[pd][:]), neg_strict[:, None, :].broadcast_to([128, 8, 64]), ALU.add,
                         reads=[("ps", pd), "smalls"], writes=[("tmp", i)])
                    P.act(tmp[i][:], tmp[i][:], AF.Exp, reads=[("tmp", i)], writes=[("tmp", i)])
                    P.tt("dve", tmp[i][:], tmp[i][:], ps[pk][:], ALU.mult, reads=[("tmp", i), ("ps", pk)], writes=[("tmp", i)])
                    cur = nxt = gq
                    nb_bc = d_nbeta[:, gq * 8:(gq + 1) * 8, h:h + 1].broadcast_to([128, 8, 64])
                    P.tt("dve", d_XT[cur], v3(tmp[i][:]), nb_bc, ALU.mult, reads=[("tmp", i), K("nbeta")], writes=[K("XT", cur)])
                    yield
                    px = psZ.next()
                    for t8, tt_, hs, par, c0, cs in blocks():
                        P.mm(ps[px][hs, cs], d_XT[cur][hs, t8, :], ident_bf[hs, par * 64:par * 64 + 64], True, True,
                             reads=[K("XT", cur), "cst"], writes=[("ps", px)])
                    P.copy("act", d_X[cur], v3(ps[px][:]), reads=[("ps", px)], writes=[K("X", cur)])
                    P.tt("dve", v3(Rg), v3(ps[px][:]), ident64[:, None, :].broadcast_to([128, 8, 64]), ALU.add,
                         reads=[("ps", px), "smalls"], writes=[rk_])
                    P.copy("act", Rbg, v3(Rg), reads=[rk_], writes=[rbk_])
                    for lvl in range(1, 6):
                        if lvl < 5:
                            yield
                            p2 = psZ.next()
                            for t8, tt_, hs, par, c0, cs in blocks():
                                P.mm(ps[p2][hs, cs], d_XT[cur][hs, t8, :], d_X[cur][hs, t8, :], True, True,
                                     reads=[K("XT", cur), K("X", cur)], writes=[("ps", p2)])
                        yield
                        p2t = psE.next()
                        for t8, tt_, hs, par, c0, cs in blocks():
                            P.mm(ps[p2t][hs, cs], d_X[cur][hs, t8, :], d_XT[cur][hs, t8, :], True, True,
                                 reads=[K("XT", cur), K("X", cur)], writes=[("ps", p2t)])
                        if lvl < 5:
                            P.copy("act", d_X[nxt], v3(ps[p2][:]), reads=[("ps", p2)], writes=[K("X", nxt)])
                        P.copy("dve", d_XT[nxt], v3(ps[p2t][:]), reads=[("ps", p2t)], writes=[K("XT", nxt)])
                        yield
                        pr = psP.next()
                        for t8, tt_, hs, par, c0, cs in blocks():
                            P.mm(ps[pr][hs, cs], d_XT[nxt][hs, t8, :], Rbg[hs, t8, :], True, True,
                                 reads=[K("XT", nxt), rbk_], writes=[("ps", pr)])
                        P.tt("dve", Rg, Rg, ps[pr][:], ALU.add, reads=[rk_, ("ps", pr)], writes=[rk_])
                        if lvl < 5:
                            P.copy("act", Rbg, v3(Rg), reads=[rk_], writes=[rbk_])
                        else:
                            P.copy("act", d_TT[:, gq * 8:(gq + 1) * 8, :], v3(Rg), reads=[rk_], writes=[K("TT", gq)])
                    yield
                    pq = psZ.next()
                    for t8, tt_, hs, par, c0, cs in blocks():
                        P.mm(ps[pq][hs, cs], kT[:, c0:c0 + 64], qb[:, c0:c0 + 64], True, True, reads=kkeys + qkeys, writes=[("ps", pq)])
                    yield
                    pdt = psE.next()
                    for t8, tt_, hs, par, c0, cs in blocks():
                        P.mm(ps[pdt][hs, cs], onesf[hs, 0:64], d_Y[hs, tt_, :], True, False, reads=[K("Y"), "smalls"], writes=[("ps", pdt)])
                        P.mm(ps[pdt][hs, cs], d_Y[hs, tt_, :], negones[hs, :], False, True, reads=[K("Y"), "smalls"], writes=[("ps", pdt)])
                    i = tmpring.next()
                    P.tt("dve", v3(tmp[i][:]), v3(ps[pdt][:]), neg_get[:, None, :].broadcast_to([128, 8, 64]), ALU.add,
                         reads=[("ps", pdt), "smalls"], writes=[("tmp", i)])
                    P.act(tmp[i][:], tmp[i][:], AF.Exp, reads=[("tmp", i)], writes=[("tmp", i)])
                    P.tt("dve", d_attn[:, gq * 8:(gq + 1) * 8, :], v3(tmp[i][:]), v3(ps[pq][:]), ALU.mult,
                         reads=[("tmp", i), ("ps", pq)], writes=[K("attn", gq)])
                gens_ = [grp(0), grp(1)]
                while gens_:
                    for g_ in list(gens_):
                        try:
                            next(g_)
                        except StopIteration:
                            gens_.remove(g_)
                for tb in range(NTB):
                    tsl = slice(tb * 512, (tb + 1) * 512)
                    pab = [psE.next(), psE.next()]
                    for par in range(2):
                        for t4 in range(4):
                            tt_ = tb * 4 + t4
                            P.mm(ps[pab[par]][:, t4 * 64:(t4 + 1) * 64], onesf[HS[par], :], d_Y[HS[par], tt_, :], True, True,
                                 reads=[K("Y"), "smalls"], writes=[("ps", pab[par])])
                    i = tmpring.next()
                    tv = tmp[i][:].rearrange("p (t q i) -> p t q i", t=4, q=2)
                    for par in range(2):
                        P.act(tv[:, :, par, :], ps[pab[par]][:, 0:256].rearrange("p (t i) -> p t i", t=4), AF.Exp,
                              reads=[("ps", pab[par])], writes=[("tmp", i)])
                    P.tt("dve", qb[:, tsl], qb[:, tsl], tmp[i][:], ALU.mult, reads=[("tmp", i), (qk, tb)], writes=[(qk, tb)])
                    pwb = [psP.next(), psP.next()]
                    for par in range(2):
                        for t4 in range(4):
                            tt_ = tb * 4 + t4
                            P.mm(ps[pwb[par]][:, t4 * 64:(t4 + 1) * 64], d_ktok[HS[par], tt_, :], d_TT[HS[par], tt_, :], True, True,
                                 reads=[K("ktok", tb), K("TT", tb // 2)], writes=[("ps", pwb[par])])
                    nv = d_nwt[:, tsl].rearrange("p (t q i) -> p t q i", t=4, q=2)
                    for par in range(2):
                        P.act(nv[:, :, par, :], ps[pwb[par]][:, 0:256].rearrange("p (t i) -> p t i", t=4), AF.Identity,
                              reads=[("ps", pwb[par])], writes=[K("nwt", tb)], scale=-1.0)

            def scan(h, gen):
                qb, qk = (qT, "qT") if h % 2 == 0 else (qTalt, "qTb")
                s, wz = hinfo[h]

                def pull(n):
                    if gen is None:
                        return
                    for _ in range(n):
                        try:
                            next(gen)
                        except StopIteration:
                            return

                P.op("dve", lambda e: e.memset(d_S, 0.0), writes=[K("S")])
                P.op("dve", lambda e: e.memset(d_Sb, 0.0), writes=[K("Sb")])
                for n in range(32):
                    tt_, par = n // 2, n % 2
                    hs = HS[par]
                    c0 = n * 64
                    tb = n // 8
                    oc = slice((n % 8) * 64, (n % 8) * 64 + 64)
                    pv = psZ.next()
                    P.mm(ps[pv][hs, 0:128], d_TT[hs, tt_, :], vS[hs, tt_, :], True, False,
                         reads=[K("TT", tt_ // 8), ("vS", tt_ // 4)], writes=[("ps", pv)])
                    P.mm(ps[pv][hs, 0:128], d_nwt[:, c0:c0 + 64], d_Sb, False, True, reads=[K("nwt", tb), K("Sb")], writes=[("ps", pv)])
                    P.mm(ps[PS_O][:, oc], d_Sb, qb[:, c0:c0 + 64], True, False, reads=[K("Sb"), (qk, tb)], writes=[("ps", PS_O)])
                    P.copy("dve", d_vnb[hs, :], ps[pv][hs, 0:128], reads=[("ps", pv)], writes=[K("vnb")])
                    pull(1)
                    P.mm(ps[PS_O][:, oc], d_vnb[hs, :], d_attn[hs, tt_, :], False, True, reads=[K("vnb"), K("attn", tt_ // 8)], writes=[("ps", PS_O)])
                    pS = psE.next()
                    P.mm(ps[pS][:, 0:128], d_ktail[hs, tt_, :], d_vnb[hs, :], True, True, reads=[K("ktail"), K("vnb")], writes=[("ps", pS)])
                    P.stt("dve", d_Sb, d_S, d_egl[par][:, tt_, h:h + 1], ps[pS][:, 0:128], ALU.mult, ALU.add,
                          reads=[K("S"), K("egl"), ("ps", pS)], writes=[K("Sb")])
                    P.stt("dve", d_S, d_S, d_egl[par][:, tt_, h:h + 1], ps[pS][:, 0:128], ALU.mult, ALU.add,
                          reads=[K("S"), K("egl"), ("ps", pS)], writes=[K("S")])
                    pull(1)
                    if n % 8 == 7:
                        tsl = slice(tb * 512, (tb + 1) * 512)
                        j = sqring.next()
                        P.act(sq[j][:], ps[PS_O][:], AF.Square, reads=[("ps", PS_O)], writes=[("sq", j)])
                        P.mm(ps[PS_N][:], ones_bf[:], sq[j][:], True, True, reads=[("sq", j), "cst"], writes=[("ps", PS_N)])
                        r = rstdring.next()
                        P.act(rstd[r][:], ps[PS_N][:], AF.Ln, reads=[("ps", PS_N)], writes=[("rstd", r)], bias=EPS, scale=1.0 / 128)
                        P.act(rstd[r][:], rstd[r][:], AF.Exp, reads=[("rstd", r)], writes=[("rstd", r)], scale=-0.5)
                        p = psP.next()
                        for kc in range(KC):
                            P.mm(ps[p][:], wz[:, kc, :], hT[:, kc, tsl], kc == 0, kc == KC - 1,
                                 reads=[("w", s), ("h", kc, tb)], writes=[("ps", p)])
                        i = tmpring.next()
                        P.act(tmp[i][:], ps[p][:], AF.Silu, reads=[("ps", p)], writes=[("tmp", i)])
                        i2 = tmpring.next()
                        P.stt("dve", tmp[i2][:], ps[PS_O][:], smalls[:, o_onorm + jl:o_onorm + jl + 1], rstd[r][:], ALU.mult, ALU.mult,
                              reads=[("ps", PS_O), ("rstd", r), "smalls"], writes=[("tmp", i2)])
                        P.tt("dve", d_oh[:, tsl], tmp[i2][:], tmp[i][:], ALU.mult, reads=[("tmp", i), ("tmp", i2)], writes=[K("oh", tb)])

            def outp(h):
                so = wload([(0, 1, 1024, dn_w_out_d[jl][h * 128:(h + 1) * 128, :])])
                wo = wview(so, 0, 1, 1024)
                for m in range(KC):
                    for tb in range(NTB):
                        tsl = slice(tb * 512, (tb + 1) * 512)
                        p = psP.next()
                        P.mm(ps[p][:], wo[:, 0, m * 128:(m + 1) * 128], d_oh[:, tsl], True, True,
                             reads=[("w", so), K("oh", tb)], writes=[("ps", p)])
                        P.stt("dve", xT[:, m, tsl], ps[p][:], modT[:, l, 16 + m:17 + m], xT[:, m, tsl], ALU.mult, ALU.add,
                              reads=[("ps", p), ("modT", l), ("x", m, tb)], writes=[("x", m, tb)])

            for _ in prep1(0):
                pass
            for h in range(8):
                part2(h)
                gen = prep1(h + 1) if h < 7 else None
                scan(h, gen)
                if gen is not None:
                    for _ in gen:
                        pass
                outp(h)
                run_hook(h)
            P.fence(lambda e: e.memset(dummy[0:1, 1:2], 0.0), ARENA_FAMS)

        import os
        DBG = os.environ.get("KDBG", "")
        first_layer = plan[0][0]
        for nb in range(0 if "noada" in DBG else 12):
            ada_block(first_layer, nb)
        for pi, (l, do_mix, do_ffn) in enumerate(plan):
            nxt = plan[pi + 1][0] if pi + 1 < len(plan) else None
            if do_mix:
                ada_hook[0] = nxt
                norm_mod(l, 0)
                if l % 2 == 0:
                    dn_mixer(l)
                else:
                    sb_mixer(l)
                ada_hook[0] = None
            if do_ffn:
                norm_mod(l, 1)
                ffn(l, None if do_mix else nxt)
            elif nxt is not None and not do_mix:
                for nb in range(12):
                    ada_block(nxt, nb)

        go = P.dma_group("st")
        for kc in range(KC):
            P.dma("sp", go, yT_d[kc * 128:(kc + 1) * 128, :], xT[:, kc, :],
                  reads=[("x", kc, tb) for tb in range(NTB)], writes=["yT"])
        P.op("sp", lambda e: e.nop(), reads=["yT"])
        P.emit()
    return nc


def _layout(inputs):
    f = lambda a: np.ascontiguousarray(np.asarray(a, dtype=np.float32))
    x = f(inputs["x"])
    c = f(inputs["c"])
    B = x.shape[0]
    shared = {
        "ada_w": f(inputs["ada_w"]),
        "ada_b": f(inputs["ada_b"]),
        "n1g": f(np.asarray(inputs["norm1_g"]).reshape(NL, KC, 128).transpose(2, 0, 1).reshape(128, NL * KC)),
        "n2g": f(np.asarray(inputs["norm2_g"]).reshape(NL, KC, 128).transpose(2, 0, 1).reshape(128, NL * KC)),
        "dn_w_in": f(inputs["dn_w_in"]),
        "dn_conv": f(np.asarray(inputs["dn_conv_w"]).reshape(2, 4, 24, 128).transpose(3, 0, 2, 1).reshape(128, 192)),
        "dn_alog": f(np.broadcast_to(np.asarray(inputs["dn_a_log"]).reshape(1, 16), (128, 16))),
        "dn_dtb": f(np.broadcast_to(np.asarray(inputs["dn_dt_bias"]).reshape(1, 16), (128, 16))),
        "dn_onorm": f(np.asarray(inputs["dn_onorm_g"]).T),
        "dn_w_out": f(inputs["dn_w_out"]),
        "sb_w_qkv": f(inputs["sb_w_qkv"]),
        "sb_qg": f(np.tile(np.asarray(inputs["sb_q_norm_g"]).T, (2, 1))),
        "sb_kg": f(np.tile(np.asarray(inputs["sb_k_norm_g"]).T, (2, 1))),
        "sb_w_out": f(inputs["sb_w_out"]),
        "ffn_w_in": f(inputs["ffn_w_in"]),
        "ffn_w_out": f(inputs["ffn_w_out"]),
        "cst": _consts(),
    }
    maps = []
    for b in range(B):
        m = dict(shared)
        m["xT"] = f(x[b].T)
        m["cT"] = f(c[b].reshape(KC, 128).T)
        maps.append(m)
    return maps


def kernel(**inputs):
    maps = _layout(inputs)
    nc = build()
    res = run_bass_kernel_spmd(nc, maps, core_ids=list(range(len(maps))))
    out = np.stack([np.ascontiguousarray(r["yT"].T) for r in res.results], axis=0)
    return out.astype(np.float32)
```

```python
import contextlib
import numpy as np
import concourse.bass as bass
import concourse.mybir as mybir
from concourse.bass_utils import run_bass_kernel_spmd

F32 = mybir.dt.float32
F32R = mybir.dt.float32r
BF16 = mybir.dt.bfloat16
F16 = mybir.dt.float16
AF = mybir.ActivationFunctionType
ALU = mybir.AluOpType

D = 1024
T = 2048
DFF = 2816
NL = 4
KC = 8
NTB = 4
EPS = 1e-6
NMOD = 48


class _Op:
    __slots__ = ("eng", "fn", "deps", "sig", "count", "dma", "batch", "is_mm")

    def __init__(self, eng, fn, is_mm=False):
        self.eng = eng
        self.fn = fn
        self.deps = []
        self.sig = False
        self.count = 0
        self.dma = None
        self.batch = 0
        self.is_mm = is_mm


class DmaGroup:
    def __init__(self, name):
        self.name = name
        self.ops = []
        self.batch = -1
        self.sem = None
        self.cum = {}

    def new_batch(self):
        self.batch += 1


class Prog:
    ENGS = ("pe", "act", "dve", "pool", "sp")

    def __init__(self, nc):
        self.nc = nc
        self.ops = {e: [] for e in self.ENGS}
        self.last_w = {}
        self.readers = {}
        self.groups = []

    def dma_group(self, name):
        g = DmaGroup(name)
        self.groups.append(g)
        return g

    def _track(self, op, reads, writes):
        psr = [r for r in reads if isinstance(r, tuple) and r[0] == "ps"]
        if psr:
            reads = [r for r in reads if not (isinstance(r, tuple) and r[0] == "ps")]
            writes = list(writes) + psr
        deps = {}
        for r in reads:
            w = self.last_w.get(r)
            if w is not None:
                deps[id(w)] = w
            self.readers.setdefault(r, []).append(op)
        for wkey in writes:
            w = self.last_w.get(wkey)
            if w is not None:
                deps[id(w)] = w
            for rd in self.readers.get(wkey, ()):
                if rd is not op:
                    deps[id(rd)] = rd
            self.readers[wkey] = []
            self.last_w[wkey] = op
        for d in deps.values():
            if d is op:
                continue
            if d.is_mm and op.is_mm:
                continue
            if d.dma is not None and op.dma is d.dma and d.batch == op.batch:
                continue
            d.sig = True
            op.deps.append(d)

    def op(self, eng, fn, reads=(), writes=(), is_mm=False):
        o = _Op(eng, fn, is_mm=is_mm)
        self._track(o, reads, writes)
        self.ops[eng].append(o)
        return o

    def dma(self, eng, grp, out, in_, reads=(), writes=()):
        o = _Op(eng, lambda e: e.dma_start(out=out, in_=in_))
        o.dma = grp
        if grp.batch < 0:
            grp.new_batch()
        o.batch = grp.batch
        grp.ops.append(o)
        self._track(o, reads, writes)
        self.ops[eng].append(o)
        return o

    def mm(self, out, lhsT, rhs, start, stop, reads, writes):
        return self.op("pe", lambda e: e.matmul(out, lhsT, rhs, start=start, stop=stop), reads, writes, is_mm=True)

    def tr(self, out, in_, ident, reads, writes):
        return self.op("pe", lambda e: e.transpose(out, in_, ident), reads, writes, is_mm=True)

    def act(self, out, in_, func, reads, writes, bias=0.0, scale=1.0):
        return self.op("act", lambda e: e.activation(out, in_, func, bias=bias, scale=scale), reads, writes)

    def tt(self, eng, out, in0, in1, op, reads, writes):
        return self.op(eng, lambda e: e.tensor_tensor(out, in0, in1, op), reads, writes)

    def stt(self, eng, out, in0, scalar, in1, op0, op1, reads, writes):
        return self.op(eng, lambda e: e.scalar_tensor_tensor(out, in0, scalar, in1, op0, op1), reads, writes)

    def ts(self, eng, out, in0, s1, s2, op0, op1, reads, writes):
        if s2 is None:
            return self.op(eng, lambda e: e.tensor_scalar(out, in0, s1, None, op0), reads, writes)
        return self.op(eng, lambda e: e.tensor_scalar(out, in0, s1, s2, op0, op1), reads, writes)

    def copy(self, eng, out, in_, reads, writes):
        if eng == "act":
            return self.op("act", lambda e: e.activation(out, in_, AF.Copy), reads, writes)
        return self.op(eng, lambda e: e.tensor_copy(out, in_), reads, writes)

    def fence(self, fn, fams):
        fam = lambda k: k if isinstance(k, str) else k[0]
        keys = [k for k in (set(self.last_w) | set(self.readers)) if fam(k) in fams]
        return self.op("dve", fn, reads=(), writes=keys)

    def emit(self):
        nc = self.nc
        for e in self.ENGS:
            c = 0
            for o in self.ops[e]:
                if o.dma is None and o.sig:
                    c += 1
                    o.count = c
        for g in self.groups:
            n = 0
            for o in g.ops:
                n += 1
                g.cum[o.batch] = n
        with contextlib.ExitStack() as st:
            esem = {e: st.enter_context(nc.semaphore("s_" + e)) for e in self.ENGS}
            for g in self.groups:
                g.sem = st.enter_context(nc.semaphore("d_" + g.name))
            block = st.enter_context(nc.Block())

            def run(ename, eng):
                waited = {}
                for o in self.ops[ename]:
                    for d in o.deps:
                        if d.dma is not None:
                            sem, val = d.dma.sem, 16 * d.dma.cum[d.batch]
                        else:
                            sem, val = esem[d.eng], d.count
                        k = id(sem)
                        if waited.get(k, 0) < val:
                            eng.wait_ge(sem, val)
                            waited[k] = val
                    ins = o.fn(eng)
                    if o.dma is not None:
                        ins.then_inc(o.dma.sem, 16)
                    elif o.sig:
                        ins.then_inc(esem[ename], 1)

            @block.tensor
            def _(eng):
                run("pe", eng)

            @block.scalar
            def _(eng):
                run("act", eng)

            @block.vector
            def _(eng):
                run("dve", eng)

            @block.gpsimd
            def _(eng):
                run("pool", eng)

            @block.sync
            def _(eng):
                run("sp", eng)


class Ring:
    def __init__(self, items):
        self.items = items
        self.i = -1

    def next(self):
        self.i = (self.i + 1) % len(self.items)
        return self.items[self.i]


C_ONES = 0
C_BLK = 128
C_IDENT = 256
C_NEGTRI = 384
C_NEGONES = 512
C_MASK = 640
C_DN = C_MASK + 4 * 512
DN_TRI_LE = 0
DN_TRI_GT = 64
DN_NEG_STRICT = 128
DN_NEG_GET = 192
DN_IDENT64 = 256
DN_NEGONES = 320
DN_NCOL = 384
C_TOTAL = C_DN + DN_NCOL


def _consts():
    c = np.zeros((128, C_TOTAL), np.float32)
    c[:, C_ONES:C_ONES + 128] = 1.0
    c[:64, C_BLK:C_BLK + 64] = 1.0
    c[64:, C_BLK + 64:C_BLK + 128] = 1.0
    c[:, C_IDENT:C_IDENT + 128] = np.eye(128, dtype=np.float32)
    j = np.arange(128)[:, None]
    s = np.arange(128)[None, :]
    c[:, C_NEGTRI:C_NEGTRI + 128] = -(j >= s).astype(np.float32)
    c[:, C_NEGONES:C_NEGONES + 128] = -1.0
    t = np.arange(512)[None, :]
    for jj in range(4):
        c[:, C_MASK + jj * 512:C_MASK + (jj + 1) * 512] = ((j + 128 * jj) < t).astype(np.float32)
    a = np.arange(64)[:, None]
    b = np.arange(64)[None, :]
    for half in range(2):
        r = slice(half * 64, half * 64 + 64)
        o = C_DN
        c[r, o + DN_TRI_LE:o + DN_TRI_LE + 64] = (a <= b)
        c[r, o + DN_TRI_GT:o + DN_TRI_GT + 64] = (a > b)
        c[r, o + DN_NEG_STRICT:o + DN_NEG_STRICT + 64] = ((a > b) - 1.0) * 3e4
        c[r, o + DN_NEG_GET:o + DN_NEG_GET + 64] = ((b >= a) - 1.0) * 3e4
        c[r, o + DN_IDENT64:o + DN_IDENT64 + 64] = (a == b)
        c[r, o + DN_NEGONES:o + DN_NEGONES + 64] = -1.0
    return c


def build(plan=None):
    if plan is None:
        plan = [(l, True, True) for l in range(NL)]
    nc = bass.Bass("TRN2", target_bir_lowering=False)
    dt_in = lambda n, s: nc.dram_tensor(n, s, F32, kind="ExternalInput").ap()
    xT_d = dt_in("xT", [D, T])
    cT_d = dt_in("cT", [128, KC])
    ada_w_d = dt_in("ada_w", [NL, D, 6 * D])
    ada_b_d = dt_in("ada_b", [NL, 6 * D])
    n1g_d = dt_in("n1g", [128, NL * KC])
    n2g_d = dt_in("n2g", [128, NL * KC])
    dn_w_in_d = dt_in("dn_w_in", [2, D, 4112])
    dn_conv_d = dt_in("dn_conv", [128, 2 * 24 * 4])
    dn_alog_d = dt_in("dn_alog", [128, 16])
    dn_dtb_d = dt_in("dn_dtb", [128, 16])
    dn_onorm_d = dt_in("dn_onorm", [128, 2])
    dn_w_out_d = dt_in("dn_w_out", [2, D, D])
    sb_w_qkv_d = dt_in("sb_w_qkv", [2, D, 3 * D])
    sb_qg_d = dt_in("sb_qg", [128, 2])
    sb_kg_d = dt_in("sb_kg", [128, 2])
    sb_w_out_d = dt_in("sb_w_out", [2, D, D])
    ffn_w_in_d = dt_in("ffn_w_in", [NL, D, 2 * DFF])
    ffn_w_out_d = dt_in("ffn_w_out", [NL, DFF, D])
    cst_d = dt_in("cst", [128, C_TOTAL])
    yT_d = nc.dram_tensor("yT", [D, T], F32, kind="ExternalOutput").ap()

    P = Prog(nc)
    with contextlib.ExitStack() as st:
        sb = lambda n, s, d: st.enter_context(nc.sbuf_tensor("sb_" + n, s, d))
        xT = sb("xT", [128, KC, T], F32)
        hT = sb("hT", [128, KC, T], BF16)
        S = sb("S", [128, KC, T], BF16)
        wslots = [sb(f"wslot{i}", [128, 4096], BF16) for i in range(3)]
        wgrp = [P.dma_group(f"w{i}") for i in range(3)]
        wring = Ring(list(range(3)))
        adast = [sb(f"adast{i}", [128, 512], F32R) for i in range(3)]
        adagrp = [P.dma_group(f"a{i}") for i in range(3)]
        adaring = Ring(list(range(3)))
        adab = sb("adab", [1, 512], F32)
        adabgrp = P.dma_group("adab")
        modrow = adab
        modT = sb("modT", [128, NL, NMOD], F32)
        AB = sb("AB", [128, NL, 2, KC], F32)
        smalls = sb("smalls", [128, 2 * NL * KC + 16 + 16 + 2 + 2 + 2 + KC], F32)
        o_n1 = 0
        o_n2 = NL * KC
        o_alog = 2 * NL * KC
        o_dtb = o_alog + 16
        o_onorm = o_dtb + 16
        o_qg = o_onorm + 2
        o_kg = o_qg + 2
        o_c = o_kg + 2
        smalls_conv = sb("convw", [128, 192], F32)
        condr = sb("condr", [128, KC], F32R)
        one11 = sb("one11", [1, 1], F32)
        qgs = sb("qgs", [128, 2], F32)
        ones_bf = sb("ones_bf", [128, 128], BF16)
        blk_bf = sb("blk_bf", [128, 128], BF16)
        ident_bf = sb("ident_bf", [128, 128], BF16)
        negtri_r = sb("negtri_r", [128, 128], F16)
        negones_r = sb("negones_r", [128, 128], F16)
        masks = sb("masks", [128, 4, 512], BF16)
        sq = [sb(f"sq{i}", [128, 512], BF16) for i in range(2)]
        sqring = Ring([0, 1])
        rstd = [sb(f"rstd{i}", [128, 512], F32) for i in range(2)]
        rstdring = Ring([0, 1])
        tmp = [sb(f"tmp{i}", [128, 512], F32) for i in range(2)]
        tmpring = Ring([0, 1])
        qT = sb("qT", [128, T], BF16)
        kT = sb("kT", [128, T], BF16)
        vS = sb("vS", [128, 16, 128], BF16)
        sp4 = sb("sp4", [128, 2048], F16)
        spt = [sp4[:, i * 512:(i + 1) * 512] for i in range(3)]
        spring = Ring([0, 1, 2])
        spsum = sp4[:, 1536:2048]
        spsumB = sb("spsumB", [128, 512], F16)[:]
        aT = [sb(f"aT{i}", [128, 512], BF16) for i in range(2)]
        aring = Ring([0, 1])

        wab = sb("wab", [128, KC, 16], BF16)
        wabgrp = P.dma_group("wab")
        dncst = sb("dncst", [128, DN_NCOL], F32)
        onesf = sb("onesf", [128, 128], F32)
        dummy = sb("dummy", [1, 8], F32)
        S2 = S[:].rearrange("p k t -> p (k t)")
        d_oh = S2[:, 0:2048]
        d_ktok = S2[:, 2176:4224].rearrange("p (a b) -> p a b", a=16)
        d_ktail = S2[:, 4224:6272].rearrange("p (a b) -> p a b", a=16)
        d_nwt = S2[:, 6272:8320]
        d_Y = S2[:, 8320:10368].bitcast(F32).rearrange("p (a b) -> p a b", a=16)
        d_attn = S2[:, 10368:11392].rearrange("p (a b) -> p a b", a=16)
        d_TT = S2[:, 11392:12416].rearrange("p (a b) -> p a b", a=16)
        d_diag = S2[:, 12416:13952].rearrange("p (a b) -> p a b", a=12)
        d_sc = S2[:, 13952:16384].bitcast(F32)
        d_ab = d_sc[:, 0:256].rearrange("p (a b) -> p a b", a=16)
        sc3 = lambda i: d_sc[:, 256 + i * 128:256 + (i + 1) * 128].rearrange("p (a b) -> p a b", a=16)
        d_g, d_beta, d_nbeta, d_beg, d_et, d_egl0, d_egl1 = [sc3(i) for i in range(7)]
        d_nea = d_sc[:, 1152:1160]
        dxbuf = sb("dxbuf", [128, 4, 512], BF16)
        d_pre = sb("pre2", [128, 2176], BF16)[:]
        d_vf = sp4[:].bitcast(BF16)
        qTalt = masks[:].rearrange("p j t -> p (j t)")
        d_X = [dxbuf[:, 0, :].rearrange("p (a b) -> p a b", a=8), dxbuf[:, 1, :].rearrange("p (a b) -> p a b", a=8)]
        d_XT = [dxbuf[:, 2, :].rearrange("p (a b) -> p a b", a=8), dxbuf[:, 3, :].rearrange("p (a b) -> p a b", a=8)]
        d_R = rstd[1][:]
        d_Rb = aT[0][:].rearrange("p (a b) -> p a b", a=8)
        d_S = aT[1][:, 0:256].bitcast(F32)
        d_Sb = aT[1][:, 256:384]
        d_vnb = aT[1][:, 384:512]
        ARENA_FAMS = ("S", "sp", "spsum", "aT", "dn", "masks", "qTb")

        ps = [st.enter_context(nc.psum_tensor(f"ps{i}", [128, 512], F32)) for i in range(8)]
        psP = Ring([0, 1])
        PS_N = 2
        psZ = Ring([3, 4])
        psE = Ring([5, 6])
        psZE = Ring([3, 4, 5, 6])
        PS_O = 7

        g0 = P.dma_group("ld0")
        for kc in range(KC):
            P.dma("sp", g0, xT[:, kc, :], xT_d[kc * 128:(kc + 1) * 128, :],
                  writes=[("x", kc, tb) for tb in range(NTB)])
        gs = P.dma_group("lds")
        P.dma("sp", gs, smalls[:, o_n1:o_n1 + NL * KC], n1g_d, writes=["smalls"])
        P.dma("sp", gs, smalls[:, o_n2:o_n2 + NL * KC], n2g_d, writes=["smalls"])
        P.dma("sp", gs, smalls[:, o_alog:o_alog + 16], dn_alog_d, writes=["smalls"])
        P.dma("sp", gs, smalls[:, o_dtb:o_dtb + 16], dn_dtb_d, writes=["smalls"])
        P.dma("sp", gs, smalls[:, o_onorm:o_onorm + 2], dn_onorm_d, writes=["smalls"])
        P.dma("sp", gs, smalls[:, o_qg:o_qg + 2], sb_qg_d, writes=["smalls"])
        P.dma("sp", gs, smalls[:, o_kg:o_kg + 2], sb_kg_d, writes=["smalls"])
        P.dma("sp", gs, smalls[:, o_c:o_c + KC], cT_d, writes=["smalls"])
        P.dma("sp", gs, smalls_conv[:], dn_conv_d, writes=["convw"])
        gc = P.dma_group("ldc")
        P.dma("pool", gc, ones_bf[:], cst_d[:, C_ONES:C_ONES + 128], writes=["cst"])
        P.dma("pool", gc, blk_bf[:], cst_d[:, C_BLK:C_BLK + 128], writes=["cst"])
        P.dma("pool", gc, ident_bf[:], cst_d[:, C_IDENT:C_IDENT + 128], writes=["cst"])
        P.dma("pool", gc, negtri_r[:], cst_d[:, C_NEGTRI:C_NEGTRI + 128], writes=["cst"])
        P.dma("pool", gc, negones_r[:], cst_d[:, C_NEGONES:C_NEGONES + 128], writes=["cst"])
        mgrp = P.dma_group("masks")
        P.dma("pool", mgrp, masks[:], cst_d[:, C_MASK:C_MASK + 2048].rearrange("p (j t) -> p j t", j=4), writes=["masks"])
        P.op("dve", lambda e: e.memset(dummy[0:1, 2:3], 0.0),
             writes=[("sp", 0), ("sp", 1), ("sp", 2), "spsum", ("aT", 0), ("aT", 1)]
             + [("S", c_, t_) for c_ in range(KC) for t_ in range(NTB)])
        P.op("dve", lambda e: e.memset(one11[:], 1.0), writes=["one11"])
        P.dma("sp", gs, dncst[:], cst_d[:, C_DN:C_DN + DN_NCOL], writes=["smalls"])
        P.dma("sp", gs, onesf[:], cst_d[:, C_ONES:C_ONES + 128], writes=["smalls"])
        P.act(condr[:], smalls[:, o_c:o_c + KC], AF.Silu, reads=["smalls"], writes=["condr"])
        P.ts("dve", qgs[:], smalls[:, o_qg:o_qg + 2], 0.125, None, ALU.mult, None, reads=["smalls"], writes=["qgs"])

        def wload(pieces):
            s = wring.next()
            wgrp[s].new_batch()
            for (off, kcs, ncols, src) in pieces:
                dst = wslots[s][:, off:off + kcs * ncols].rearrange("p (k n) -> p k n", k=kcs)
                P.dma("pool", wgrp[s], dst, src.rearrange("(k p) n -> p k n", p=128), writes=[("w", s)])
            return s

        def wview(s, off, kcs, ncols):
            return wslots[s][:, off:off + kcs * ncols].rearrange("p (k n) -> p k n", k=kcs)

        def ada_block(l, nb):
            pso = ps[PS_O]
            for kc in range(KC):
                a = adaring.next()
                adagrp[a].new_batch()
                P.dma("pool", adagrp[a], adast[a][:], ada_w_d[l, kc * 128:(kc + 1) * 128, nb * 512:(nb + 1) * 512],
                      writes=[("adast", a)])
                P.mm(pso[0:1, :], condr[:, kc:kc + 1], adast[a][:], kc == 0, kc == KC - 1,
                     reads=[("adast", a), "condr"], writes=[("ps", PS_O)])
            adabgrp.new_batch()
            P.dma("sp", adabgrp, adab[:], ada_b_d[l:l + 1, nb * 512:(nb + 1) * 512], writes=["adab"])
            P.tt("dve", modrow[:], pso[0:1, :], adab[:], ALU.add, reads=[("ps", PS_O), "adab"], writes=["adab"])
            for j in range(4):
                col = nb * 4 + j
                P.mm(ps[PS_N][:, col:col + 1], modrow[0:1, j * 128:(j + 1) * 128], one11[0:1, 0:1], True, True,
                     reads=["adab", "one11"], writes=[("ps", PS_N)])
            P.copy("dve", modT[:, l, nb * 4:(nb + 1) * 4], ps[PS_N][:, nb * 4:(nb + 1) * 4], reads=[("ps", PS_N)], writes=[("modT", l)])
            if nb == 11:
                P.stt("dve", AB[:, l, 0, :], modT[:, l, 8:16], 1.0, smalls[:, o_n1 + l * KC:o_n1 + (l + 1) * KC],
                      ALU.add, ALU.mult, reads=[("modT", l), "smalls"], writes=[("AB", l)])
                P.stt("dve", AB[:, l, 1, :], modT[:, l, 32:40], 1.0, smalls[:, o_n2 + l * KC:o_n2 + (l + 1) * KC],
                      ALU.add, ALU.mult, reads=[("modT", l), "smalls"], writes=[("AB", l)])

        def norm_mod(l, which):
            sh = 0 if which == 0 else 24
            for tb in range(NTB):
                tsl = slice(tb * 512, (tb + 1) * 512)
                for kc in range(KC):
                    i = sqring.next()
                    P.act(sq[i][:], xT[:, kc, tsl], AF.Square, reads=[("x", kc, tb)], writes=[("sq", i)])
                    P.mm(ps[PS_N][:], ones_bf[:], sq[i][:], kc == 0, kc == KC - 1,
                         reads=[("sq", i), "cst"], writes=[("ps", PS_N)])
                r = rstdring.next()
                P.act(rstd[r][:], ps[PS_N][:], AF.Ln, reads=[("ps", PS_N)], writes=[("rstd", r)], bias=EPS, scale=1.0 / D)
                P.act(rstd[r][:], rstd[r][:], AF.Exp, reads=[("rstd", r)], writes=[("rstd", r)], scale=-0.5)
                for kc in range(KC):
                    i = tmpring.next()
                    P.tt("dve", tmp[i][:], xT[:, kc, tsl], rstd[r][:], ALU.mult,
                         reads=[("x", kc, tb), ("rstd", r)], writes=[("tmp", i)])
                    P.act(hT[:, kc, tsl], tmp[i][:], AF.Identity, reads=[("tmp", i), ("AB", l), ("modT", l)],
                          writes=[("h", kc, tb)], bias=modT[:, l, sh + kc:sh + kc + 1], scale=AB[:, l, which, kc:kc + 1])

        def out_proj(l, w_d, nkc, gate_off, skeys_fn, s_view_fn):
            for mb in range(2 if nkc <= 8 else 4):
                ncols = 512 if nkc <= 8 else 256
                s = wload([(0, nkc, ncols, w_d[:, mb * ncols:(mb + 1) * ncols])])
                wv = wview(s, 0, nkc, ncols)
                for mc in range(ncols // 128):
                    m = mb * (ncols // 128) + mc
                    for tb in range(NTB):
                        tsl = slice(tb * 512, (tb + 1) * 512)
                        p = psP.next()
                        for kc in range(nkc):
                            P.mm(ps[p][:], wv[:, kc, mc * 128:(mc + 1) * 128], s_view_fn(kc, tsl), kc == 0, kc == nkc - 1,
                                 reads=[("w", s), skeys_fn(kc, tb)], writes=[("ps", p)])
                        P.stt("dve", xT[:, m, tsl], ps[p][:], modT[:, l, gate_off + m:gate_off + m + 1], xT[:, m, tsl],
                              ALU.mult, ALU.add, reads=[("ps", p), ("modT", l), ("x", m, tb)], writes=[("x", m, tb)])

        def ffn(l, ada_next):
            groups = [(0, 8), (8, 7), (15, 7)]
            for gi, (c0, ncg) in enumerate(groups):
                j = 0
                while j < ncg:
                    nsub = min(4, ncg - j)
                    cc = c0 + j
                    sg = wload([(0, KC, nsub * 128, ffn_w_in_d[l, :, cc * 128:(cc + nsub) * 128])])
                    su = wload([(0, KC, nsub * 128, ffn_w_in_d[l, :, DFF + cc * 128:DFF + (cc + nsub) * 128])])
                    wg = wview(sg, 0, KC, nsub * 128)
                    wu = wview(su, 0, KC, nsub * 128)
                    for q in range(nsub):
                        for tb in range(NTB):
                            tsl = slice(tb * 512, (tb + 1) * 512)
                            pg = psZ.next()
                            pu = psE.next()
                            for kc in range(KC):
                                P.mm(ps[pg][:], wg[:, kc, q * 128:(q + 1) * 128], hT[:, kc, tsl], kc == 0, kc == KC - 1,
                                     reads=[("w", sg), ("h", kc, tb)], writes=[("ps", pg)])
                            for kc in range(KC):
                                P.mm(ps[pu][:], wu[:, kc, q * 128:(q + 1) * 128], hT[:, kc, tsl], kc == 0, kc == KC - 1,
                                     reads=[("w", su), ("h", kc, tb)], writes=[("ps", pu)])
                            i = tmpring.next()
                            P.act(tmp[i][:], ps[pg][:], AF.Silu, reads=[("ps", pg)], writes=[("tmp", i)])
                            P.tt("dve", S[:, j + q, tsl], tmp[i][:], ps[pu][:], ALU.mult,
                                 reads=[("tmp", i), ("ps", pu)], writes=[("S", j + q, tb)])
                    j += nsub
                out_proj(l, ffn_w_out_d[l, c0 * 128:(c0 + ncg) * 128, :], ncg, 40,
                         lambda kc, tb: ("S", kc, tb), lambda kc, tsl: S[:, kc, tsl])
                if ada_next is not None:
                    for nb in range(gi * 4, gi * 4 + 4):
                        ada_block(ada_next, nb)

        ada_hook = [None]

        def run_hook(i):
            if ada_hook[0] is not None and 1 <= i <= 4:
                for nb in range((i - 1) * 3, (i - 1) * 3 + 3):
                    ada_block(ada_hook[0], nb)

        def sb_mixer(l):
            jl = l // 2
            w_d = sb_w_qkv_d[jl]
            mgrp.new_batch()
            P.dma("pool", mgrp, masks[:], cst_d[:, C_MASK:C_MASK + 2048].rearrange("p (j t) -> p j t", j=4), writes=["masks"])
            for hp in range(8):
                s = wload([(0, KC, 128, w_d[:, hp * 128:(hp + 1) * 128]),
                           (1024, KC, 128, w_d[:, D + hp * 128:D + (hp + 1) * 128]),
                           (2048, KC, 128, w_d[:, 2 * D + hp * 128:2 * D + (hp + 1) * 128])])
                wq = wview(s, 0, KC, 128)
                wk = wview(s, 1024, KC, 128)
                wv = wview(s, 2048, KC, 128)
                for (wmat, dst, dkey, gcol) in ((wq, qT, "qT", qgs[:, jl:jl + 1]),
                                                (wk, kT, "kT", smalls[:, o_kg + jl:o_kg + jl + 1])):
                    for tb in range(NTB):
                        tsl = slice(tb * 512, (tb + 1) * 512)
                        p = psP.next()
                        for kc in range(KC):
                            P.mm(ps[p][:], wmat[:, kc, :], hT[:, kc, tsl], kc == 0, kc == KC - 1,
                                 reads=[("w", s), ("h", kc, tb)], writes=[("ps", p)])
                        i = sqring.next()
                        P.act(sq[i][:], ps[p][:], AF.Square, reads=[("ps", p)], writes=[("sq", i)])
                        P.mm(ps[PS_N][:], blk_bf[:], sq[i][:], True, True, reads=[("sq", i), "cst"], writes=[("ps", PS_N)])
                        r = rstdring.next()
                        P.act(rstd[r][:], ps[PS_N][:], AF.Ln, reads=[("ps", PS_N)], writes=[("rstd", r)], bias=EPS, scale=1.0 / 64)
                        P.act(rstd[r][:], rstd[r][:], AF.Exp, reads=[("rstd", r)], writes=[("rstd", r)], scale=-0.5)
                        P.stt("dve", dst[:, tsl], ps[p][:], gcol, rstd[r][:], ALU.mult, ALU.mult,
                              reads=[("ps", p), ("rstd", r), "qgs", "smalls"], writes=[(dkey, tb)])
                for t4 in range(4):
                    p = psP.next()
                    for jj in range(4):
                        tt_ = t4 * 4 + jj
                        for kc in range(KC):
                            P.mm(ps[p][:, jj * 128:(jj + 1) * 128], hT[:, kc, tt_ * 128:(tt_ + 1) * 128], wv[:, kc, :],
                                 kc == 0, kc == KC - 1, reads=[("w", s), ("h", kc, t4)], writes=[("ps", p)])
                    P.copy("dve", vS[:, t4 * 4:(t4 + 1) * 4, :], ps[p][:].rearrange("p (j n) -> p j n", j=4),
                           reads=[("ps", p)], writes=[("vS", t4)])
                tiles = []
                for qb in range(NTB):
                    nkb = 4 * (qb + 1)
                    for kb in range(nkb - 1, -1, -1):
                        for hd in range(2):
                            tiles.append(dict(hd=hd, qb=qb, kb=kb, first=(kb == nkb - 1), last=(kb == 0),
                                              diag=(kb >= 4 * qb), jm=kb - 4 * qb))
                SPS = [(spsum, "spsum"), (spsumB, "spsumB")]
                PSO = [PS_O, 0]

                def stA(t):
                    r0 = t["hd"] * 64
                    ksl = slice(t["kb"] * 128, (t["kb"] + 1) * 128)
                    c0 = 128 * t["jm"] if t["diag"] else 0
                    t["c0"] = c0
                    qsl = slice(t["qb"] * 512 + c0, (t["qb"] + 1) * 512)
                    cs = slice(c0, 512)
                    pz = psZ.next()
                    P.mm(ps[pz][:, cs], kT[r0:r0 + 64, ksl], qT[r0:r0 + 64, qsl], True, True,
                         reads=[("kT", t["kb"] // 4), ("qT", t["qb"])], writes=[("ps", pz)])
                    si = spring.next()
                    t["si"] = si
                    P.act(spt[si][:, cs], ps[pz][:, cs], AF.Exp, reads=[("ps", pz)], writes=[("sp", si)])
                    P.act(spt[si][:, cs], spt[si][:, cs], AF.Ln, reads=[("sp", si)], writes=[("sp", si)], bias=1.0)
                    if t["diag"]:
                        P.tt("dve", spt[si][:, cs], spt[si][:, cs], masks[:, t["jm"], cs], ALU.mult,
                             reads=[("sp", si), "masks"], writes=[("sp", si)])

                def stB(t):
                    r0 = t["hd"] * 64
                    ksl = slice(t["kb"] * 128, (t["kb"] + 1) * 128)
                    c0 = t["c0"]
                    qsl = slice(t["qb"] * 512 + c0, (t["qb"] + 1) * 512)
                    cs = slice(c0, 512)
                    si = t["si"]
                    spsum_, spk_ = SPS[t["hd"]]
                    pe_ = psE.next()
                    P.mm(ps[pe_][:, cs], kT[r0:r0 + 64, ksl], qT[r0:r0 + 64, qsl], True, False,
                         reads=[("kT", t["kb"] // 4), ("qT", t["qb"])], writes=[("ps", pe_)])
                    P.mm(ps[pe_][:, cs], negtri_r[:], spt[si][:, cs], False, t["first"],
                         reads=[("sp", si), "cst"], writes=[("ps", pe_)])
                    if not t["first"]:
                        P.mm(ps[pe_][:, cs], negones_r[:], spsum_[:, cs], False, True,
                             reads=[spk_, "cst"], writes=[("ps", pe_)])
                    if not t["last"]:
                        if t["first"]:
                            if c0 > 0:
                                P.ts("dve", spsum_[:, 0:c0], masks[:, 0, 0:c0], 0.0, None, ALU.mult, None,
                                     reads=["masks"], writes=[spk_])
                            P.copy("dve", spsum_[:, cs], spt[si][:, cs], reads=[("sp", si)], writes=[spk_])
                        else:
                            P.tt("dve", spsum_[:, cs], spsum_[:, cs], spt[si][:, cs], ALU.add,
                                 reads=[("sp", si), spk_], writes=[spk_])
                    P.mm(ps[1][:], ones_bf[:], hT[:, 0, 0:512], True, True, reads=[("h", 0, 0), "cst"], writes=[("ps", 1)])
                    ai = aring.next()
                    t["ai"] = ai
                    if t["first"] and c0 > 0:
                        P.op("dve", lambda e, ai=ai, c0=c0: e.memset(aT[ai][:, 0:c0], 0.0), writes=[("aT", ai)])
                    P.act(aT[ai][:, cs], ps[pe_][:, cs], AF.Exp, reads=[("ps", pe_)], writes=[("aT", ai)])
                    if t["diag"]:
                        P.tt("dve", aT[ai][:, cs], aT[ai][:, cs], masks[:, t["jm"], cs], ALU.mult,
                             reads=[("aT", ai), "masks"], writes=[("aT", ai)])

                def stC(t):
                    r0 = t["hd"] * 64
                    qsl = slice(t["qb"] * 512, (t["qb"] + 1) * 512)
                    ai = t["ai"]
                    po = PSO[t["hd"]]
                    cs = slice(0, 512) if t["first"] else slice(t["c0"], 512)
                    P.mm(ps[po][:, cs], vS[:, t["kb"], :], aT[ai][:, cs], t["first"], t["last"],
                         reads=[("vS", t["kb"] // 4), ("aT", ai)], writes=[("ps", po)])
                    if t["last"]:
                        P.copy("dve", S[r0:r0 + 64, hp, qsl], ps[po][r0:r0 + 64, :],
                               reads=[("ps", po)], writes=[("S", hp, t["qb"])])

                nt = len(tiles)
                for idx in range(nt + 2):
                    if idx < nt:
                        stA(tiles[idx])
                    if 0 <= idx - 1 < nt:
                        stB(tiles[idx - 1])
                    if 0 <= idx - 2 < nt:
                        stC(tiles[idx - 2])
                run_hook(hp)
            out_proj(l, sb_w_out_d[jl], KC, 16, lambda kc, tb: ("S", kc, tb), lambda kc, tsl: S[:, kc, tsl])

        class _Stop(Exception):
            pass

        def stage(n):
            import os
            if float(os.environ.get("KDN", "99")) < n:
                raise _Stop()

        def dn_mixer(l):
            try:
                dn_mixer_(l)
            except _Stop:
                P.fence(lambda e: e.memset(dummy[0:1, 1:2], 0.0), ARENA_FAMS)

        def dn_mixer_(l):
            jl = l // 2
            w_d = dn_w_in_d[jl]
            K = lambda *a: ("dn",) + a
            tri_le = dncst[:, DN_TRI_LE:DN_TRI_LE + 64]
            tri_gt = dncst[:, DN_TRI_GT:DN_TRI_GT + 64]
            neg_strict = dncst[:, DN_NEG_STRICT:DN_NEG_STRICT + 64]
            neg_get = dncst[:, DN_NEG_GET:DN_NEG_GET + 64]
            ident64 = dncst[:, DN_IDENT64:DN_IDENT64 + 64]
            negones = dncst[:, DN_NEGONES:DN_NEGONES + 64]
            HS = [slice(0, 64), slice(64, 128)]
            P.fence(lambda e: e.memset(dummy[0:1, 0:1], 0.0), ARENA_FAMS)
            wabgrp.new_batch()
            P.dma("pool", wabgrp, wab[:], w_d[:, 4096:4112].rearrange("(k p) n -> p k n", p=128), writes=["wab"])
            for tt_ in range(16):
                for kc in range(KC):
                    P.mm(ps[PS_O][:, tt_ * 16:(tt_ + 1) * 16], hT[:, kc, tt_ * 128:(tt_ + 1) * 128], wab[:, kc, :],
                         kc == 0, kc == KC - 1, reads=["wab", ("h", kc, tt_ // 4)], writes=[("ps", PS_O)])
            P.copy("dve", d_ab, ps[PS_O][:, 0:256].rearrange("p (a b) -> p a b", a=16), reads=[("ps", PS_O)], writes=[K("ab")])
            P.act(d_nea, smalls[:, o_alog + jl * 8:o_alog + jl * 8 + 8], AF.Exp, reads=["smalls"], writes=[K("nea")])
            P.ts("dve", d_nea, d_nea, -1.0, None, ALU.mult, None, reads=[K("nea")], writes=[K("nea")])
            dtb_bc = smalls[:, o_dtb + jl * 8:o_dtb + jl * 8 + 8][:, None, :].broadcast_to([128, 16, 8])
            P.tt("dve", d_g, d_ab[:, :, 0:8], dtb_bc, ALU.add, reads=[K("ab"), "smalls"], writes=[K("g")])
            P.act(d_g, d_g, AF.Exp, reads=[K("g")], writes=[K("g")])
            P.act(d_g, d_g, AF.Ln, reads=[K("g")], writes=[K("g")], bias=1.0)
            P.tt("dve", d_g, d_g, d_nea[:, None, :].broadcast_to([128, 16, 8]), ALU.mult, reads=[K("g"), K("nea")], writes=[K("g")])
            P.act(d_beta, d_ab[:, :, 8:16], AF.Exp, reads=[K("ab")], writes=[K("beta")], scale=-1.0)
            P.ts("dve", d_beta, d_beta, 1.0, None, ALU.add, None, reads=[K("beta")], writes=[K("beta")])
            P.op("dve", lambda e: e.reciprocal(d_beta, d_beta), reads=[K("beta")], writes=[K("beta")])
            P.ts("dve", d_nbeta, d_beta, -1.0, None, ALU.mult, None, reads=[K("beta")], writes=[K("nbeta")])
            g2d = d_g.rearrange("p a b -> p (a b)")
            for par in range(2):
                hs = HS[par]
                P.mm(ps[PS_N][hs, 0:128], tri_le[hs, :], g2d[hs, :], True, True, reads=[K("g"), "smalls"], writes=[("ps", PS_N)])
                P.mm(ps[PS_N][hs, 128:256], tri_gt[hs, :], g2d[hs, :], True, True, reads=[K("g"), "smalls"], writes=[("ps", PS_N)])
            pgl = [PS_O, psP.next()]
            for par in range(2):
                P.mm(ps[pgl[par]][:, 0:128], onesf[HS[par], :], g2d[HS[par], :], True, True,
                     reads=[K("g"), "smalls"], writes=[("ps", pgl[par])])
            f2 = lambda v: v.rearrange("p a b -> p (a b)")
            P.act(f2(d_beg), ps[PS_N][:, 0:128], AF.Exp, reads=[("ps", PS_N)], writes=[K("beg")])
            P.tt("dve", f2(d_beg), f2(d_beg), f2(d_beta), ALU.mult, reads=[K("beg"), K("beta")], writes=[K("beg")])
            P.act(f2(d_et), ps[PS_N][:, 128:256], AF.Exp, reads=[("ps", PS_N)], writes=[K("et")])
            P.act(f2(d_egl0), ps[pgl[0]][:, 0:128], AF.Exp, reads=[("ps", pgl[0])], writes=[K("egl")])
            P.act(f2(d_egl1), ps[pgl[1]][:, 0:128], AF.Exp, reads=[("ps", pgl[1])], writes=[K("egl")])
            d_egl = [d_egl0, d_egl1]

            hinfo = {}

            def prep1(h):
                qb, qk = (qT, "qT") if h % 2 == 0 else (qTalt, "qTb")
                s = wload([(0, KC, 128, w_d[:, h * 128:(h + 1) * 128]),
                           (1024, KC, 128, w_d[:, D + h * 128:D + (h + 1) * 128]),
                           (2048, KC, 128, w_d[:, 2 * D + h * 128:2 * D + (h + 1) * 128]),
                           (3072, KC, 128, w_d[:, 3 * D + h * 128:3 * D + (h + 1) * 128])])
                wz = wview(s, 3072, KC, 128)
                for which in range(3):
                    ch = which * 8 + h
                    for tap in range(4):
                        col = smalls_conv[:, ((jl * 24 + ch) * 4 + tap):((jl * 24 + ch) * 4 + tap) + 1]
                        P.ts("dve", d_diag[:, which * 4 + tap, :], ident_bf[:], col, None, ALU.mult, None,
                             reads=["cst", "convw"], writes=[K("diag", which)])
                yield
                for which in range(3):
                    wmat = wview(s, which * 1024, KC, 128)
                    P.op("dve", lambda e: e.memset(d_pre[:, 0:3], 0.0), writes=[K("pre", 0)])
                    for tb in range(NTB):
                        tsl = slice(tb * 512, (tb + 1) * 512)
                        p = psP.next()
                        for kc in range(KC):
                            P.mm(ps[p][:], wmat[:, kc, :], hT[:, kc, tsl], kc == 0, kc == KC - 1,
                                 reads=[("w", s), ("h", kc, tb)], writes=[("ps", p)])
                            if kc == 3:
                                yield
                        P.copy("act", d_pre[:, 3 + tb * 512:3 + (tb + 1) * 512], ps[p][:], reads=[("ps", p)], writes=[K("pre", tb), K("pre", tb + 1)])
                        yield
                    for tb in range(NTB):
                        tsl = slice(tb * 512, (tb + 1) * 512)
                        pc = psZ.next()
                        for tap in range(4):
                            P.mm(ps[pc][:], d_diag[:, which * 4 + tap, :], d_pre[:, tb * 512 + tap:tb * 512 + tap + 512],
                                 tap == 0, tap == 3, reads=[K("diag", which), K("pre", tb), K("pre", tb + 1)], writes=[("ps", pc)])
                        yield
                        if which == 2:
                            P.act(d_vf[:, tsl], ps[pc][:], AF.Silu, reads=[("ps", pc)], writes=[K("vf", tb)])
                        else:
                            dst, dkey = (qb, qk) if which == 0 else (kT, "kT")
                            i = tmpring.next()
                            P.act(tmp[i][:], ps[pc][:], AF.Silu, reads=[("ps", pc)], writes=[("tmp", i)])
                            j = sqring.next()
                            P.act(sq[j][:], tmp[i][:], AF.Square, reads=[("tmp", i)], writes=[("sq", j)])
                            P.mm(ps[PS_N][:], ones_bf[:], sq[j][:], True, True, reads=[("sq", j), "cst"], writes=[("ps", PS_N)])
                            r = rstdring.next()
                            P.act(rstd[r][:], ps[PS_N][:], AF.Ln, reads=[("ps", PS_N)], writes=[("rstd", r)], bias=EPS, scale=1.0)
                            P.act(rstd[r][:], rstd[r][:], AF.Exp, reads=[("rstd", r)], writes=[("rstd", r)], scale=-0.5)
                            P.stt("dve", dst[:, tsl], tmp[i][:], (128.0 ** -0.5) if which == 0 else 1.0, rstd[r][:], ALU.mult, ALU.mult,
                                  reads=[("tmp", i), ("rstd", r)], writes=[(dkey, tb)])
                        yield
                hinfo[h] = (s, wz)

            def part2(h):
                qb, qk = (qT, "qT") if h % 2 == 0 else (qTalt, "qTb")
                for (src, skey, dst, dkey) in ((kT, "kT", d_ktok, "ktok"), (d_vf, None, vS, "vS")):
                    for t4 in range(4):
                        p = psP.next()
                        for jj in range(4):
                            tt_ = t4 * 4 + jj
                            rk = (skey, t4) if skey else K("vf", t4)
                            P.mm(ps[p][:, jj * 128:(jj + 1) * 128], src[:, tt_ * 128:(tt_ + 1) * 128], ident_bf[:], True, True,
                                 reads=[rk, "cst"], writes=[("ps", p)])
                        wk_ = K("ktok", t4) if dkey == "ktok" else ("vS", t4)
                        P.copy("act", dst[:, t4 * 4:(t4 + 1) * 4, :], ps[p][:].rearrange("p (j n) -> p j n", j=4),
                               reads=[("ps", p)], writes=[wk_])
                allk = [K("ktok", t4) for t4 in range(4)]
                allv = [("vS", t4) for t4 in range(4)]
                bc = lambda v: v[:, :, h:h + 1].broadcast_to([128, 16, 128])
                P.tt("dve", vS[:], vS[:], bc(d_beta), ALU.mult, reads=allv + [K("beta")], writes=allv)
                P.tt("dve", d_ktail, d_ktok, bc(d_et), ALU.mult, reads=allk + [K("et")], writes=[K("ktail")])
                P.tt("dve", d_ktok, d_ktok, bc(d_beg), ALU.mult, reads=allk + [K("beg")], writes=allk)
                P.tt("dve", d_Y, tri_le[:, None, :].broadcast_to([128, 16, 64]), d_g[:, :, h:h + 1].broadcast_to([128, 16, 64]),
                     ALU.mult, reads=[K("g"), "smalls"], writes=[K("Y")])
                def grp(gq):
                    Rg = rstd[1][:] if gq == 0 else rstd[0][:]
                    rk_ = ("rstd", 1) if gq == 0 else ("rstd", 0)
                    Rbg = aT[0][:].rearrange("p (a b) -> p a b", a=8) if gq == 0 else sq[0][:].rearrange("p (a b) -> p a b", a=8)
                    rbk_ = K("Rb") if gq == 0 else ("sq", 0)
                    def blocks():
                        for t8 in range(8):
                            for par in range(2):
                                tt_ = gq * 8 + t8
                                yield t8, tt_, HS[par], par, tt_ * 128 + par * 64, slice(t8 * 64, (t8 + 1) * 64)
                    kkeys = [("kT", gq * 2), ("kT", gq * 2 + 1)]
                    qkeys = [(qk, gq * 2), (qk, gq * 2 + 1)]
                    yield
                    pk = psZ.next()
                    for t8, tt_, hs, par, c0, cs in blocks():
                        P.mm(ps[pk][hs, cs], kT[:, c0:c0 + 64], kT[:, c0:c0 + 64], True, True, reads=kkeys, writes=[("ps", pk)])
                    yield
                    pd = psE.next()
                    for t8, tt_, hs, par, c0, cs in blocks():
                        P.mm(ps[pd][hs, cs], d_Y[hs, tt_, :], onesf[hs, 0:64], True, False, reads=[K("Y"), "smalls"], writes=[("ps", pd)])
                        P.mm(ps[pd][hs, cs], negones[hs, :], d_Y[hs, tt_, :], False, True, reads=[K("Y"), "smalls"], writes=[("ps", pd)])
                    i = tmpring.next()
                    v3 = lambda ap: ap.rearrange("p (a b) -> p a b", a=8)
                    P.tt("dve", v3(tmp[i][:]), v3(ps[pd][:]), neg_strict[:, None, :].broadcast_to([128, 8, 64]), ALU.add,
                         reads=[("ps", pd), "smalls"], writes=[("tmp", i)])
                    P.act(tmp[i][:], tmp[i][:], AF.Exp, reads=[("tmp", i)], writes=[("tmp", i)])
                    P.tt("dve", tmp[i][:], tmp[i][:], ps[pk][:], ALU.mult, reads=[("tmp", i), ("ps", pk)], writes=[("tmp", i)])
                    cur = nxt = gq
                    nb_bc = d_nbeta[:, gq * 8:(gq + 1) * 8, h:h + 1].broadcast_to([128, 8, 64])
                    P.tt("dve", d_XT[cur], v3(tmp[i][:]), nb_bc, ALU.mult, reads=[("tmp", i), K("nbeta")], writes=[K("XT", cur)])
                    yield
                    px = psZ.next()
                    for t8, tt_, hs, par, c0, cs in blocks():
                        P.mm(ps[px][hs, cs], d_XT[cur][hs, t8, :], ident_bf[hs, par * 64:par * 64 + 64], True, True,
                             reads=[K("XT", cur), "cst"], writes=[("ps", px)])
                    P.copy("act", d_X[cur], v3(ps[px][:]), reads=[("ps", px)], writes=[K("X", cur)])
                    P.tt("dve", v3(Rg), v3(ps[px][:]), ident64[:, None, :].broadcast_to([128, 8, 64]), ALU.add,
                         reads=[("ps", px), "smalls"], writes=[rk_])
                    P.copy("act", Rbg, v3(Rg), reads=[rk_], writes=[rbk_])
                    for lvl in range(1, 6):
                        if lvl < 5:
                            yield
                            p2 = psZ.next()
                            for t8, tt_, hs, par, c0, cs in blocks():
                                P.mm(ps[p2][hs, cs], d_XT[cur][hs, t8, :], d_X[cur][hs, t8, :], True, True,
                                     reads=[K("XT", cur), K("X", cur)], writes=[("ps", p2)])
                        yield
                        p2t = psE.next()
                        for t8, tt_, hs, par, c0, cs in blocks():
                            P.mm(ps[p2t][hs, cs], d_X[cur][hs, t8, :], d_XT[cur][hs, t8, :], True, True,
                                 reads=[K("XT", cur), K("X", cur)], writes=[("ps", p2t)])
                        if lvl < 5:
                            P.copy("act", d_X[nxt], v3(ps[p2][:]), reads=[("ps", p2)], writes=[K("X", nxt)])
                        P.copy("dve", d_XT[nxt], v3(ps[p2t][:]), reads=[("ps", p2t)], writes=[K("XT", nxt)])
                        yield
                        pr = psP.next()
                        for t8, tt_, hs, par, c0, cs in blocks():
                            P.mm(ps[pr][hs, cs], d_XT[nxt][hs, t8, :], Rbg[hs, t8, :], True, True,
                                 reads=[K("XT", nxt), rbk_], writes=[("ps", pr)])
                        P.tt("dve", Rg, Rg, ps[pr][:], ALU.add, reads=[rk_, ("ps", pr)], writes=[rk_])
                        if lvl < 5:
                            P.copy("act", Rbg, v3(Rg), reads=[rk_], writes=[rbk_])
                        else:
                            P.copy("act", d_TT[:, gq * 8:(gq + 1) * 8, :], v3(Rg), reads=[rk_], writes=[K("TT", gq)])
                    yield
                    pq = psZ.next()
                    for t8, tt_, hs, par, c0, cs in blocks():
                        P.mm(ps[pq][hs, cs], kT[:, c0:c0 + 64], qb[:, c0:c0 + 64], True, True, reads=kkeys + qkeys, writes=[("ps", pq)])
                    yield
                    pdt = psE.next()
                    for t8, tt_, hs, par, c0, cs in blocks():
                        P.mm(ps[pdt][hs, cs], onesf[hs, 0:64], d_Y[hs, tt_, :], True, False, reads=[K("Y"), "smalls"], writes=[("ps", pdt)])
                        P.mm(ps[pdt][hs, cs], d_Y[hs, tt_, :], negones[hs, :], False, True, reads=[K("Y"), "smalls"], writes=[("ps", pdt)])
                    i = tmpring.next()
                    P.tt("dve", v3(tmp[i][:]), v3(ps[pdt][:]), neg_get[:, None, :].broadcast_to([128, 8, 64]), ALU.add,
                         reads=[("ps", pdt), "smalls"], writes=[("tmp", i)])
                    P.act(tmp[i][:], tmp[i][:], AF.Exp, reads=[("tmp", i)], writes=[("tmp", i)])
                    P.tt("dve", d_attn[:, gq * 8:(gq + 1) * 8, :], v3(tmp[i][:]), v3(ps[pq][:]), ALU.mult,
                         reads=[("tmp", i), ("ps", pq)], writes=[K("attn", gq)])
                gens_ = [grp(0), grp(1)]
                while gens_:
                    for g_ in list(gens_):
                        try:
                            next(g_)
                        except StopIteration:
                            gens_.remove(g_)
                for tb in range(NTB):
                    tsl = slice(tb * 512, (tb + 1) * 512)
                    pab = [psE.next(), psE.next()]
                    for par in range(2):
                        for t4 in range(4):
                            tt_ = tb * 4 + t4
                            P.mm(ps[pab[par]][:, t4 * 64:(t4 + 1) * 64], onesf[HS[par], :], d_Y[HS[par], tt_, :], True, True,
                                 reads=[K("Y"), "smalls"], writes=[("ps", pab[par])])
                    i = tmpring.next()
                    tv = tmp[i][:].rearrange("p (t q i) -> p t q i", t=4, q=2)
                    for par in range(2):
                        P.act(tv[:, :, par, :], ps[pab[par]][:, 0:256].rearrange("p (t i) -> p t i", t=4), AF.Exp,
                              reads=[("ps", pab[par])], writes=[("tmp", i)])
                    P.tt("dve", qb[:, tsl], qb[:, tsl], tmp[i][:], ALU.mult, reads=[("tmp", i), (qk, tb)], writes=[(qk, tb)])
                    pwb = [psP.next(), psP.next()]
                    for par in range(2):
                        for t4 in range(4):
                            tt_ = tb * 4 + t4
                            P.mm(ps[pwb[par]][:, t4 * 64:(t4 + 1) * 64], d_ktok[HS[par], tt_, :], d_TT[HS[par], tt_, :], True, True,
                                 reads=[K("ktok", tb), K("TT", tb // 2)], writes=[("ps", pwb[par])])
                    nv = d_nwt[:, tsl].rearrange("p (t q i) -> p t q i", t=4, q=2)
                    for par in range(2):
                        P.act(nv[:, :, par, :], ps[pwb[par]][:, 0:256].rearrange("p (t i) -> p t i", t=4), AF.Identity,
                              reads=[("ps", pwb[par])], writes=[K("nwt", tb)], scale=-1.0)

            def scan(h, gen):
                qb, qk = (qT, "qT") if h % 2 == 0 else (qTalt, "qTb")
                s, wz = hinfo[h]

                def pull(n):
                    if gen is None:
                        return
                    for _ in range(n):
                        try:
                            next(gen)
                        except StopIteration:
                            return

                P.op("dve", lambda e: e.memset(d_S, 0.0), writes=[K("S")])
                P.op("dve", lambda e: e.memset(d_Sb, 0.0), writes=[K("Sb")])
                for n in range(32):
                    tt_, par = n // 2, n % 2
                    hs = HS[par]
                    c0 = n * 64
                    tb = n // 8
                    oc = slice((n % 8) * 64, (n % 8) * 64 + 64)
                    pv = psZ.next()
                    P.mm(ps[pv][hs, 0:128], d_TT[hs, tt_, :], vS[hs, tt_, :], True, False,
                         reads=[K("TT", tt_ // 8), ("vS", tt_ // 4)], writes=[("ps", pv)])
                    P.mm(ps[pv][hs, 0:128], d_nwt[:, c0:c0 + 64], d_Sb, False, True, reads=[K("nwt", tb), K("Sb")], writes=[("ps", pv)])
                    P.mm(ps[PS_O][:, oc], d_Sb, qb[:, c0:c0 + 64], True, False, reads=[K("Sb"), (qk, tb)], writes=[("ps", PS_O)])
                    P.copy("dve", d_vnb[hs, :], ps[pv][hs, 0:128], reads=[("ps", pv)], writes=[K("vnb")])
                    pull(1)
                    P.mm(ps[PS_O][:, oc], d_vnb[hs, :], d_attn[hs, tt_, :], False, True, reads=[K("vnb"), K("attn", tt_ // 8)], writes=[("ps", PS_O)])
                    pS = psE.next()
                    P.mm(ps[pS][:, 0:128], d_ktail[hs, tt_, :], d_vnb[hs, :], True, True, reads=[K("ktail"), K("vnb")], writes=[("ps", pS)])
                    P.stt("dve", d_Sb, d_S, d_egl[par][:, tt_, h:h + 1], ps[pS][:, 0:128], ALU.mult, ALU.add,
                          reads=[K("S"), K("egl"), ("ps", pS)], writes=[K("Sb")])
                    P.stt("dve", d_S, d_S, d_egl[par][:, tt_, h:h + 1], ps[pS][:, 0:128], ALU.mult, ALU.add,
                          reads=[K("S"), K("egl"), ("ps", pS)], writes=[K("S")])
                    pull(1)
                    if n % 8 == 7:
                        tsl = slice(tb * 512, (tb + 1) * 512)
                        j = sqring.next()
                        P.act(sq[j][:], ps[PS_O][:], AF.Square, reads=[("ps", PS_O)], writes=[("sq", j)])
                        P.mm(ps[PS_N][:], ones_bf[:], sq[j][:], True, True, reads=[("sq", j), "cst"], writes=[("ps", PS_N)])
                        r = rstdring.next()
                        P.act(rstd[r][:], ps[PS_N][:], AF.Ln, reads=[("ps", PS_N)], writes=[("rstd", r)], bias=EPS, scale=1.0 / 128)
                        P.act(rstd[r][:], rstd[r][:], AF.Exp, reads=[("rstd", r)], writes=[("rstd", r)], scale=-0.5)
                        p = psP.next()
                        for kc in range(KC):
                            P.mm(ps[p][:], wz[:, kc, :], hT[:, kc, tsl], kc == 0, kc == KC - 1,
                                 reads=[("w", s), ("h", kc, tb)], writes=[("ps", p)])
                        i = tmpring.next()
                        P.act(tmp[i][:], ps[p][:], AF.Silu, reads=[("ps", p)], writes=[("tmp", i)])
                        i2 = tmpring.next()
                        P.stt("dve", tmp[i2][:], ps[PS_O][:], smalls[:, o_onorm + jl:o_onorm + jl + 1], rstd[r][:], ALU.mult, ALU.mult,
                              reads=[("ps", PS_O), ("rstd", r), "smalls"], writes=[("tmp", i2)])
                        P.tt("dve", d_oh[:, tsl], tmp[i2][:], tmp[i][:], ALU.mult, reads=[("tmp", i), ("tmp", i2)], writes=[K("oh", tb)])

            def outp(h):
                so = wload([(0, 1, 1024, dn_w_out_d[jl][h * 128:(h + 1) * 128, :])])
                wo = wview(so, 0, 1, 1024)
                for m in range(KC):
                    for tb in range(NTB):
                        tsl = slice(tb * 512, (tb + 1) * 512)
                        p = psP.next()
                        P.mm(ps[p][:], wo[:, 0, m * 128:(m + 1) * 128], d_oh[:, tsl], True, True,
                             reads=[("w", so), K("oh", tb)], writes=[("ps", p)])
                        P.stt("dve", xT[:, m, tsl], ps[p][:], modT[:, l, 16 + m:17 + m], xT[:, m, tsl], ALU.mult, ALU.add,
                              reads=[("ps", p), ("modT", l), ("x", m, tb)], writes=[("x", m, tb)])

            for _ in prep1(0):
                pass
            for h in range(8):
                part2(h)
                gen = prep1(h + 1) if h < 7 else None
                scan(h, gen)
                if gen is not None:
                    for _ in gen:
                        pass
                outp(h)
                run_hook(h)
            P.fence(lambda e: e.memset(dummy[0:1, 1:2], 0.0), ARENA_FAMS)

        import os
        DBG = os.environ.get("KDBG", "")
        first_layer = plan[0][0]
        for nb in range(0 if "noada" in DBG else 12):
            ada_block(first_layer, nb)
        for pi, (l, do_mix, do_ffn) in enumerate(plan):
            nxt = plan[pi + 1][0] if pi + 1 < len(plan) else None
            if do_mix:
                ada_hook[0] = nxt
                norm_mod(l, 0)
                if l % 2 == 0:
                    dn_mixer(l)
                else:
                    sb_mixer(l)
                ada_hook[0] = None
            if do_ffn:
                norm_mod(l, 1)
                ffn(l, None if do_mix else nxt)
            elif nxt is not None and not do_mix:
                for nb in range(12):
                    ada_block(nxt, nb)

        go = P.dma_group("st")
        for kc in range(KC):
            P.dma("sp", go, yT_d[kc * 128:(kc + 1) * 128, :], xT[:, kc, :],
                  reads=[("x", kc, tb) for tb in range(NTB)], writes=["yT"])
        P.op("sp", lambda e: e.nop(), reads=["yT"])
        P.emit()
    return nc


def _layout(inputs):
    f = lambda a: np.ascontiguousarray(np.asarray(a, dtype=np.float32))
    x = f(inputs["x"])
    c = f(inputs["c"])
    B = x.shape[0]
    shared = {
        "ada_w": f(inputs["ada_w"]),
        "ada_b": f(inputs["ada_b"]),
        "n1g": f(np.asarray(inputs["norm1_g"]).reshape(NL, KC, 128).transpose(2, 0, 1).reshape(128, NL * KC)),
        "n2g": f(np.asarray(inputs["norm2_g"]).reshape(NL, KC, 128).transpose(2, 0, 1).reshape(128, NL * KC)),
        "dn_w_in": f(inputs["dn_w_in"]),
        "dn_conv": f(np.asarray(inputs["dn_conv_w"]).reshape(2, 4, 24, 128).transpose(3, 0, 2, 1).reshape(128, 192)),
        "dn_alog": f(np.broadcast_to(np.asarray(inputs["dn_a_log"]).reshape(1, 16), (128, 16))),
        "dn_dtb": f(np.broadcast_to(np.asarray(inputs["dn_dt_bias"]).reshape(1, 16), (128, 16))),
        "dn_onorm": f(np.asarray(inputs["dn_onorm_g"]).T),
        "dn_w_out": f(inputs["dn_w_out"]),
        "sb_w_qkv": f(inputs["sb_w_qkv"]),
        "sb_qg": f(np.tile(np.asarray(inputs["sb_q_norm_g"]).T, (2, 1))),
        "sb_kg": f(np.tile(np.asarray(inputs["sb_k_norm_g"]).T, (2, 1))),
        "sb_w_out": f(inputs["sb_w_out"]),
        "ffn_w_in": f(inputs["ffn_w_in"]),
        "ffn_w_out": f(inputs["ffn_w_out"]),
        "cst": _consts(),
    }
    maps = []
    for b in range(B):
        m = dict(shared)
        m["xT"] = f(x[b].T)
        m["cT"] = f(c[b].reshape(KC, 128).T)
        maps.append(m)
    return maps


def kernel(**inputs):
    maps = _layout(inputs)
    nc = build()
    res = run_bass_kernel_spmd(nc, maps, core_ids=list(range(len(maps))))
    out = np.stack([np.ascontiguousarray(r["yT"].T) for r in res.results], axis=0)
    return out.astype(np.float32)
```
